# Optimizing a Trainium2 kernel written in Bass

```python
import jax, jax.numpy as jnp
from jax import lax
import numpy as np

D_MODEL = 1024
BATCH = 16
SEQ = 2048
DEPTH = 4
DEC_BATCH = 32
DEC_SEQ = 32
PAST_LEN = 2048

CHUNK = 64
BR_WIDTH = D_MODEL // 4
N_BRANCH = 5
A_BLOCK = 128
A_GROUPS = 4
A_HEAD = BR_WIDTH // A_GROUPS
B_KERNEL = 31
C_KERNEL = 3
D_WINDOWS = (2, 4, 8, 16)
D_GROUPS = 4
D_GROUP_DIM = BR_WIDTH // D_GROUPS
D_HIST = max(D_WINDOWS) - 1
N_MEM = 256
X_HEADS = 4
X_HEAD_DIM = BR_WIDTH // X_HEADS
D_FF = ((8 * D_MODEL // 3 + 255) // 256) * 256
W_IN_COLS = 9 * BR_WIDTH
EPS = 1e-6

kernel_name = "hybrid_streaming_encoder_step"


def rmsnorm(x, g):
    x32 = x.astype(jnp.float32)
    y = x32 * lax.rsqrt(jnp.mean(x32 * x32, axis=-1, keepdims=True) + EPS) * g.astype(jnp.float32)
    return y.astype(x.dtype)


def layernorm(x, g, b):
    x32 = x.astype(jnp.float32)
    mu = jnp.mean(x32, axis=-1, keepdims=True)
    var = jnp.mean(jnp.square(x32 - mu), axis=-1, keepdims=True)
    y = (x32 - mu) * lax.rsqrt(var + EPS) * g.astype(jnp.float32) + b.astype(jnp.float32)
    return y.astype(x.dtype)


def causal_dwconv(x_ext, w):
    c = x_ext.shape[-1]
    return lax.conv_general_dilated(x_ext, w[:, None, :].astype(x_ext.dtype), (1,), 'VALID',
                                    dimension_numbers=('NWC', 'WIO', 'NWC'), feature_group_count=c)


def gmlp_spatial(v, ws, bs):
    bsz, L, _ = v.shape
    pad = (-L) % A_BLOCK
    vb = jnp.pad(v, ((0, 0), (0, pad), (0, 0))).reshape(bsz, -1, A_BLOCK, A_GROUPS, A_HEAD)
    idx = jnp.arange(A_BLOCK)
    mask = (idx[None, :] // CHUNK) <= (idx[:, None] // CHUNK)
    wm = jnp.where(mask[None], ws, jnp.zeros_like(ws))
    s = jnp.einsum('gij,bnjgc->bnigc', wm, vb) + bs.T[None, None, :, :, None]
    return s.reshape(bsz, -1, A_GROUPS * A_HEAD)[:, :L]


def multiscale_pool(d_ext, pos0, d_w, d_scale):
    bsz, Lx, _ = d_ext.shape
    L = Lx - D_HIST
    cs = jnp.cumsum(d_ext.astype(jnp.float32), axis=1)
    cs0 = jnp.concatenate([jnp.zeros((bsz, 1, BR_WIDTH), jnp.float32), cs], axis=1)
    pos = pos0 + jnp.arange(L)
    outs = []
    for g, w in enumerate(D_WINDOWS):
        sl = slice(g * D_GROUP_DIM, (g + 1) * D_GROUP_DIM)
        csg = cs0[:, :, sl]
        win_sum = csg[:, D_HIST + 1:D_HIST + 1 + L] - csg[:, D_HIST + 1 - w:D_HIST + 1 - w + L]
        cnt = jnp.minimum(w, pos + 1).astype(jnp.float32)[None, :, None]
        outs.append(win_sum / cnt - d_ext[:, D_HIST:, sl].astype(jnp.float32))
    pooled = jnp.stack(outs, axis=2).astype(d_ext.dtype)
    mixed = jnp.einsum('blgc,gcd->blgd', pooled, d_w).reshape(bsz, L, BR_WIDTH)
    return mixed * d_scale


def memory_attention(zq, mem_k, mem_v):
    bsz, L, _ = zq.shape
    q = zq.reshape(bsz, L, X_HEADS, X_HEAD_DIM)
    s = jnp.einsum('blhd,bmhd->bhlm', q, mem_k).astype(jnp.float32) * (X_HEAD_DIM ** -0.5)
    p = jax.nn.softmax(s, axis=-1)
    o = jnp.einsum('bhlm,bmhd->blhd', p, mem_v.astype(jnp.float32))
    return o.reshape(bsz, L, BR_WIDTH).astype(zq.dtype)


def memory_kv(mem, mem_norm_g, w_mem_kv):
    bsz = mem.shape[0]
    kv = rmsnorm(mem, mem_norm_g) @ w_mem_kv
    k = kv[..., :BR_WIDTH].reshape(bsz, N_MEM, X_HEADS, X_HEAD_DIM)
    v = kv[..., BR_WIDTH:].reshape(bsz, N_MEM, X_HEADS, X_HEAD_DIM)
    return k, v


def trunk_layer(x, pos0, hist_b, hist_c, hist_d, mem_k, mem_v, lp):
    (norm1_g, w_in, a_ln_g, a_ln_b, a_ws, a_bs, b_conv_w, b_conv_b, b_ln_g, b_ln_b,
     c_conv_w, d_w, d_scale, w_branch, w_gate, b_gate, w_out, norm2_g,
     w_ffn_gate, w_ffn_up, w_ffn_down) = lp
    W = BR_WIDTH
    xn = rmsnorm(x, norm1_g)
    z = xn @ w_in
    za, zb, zc, zd, zq = z[..., :2 * W], z[..., 2 * W:4 * W], z[..., 4 * W:7 * W], z[..., 7 * W:8 * W], z[..., 8 * W:]
    ga = jax.nn.gelu(za)
    ua = ga[..., :W]
    va = layernorm(ga[..., W:], a_ln_g, a_ln_b)
    oa = ua * gmlp_spatial(va, a_ws, a_bs)
    b_in = zb[..., :W] * jax.nn.sigmoid(zb[..., W:])
    b_ext = jnp.concatenate([hist_b, b_in], axis=1)
    ob = jax.nn.silu(layernorm(causal_dwconv(b_ext, b_conv_w) + b_conv_b, b_ln_g, b_ln_b))
    c_b, c_c, c_x = zc[..., :W], zc[..., W:2 * W], zc[..., 2 * W:]
    c_ext = jnp.concatenate([hist_c, c_c * c_x], axis=1)
    oc = c_b * causal_dwconv(c_ext, c_conv_w)
    d_ext = jnp.concatenate([hist_d, zd], axis=1)
    od = multiscale_pool(d_ext, pos0, d_w, d_scale)
    oe = memory_attention(zq, mem_k, mem_v)
    branches = (oa, ob, oc, od, oe)
    merged = jax.nn.sigmoid(xn @ w_gate[0] + b_gate[0]) * (branches[0] @ w_branch[0])
    for n in range(1, N_BRANCH):
        merged = merged + jax.nn.sigmoid(xn @ w_gate[n] + b_gate[n]) * (branches[n] @ w_branch[n])
    x = x + merged @ w_out
    hn = rmsnorm(x, norm2_g)
    x = x + (jax.nn.silu(hn @ w_ffn_gate) * (hn @ w_ffn_up)) @ w_ffn_down
    return x, va, b_ext[:, -(B_KERNEL - 1):], c_ext[:, -(C_KERNEL - 1):], d_ext[:, -D_HIST:]


def setup_inputs(seed: int = 0) -> dict:
    key = jax.random.key(seed)
    ks = iter(jax.random.split(key, 40))
    f32 = jnp.float32

    def nrm(shape, scale):
        return jax.random.normal(next(ks), shape, f32) * scale

    W = BR_WIDTH
    return {
        "x_prompt": nrm((BATCH, SEQ, D_MODEL), 1.0),
        "x_sample": nrm((DEC_BATCH, DEC_SEQ, D_MODEL), 1.0),
        "mem_prompt": nrm((BATCH, N_MEM, D_MODEL), 1.0),
        "cache_mem_k": nrm((DEPTH, DEC_BATCH, N_MEM, X_HEADS, X_HEAD_DIM), 1.0),
        "cache_mem_v": nrm((DEPTH, DEC_BATCH, N_MEM, X_HEADS, X_HEAD_DIM), 1.0),
        "state_conv_b": nrm((DEPTH, DEC_BATCH, B_KERNEL - 1, W), 0.5),
        "state_conv_c": nrm((DEPTH, DEC_BATCH, C_KERNEL - 1, W), 0.5),
        "state_pool_d": nrm((DEPTH, DEC_BATCH, D_HIST, W), 1.0),
        "norm1_g": 1.0 + nrm((DEPTH, D_MODEL), 0.05),
        "mem_norm_g": 1.0 + nrm((DEPTH, D_MODEL), 0.05),
        "w_in": nrm((DEPTH, D_MODEL, W_IN_COLS), D_MODEL ** -0.5),
        "a_ln_g": 1.0 + nrm((DEPTH, W), 0.05),
        "a_ln_b": nrm((DEPTH, W), 0.02),
        "a_ws": nrm((DEPTH, A_GROUPS, A_BLOCK, A_BLOCK), A_BLOCK ** -0.5),
        "a_bs": 1.0 + nrm((DEPTH, A_GROUPS, A_BLOCK), 0.1),
        "b_conv_w": nrm((DEPTH, B_KERNEL, W), B_KERNEL ** -0.5),
        "b_conv_b": nrm((DEPTH, W), 0.02),
        "b_ln_g": 1.0 + nrm((DEPTH, W), 0.05),
        "b_ln_b": nrm((DEPTH, W), 0.02),
        "c_conv_w": nrm((DEPTH, C_KERNEL, W), C_KERNEL ** -0.5),
        "d_w": nrm((DEPTH, D_GROUPS, D_GROUP_DIM, D_GROUP_DIM), D_GROUP_DIM ** -0.5),
        "d_scale": 1.0 + nrm((DEPTH, W), 0.1),
        "w_mem_kv": nrm((DEPTH, D_MODEL, 2 * W), D_MODEL ** -0.5),
        "w_branch": nrm((DEPTH, N_BRANCH, W, D_MODEL), W ** -0.5),
        "w_gate": nrm((DEPTH, N_BRANCH, D_MODEL, D_MODEL), D_MODEL ** -0.5),
        "b_gate": nrm((DEPTH, N_BRANCH, D_MODEL), 0.02),
        "w_out": nrm((DEPTH, D_MODEL, D_MODEL), 0.5 * D_MODEL ** -0.5),
        "norm2_g": 1.0 + nrm((DEPTH, D_MODEL), 0.05),
        "w_ffn_gate": nrm((DEPTH, D_MODEL, D_FF), D_MODEL ** -0.5),
        "w_ffn_up": nrm((DEPTH, D_MODEL, D_FF), D_MODEL ** -0.5),
        "w_ffn_down": nrm((DEPTH, D_FF, D_MODEL), 0.5 * D_FF ** -0.5),
        "final_norm_g": 1.0 + nrm((D_MODEL,), 0.05),
    }


def reference(x_prompt, x_sample, mem_prompt, cache_mem_k, cache_mem_v, state_conv_b, state_conv_c,
              state_pool_d, norm1_g, mem_norm_g, w_in, a_ln_g, a_ln_b, a_ws, a_bs, b_conv_w, b_conv_b,
              b_ln_g, b_ln_b, c_conv_w, d_w, d_scale, w_mem_kv, w_branch, w_gate, b_gate, w_out,
              norm2_g, w_ffn_gate, w_ffn_up, w_ffn_down, final_norm_g):
    dt = x_prompt.dtype
    xp, xs = x_prompt, x_sample
    mk_p, mv_p, cb_p, cc_p, pd_p = [], [], [], [], []
    av_s, cb_s, cc_s, pd_s = [], [], [], []
    for l in range(DEPTH):
        lp = (norm1_g[l], w_in[l], a_ln_g[l], a_ln_b[l], a_ws[l], a_bs[l], b_conv_w[l], b_conv_b[l],
              b_ln_g[l], b_ln_b[l], c_conv_w[l], d_w[l], d_scale[l], w_branch[l], w_gate[l], b_gate[l],
              w_out[l], norm2_g[l], w_ffn_gate[l], w_ffn_up[l], w_ffn_down[l])
        mk, mv = memory_kv(mem_prompt, mem_norm_g[l], w_mem_kv[l])
        hb0 = jnp.zeros((BATCH, B_KERNEL - 1, BR_WIDTH), dt)
        hc0 = jnp.zeros((BATCH, C_KERNEL - 1, BR_WIDTH), dt)
        hd0 = jnp.zeros((BATCH, D_HIST, BR_WIDTH), dt)
        xp, _, hb, hc, hd = trunk_layer(xp, 0, hb0, hc0, hd0, mk, mv, lp)
        mk_p.append(mk); mv_p.append(mv); cb_p.append(hb); cc_p.append(hc); pd_p.append(hd)
        xs, va, hb, hc, hd = trunk_layer(xs, PAST_LEN, state_conv_b[l], state_conv_c[l], state_pool_d[l],
                                         cache_mem_k[l], cache_mem_v[l], lp)
        av_s.append(va); cb_s.append(hb); cc_s.append(hc); pd_s.append(hd)
    y_prompt = rmsnorm(xp, final_norm_g)
    y_sample = rmsnorm(xs, final_norm_g)
    return (y_prompt, y_sample,
            jnp.stack(mk_p), jnp.stack(mv_p), jnp.stack(cb_p), jnp.stack(cc_p), jnp.stack(pd_p),
            jnp.stack(av_s), jnp.stack(cb_s), jnp.stack(cc_s), jnp.stack(pd_s))
```

```python
import numpy as np
from contextlib import ExitStack
import concourse.bass as bass
import concourse.mybir as mybir
from concourse.bass_utils import run_bass_kernel_spmd

F32 = mybir.dt.float32
BF16 = mybir.dt.bfloat16
AF = mybir.ActivationFunctionType
ALU = mybir.AluOpType

NCORES = 8
D = 1024
W = 256
DFF = 2816
L = 4
SEQ = 2048
TP = 1024
NMEM = 256
EPS = 1e-6
ENGS = ('pe', 'act', 'dve', 'pool', 'sp')
NSLOT = 8
import os
DEPTH_RUN = int(os.environ.get('KDEPTH', L))
KSEQS = int(os.environ.get('KSEQS', 2))
KTILES = int(os.environ.get('KTILES', 2))
KSTOP = int(os.environ.get('KSTOP', 99))
KSAMP = int(os.environ.get('KSAMP', 1))


class Sched:
    def __init__(self, nc):
        self.nc = nc
        self.ops = {e: [] for e in ENGS}
        self.clock = {e: {} for e in ENGS}
        self.evclock = {}
        self.lastw = {}
        self.readers = {}
        self.dma_count = {}
        self.signals = {e: set() for e in ENGS}

    def add(self, eng, fn, reads=(), writes=(), dma_key=None, extra=()):
        deps = set(extra)
        for r in reads:
            ev = self.lastw.get(r)
            if ev is not None:
                deps.add(ev)
            if isinstance(r, tuple) and r[0] == 'ps':
                for rv in self.readers.get(r, ()):
                    if rv[0] != eng:
                        deps.add(rv)
        for w in writes:
            ev = self.lastw.get(w)
            if ev is not None:
                deps.add(ev)
            rd = self.readers.get(w)
            if rd:
                deps.update(rd)
        if dma_key is not None and self.dma_count.get(dma_key, 0) > 0:
            deps.add((dma_key, self.dma_count[dma_key]))
        clk = self.clock[eng]
        waits = []
        best = {}
        for (s, n) in deps:
            if best.get(s, 0) < n:
                best[s] = n
        for (s, n) in sorted(best.items(), key=lambda t: str(t[0])):
            if clk.get(s, 0) >= n:
                continue
            waits.append((s, n))
            oc = self.evclock[(s, n)]
            for k, v in oc.items():
                if clk.get(k, 0) < v:
                    clk[k] = v
            if s in self.signals:
                self.signals[s].add(n)
        idx = len(self.ops[eng]) + 1
        if dma_key is None:
            ev = (eng, idx)
            if eng == 'pe':
                clk['pe'] = idx
        else:
            n = self.dma_count.get(dma_key, 0) + 1
            self.dma_count[dma_key] = n
            ev = (dma_key, n)
        snap = dict(clk)
        snap[ev[0]] = ev[1]
        self.evclock[ev] = snap
        self.ops[eng].append((fn, waits, idx, ev if dma_key is not None else None))
        for r in reads:
            self.readers.setdefault(r, []).append(ev)
        for w in writes:
            self.lastw[w] = ev
            self.readers[w] = []
        return ev

    def emit(self, es):
        nc = self.nc
        sems = {}
        for e in ENGS:
            if self.signals[e]:
                sems[e] = es.enter_context(nc.semaphore("s_" + e))
        for i, k in enumerate(self.dma_count):
            sems[k] = es.enter_context(nc.semaphore("d%d" % i))
        rank = {}
        for e in ENGS:
            for i, n in enumerate(sorted(self.signals[e])):
                rank[(e, n)] = i + 1

        def val(s, n):
            if s in self.signals:
                return rank[(s, n)]
            return 16 * n

        block = es.enter_context(nc.Block())

        def run(engname, eng):
            sig = self.signals[engname]
            for (fn, waits, idx, dma) in self.ops[engname]:
                for (s, n) in waits:
                    eng.wait_ge(sems[s], val(s, n))
                inst = fn(eng)
                if dma is not None:
                    inst.then_inc(sems[dma[0]], 16)
                elif idx in sig:
                    inst.then_inc(sems[engname], 1)
            if engname == 'sp':
                for k, n in self.dma_count.items():
                    eng.wait_ge(sems[k], 16 * n)

        @block.tensor
        def _(e):
            run('pe', e)

        @block.scalar
        def _(e):
            run('act', e)

        @block.vector
        def _(e):
            run('dve', e)

        @block.gpsimd
        def _(e):
            run('pool', e)

        @block.sync
        def _(e):
            run('sp', e)


def build_program():
    nc = bass.Bass("TRN2", target_bir_lowering=False)

    def din(name, shape):
        return nc.dram_tensor(name, list(shape), F32, kind="ExternalInput").ap()

    def dout(name, shape):
        return nc.dram_tensor(name, list(shape), F32, kind="ExternalOutput").ap()

    xp = din("xp", [2, SEQ, D])
    xs = din("xs", [128, D])
    memp = din("memp", [2, NMEM, D])
    ck = din("ck", [L, 4, NMEM, W])
    cv = din("cv", [L, 4, NMEM, W])
    scb = din("scb", [L, 4, 30, W])
    scc = din("scc", [L, 4, 2, W])
    spd = din("spd", [L, 4, 15, W])
    norm1_g = din("norm1_g", [L, D])
    mem_norm_g = din("mem_norm_g", [L, D])
    w_in = din("w_in", [L, D, 9 * W])
    a_ln_g = din("a_ln_g", [L, W])
    a_ln_b = din("a_ln_b", [L, W])
    a_ws = din("a_ws", [L, 4, 128, 128])
    a_bs = din("a_bs", [L, 4, 128])
    b_conv_w = din("b_conv_w", [L, 31, W])
    b_conv_b = din("b_conv_b", [L, W])
    b_ln_g = din("b_ln_g", [L, W])
    b_ln_b = din("b_ln_b", [L, W])
    c_conv_w = din("c_conv_w", [L, 3, W])
    d_w = din("d_w", [L, 4, 64, 64])
    d_scale = din("d_scale", [L, W])
    w_mem_kv = din("w_mem_kv", [L, D, 2 * W])
    w_branch = din("w_branch", [L, 5, W, D])
    w_gate = din("w_gate", [L, 5, D, D])
    b_gate = din("b_gate", [L, 5, D])
    w_out = din("w_out", [L, D, D])
    norm2_g = din("norm2_g", [L, D])
    w_ffn_gate = din("w_ffn_gate", [L, D, DFF])
    w_ffn_up = din("w_ffn_up", [L, D, DFF])
    w_ffn_down = din("w_ffn_down", [L, DFF, D])
    final_norm_g = din("final_norm_g", [D])
    rctab = din("rctab", [128, 2, 15])

    yp = dout("yp", [2, SEQ, D])
    ys = dout("ys", [128, D])
    mk = dout("mk", [L, 2, NMEM, W])
    mv = dout("mv", [L, 2, NMEM, W])
    cbp = dout("cbp", [L, 2, 30, W])
    ccp = dout("ccp", [L, 2, 2, W])
    pdp = dout("pdp", [L, 2, 15, W])
    avs = dout("avs", [L, 128, W])
    cbs = dout("cbs", [L, 4, 30, W])
    ccs = dout("ccs", [L, 4, 2, W])
    pds = dout("pds", [L, 4, 15, W])

    es = ExitStack()
    with es:
        S = Sched(nc)

        def SB(name, shape, dt):
            return es.enter_context(nc.sbuf_tensor(name, list(shape), dt))

        x = SB("x", [128, 8, TP], F32)
        xn = SB("xn", [128, 8, TP], BF16)
        hb = SB("hb", [128, 18, TP], BF16)
        wr = [SB("wr%d" % i, [128, 2048], BF16) for i in range(NSLOT)]
        KT = SB("KT", [128, L, 2, 256], BF16)
        VV = SB("VV", [128, L, 2, 256], BF16)
        bext = SB("bext", [128, 2, 30 + 512], F32)
        bextbs = [SB("bextb%d" % i, [128, 2, 30 + 512 + 2], BF16) for i in range(2)]
        cext = SB("cext", [128, 2, 2 + 512], F32)
        dext = SB("dext", [128, 2, 15 + 512], F32)
        acc = SB("acc", [128, 2, 512], F32)
        cacc = SB("cacc", [128, 2, 512], F32)
        trA = SB("trA", [128, 15 + 512], F32)
        trB = SB("trB", [128, 15 + 512], F32)
        pooleds = [SB("pooled%d" % i, [128, 2, 512], BF16) for i in range(2)]
        accb = SB("accb", [128, 2, 512], BF16)
        sqb = SB("sqb", [128, 2, 512], BF16)
        sgb = SB("sgb", [128, 2, 512], F32)
        vabufs = [SB("vabuf%d" % i, [128, 4, 256], BF16) for i in range(2)]
        gv = [SB("gv%d" % i, [128, 256], F32) for i in range(4)]
        bnst = [SB("bnst%d" % i, [128, 6], F32) for i in range(4)]
        bnmv_all = SB("bnmv_all", [128, 4, 2], F32)
        rs = SB("rs", [128, 512], F32)
        meansb = SB("meansb", [128, 512], F32)
        msq = SB("msq", [128, 512], F32)
        varsb = SB("varsb", [128, 512], F32)
        sgC = [SB("sgC%d" % i, [128, 512], F32) for i in range(3)]
        gpC = [SB("gpC%d" % i, [128, 512], BF16) for i in range(3)]
        PT = [SB("PT%d" % i, [128, 512], BF16) for i in range(4)]
        den = SB("den", [128, 512], F32)
        bhist = SB("bhist", [128, L, 2, 30], F32)
        chist = SB("chist", [128, L, 2, 2], F32)
        dhist = SB("dhist", [128, L, 2, 15], F32)
        ident = SB("ident", [128, 128], F32)
        identb = SB("identb", [128, 128], BF16)
        ones_b = SB("ones_b", [128, 128], BF16)
        onesw_b = SB("onesw_b", [128, 128], BF16)
        epst = SB("epst", [128, 1], F32)
        rct = SB("rct", [128, 2, 15], F32)
        NPV = 640
        PV = SB("PV", [128, NPV], F32)
        lnab = SB("lnab", [128, 2, 256], F32)
        wmT = SB("wmT", [128, 4, 128], BF16)
        bsrow = SB("bsrow", [1, 512], BF16)
        dwbd = SB("dwbd", [128, 2, 128], BF16)
        sttmp = SB("sttmp", [128, 256], F32)
        ps = [es.enter_context(nc.psum_tensor("ps%d" % i, [128, 512], F32)) for i in range(8)]

        def act(out, in_, func, reads, writes, **kw):
            S.add('act', lambda e: e.activation(out=out, in_=in_, func=func, **kw), reads, writes)

        def tt(out, a, b, op, reads, writes, eng='dve'):
            S.add(eng, lambda e: e.tensor_tensor(out=out, in0=a, in1=b, op=op), reads, writes)

        def ts(out, a, s1, s2, op0, op1, reads, writes, eng='dve'):
            if s2 is None:
                S.add(eng, lambda e: e.tensor_scalar(out=out, in0=a, scalar1=s1, scalar2=None, op0=op0), reads, writes)
            else:
                S.add(eng, lambda e: e.tensor_scalar(out=out, in0=a, scalar1=s1, scalar2=s2, op0=op0, op1=op1), reads, writes)

        def stt(out, a, sc, b, op0, op1, reads, writes):
            S.add('dve', lambda e: e.scalar_tensor_tensor(out=out, in0=a, scalar=sc, in1=b, op0=op0, op1=op1), reads, writes)

        def cp(out, in_, reads, writes, eng='dve'):
            S.add(eng, lambda e: e.tensor_copy(out=out, in_=in_), reads, writes)

        def mm(out, lhsT, rhs, start, stop, reads, writes):
            S.add('pe', lambda e: e.matmul(out, lhsT, rhs, start=start, stop=stop), reads, writes)

        def tp(out, in_, idt, reads, writes):
            S.add('pe', lambda e: e.transpose(out=out, in_=in_, identity=idt), reads, writes)

        dma_ctr = [0]

        def dma(out, in_, reads, writes, key=None, eng='sp'):
            if key is None:
                dma_ctr[0] += 1
                key = ('m', dma_ctr[0] % 16)
            S.add(eng, lambda e: e.dma_start(out=out, in_=in_), reads, writes, dma_key=key)

        def recip(out, in_, reads, writes):
            S.add('dve', lambda e: e.reciprocal(out=out, in_=in_), reads, writes)

        def memset(ap, v, writes, eng='pool'):
            S.add(eng, lambda e: e.memset(ap, v), (), writes)

        class PSA:
            free = list(range(8))

            @classmethod
            def get(cls):
                assert cls.free, "PSUM exhausted"
                return cls.free.pop(0)

            @classmethod
            def put(cls, b):
                cls.free.append(b)

        def P(b):
            return ('ps', b)

        def wdesc(tag):
            k = tag[0]
            if k == 'win':
                _, l, j = tag
                return w_in[l, :, 256 * j:256 * j + 256].rearrange("(k p) n -> p k n", p=128), 8, 256
            if k == 'gate':
                _, l, mp, n = tag
                return w_gate[l, n, :, 256 * mp:256 * mp + 256].rearrange("(k p) n -> p k n", p=128), 8, 256
            if k == 'br':
                _, l, mp, n = tag
                return w_branch[l, n, :, 256 * mp:256 * mp + 256].rearrange("(c p) m -> p c m", p=128), 2, 256
            if k == 'wout':
                _, l, j = tag
                return w_out[l, :, 256 * j:256 * j + 256].rearrange("(k p) n -> p k n", p=128), 8, 256
            if k == 'fg':
                _, l, j = tag
                return w_ffn_gate[l, :, 256 * j:256 * j + 256].rearrange("(k p) n -> p k n", p=128), 8, 256
            if k == 'fu':
                _, l, j = tag
                return w_ffn_up[l, :, 256 * j:256 * j + 256].rearrange("(k p) n -> p k n", p=128), 8, 256
            if k == 'dn':
                _, l, half, m = tag
                r0, r1 = (0, 1536) if half == 0 else (1536, DFF)
                return (w_ffn_down[l, r0:r1, 128 * m:128 * m + 128].rearrange("(k p) n -> p k n", p=128),
                        (r1 - r0) // 128, 128)
            if k == 'cvd':
                return None, 16, 128
            if k == 'kvK':
                return w_mem_kv[tag[1], :, 0:256].rearrange("(k p) n -> p k n", p=128), 8, 256
            if k == 'kvV':
                return w_mem_kv[tag[1], :, 256:512].rearrange("(k p) n -> p k n", p=128), 8, 256
            raise ValueError(tag)

        NN_ORDER = (2, 3, 4, 0, 1)
        FH = [(0, 6), (6, 11)]

        def layer_seq(l, nst, first, last):
            seq = []
            if first:
                for i in range(4):
                    seq.append(('cvd', l, i))
            for _ in range(nst):
                for j in (3, 2, 5, 6, 4, 7, 8, 0, 1):
                    seq.append(('win', l, j))
            for mp in range(4):
                for n in NN_ORDER:
                    seq.append(('gate', l, mp, n))
                    seq.append(('br', l, mp, n))
            for j in range(4):
                seq.append(('wout', l, j))
            for half in range(2):
                for j in range(*FH[half]):
                    seq.append(('fg', l, j))
                    seq.append(('fu', l, j))
                if half == 1 and not last:
                    for i in range(4):
                        seq.append(('cvd', l + 1, i))
                for _ in range(nst if half == 1 else 1):
                    for m in range(8):
                        seq.append(('dn', l, half, m))
            return seq

        MAXFLY = 3
        CVD_ENG = os.environ.get('KCVD', 'act')

        class WR:
            evs = []
            seq = []
            loaded = 0
            nxt = 0
            free = list(range(NSLOT))
            slot_of = {}

            @classmethod
            def pump(cls):
                while cls.loaded < len(cls.seq) and cls.free:
                    j = cls.loaded
                    tag = cls.seq[j]
                    src, a, b = wdesc(tag)
                    slot = cls.free.pop(0)
                    cls.slot_of[j] = slot
                    dst = wr[slot][:, 0:a * b].rearrange("p (a b) -> p a b", b=b)
                    allk = [('wr', slot)] + [('wrx', slot, kk) for kk in range(16)]
                    if tag[0] == 'cvd':
                        _, l_, i_ = tag
                        for kk in range(16):
                            e_ = i_ * 16 + kk
                            if e_ >= 62:
                                break
                            c_, k_ = e_ // 31, e_ % 31
                            wk = allk if kk == 0 else [('wrx', slot, kk)]
                            if CVD_ENG == 'act':
                                act(dst[:, kk, :], identb[:], AF.Copy, ['identb', 'PV'], wk, scale=bcwcol(l_, k_, c_))
                            else:
                                ts(dst[:, kk, :], identb[:], bcwcol(l_, k_, c_), None, ALU.mult, None, ['identb', 'PV'], wk,
                                   eng=CVD_ENG)
                        cls.evs.append(None)
                    else:
                        prev = None
                        cnt = 0
                        for jj in range(j - 1, -1, -1):
                            if cls.seq[jj][0] != 'cvd':
                                cnt += 1
                                if cnt == MAXFLY:
                                    prev = cls.evs[jj]
                                    break
                        extra = [prev] if prev is not None else []
                        ev = S.add('pool', lambda e, dst=dst, src=src: e.dma_start(out=dst, in_=src), (), allk,
                                   dma_key=('wr', slot), extra=extra)
                        cls.evs.append(ev)
                    cls.loaded += 1

            @classmethod
            def acquire(cls, tag):
                j = cls.nxt
                assert cls.seq[j] == tag, (cls.seq[j], tag)
                cls.nxt += 1
                cls.pump()
                assert j < cls.loaded, "weight ring: no free slot for %s" % (tag,)
                _, a, b = wdesc(tag)
                slot = cls.slot_of[j]
                return j, ('wr', slot), wr[slot][:, 0:a * b].rearrange("p (a b) -> p a b", b=b)

            @classmethod
            def release(cls, j):
                cls.free.append(cls.slot_of[j])
                cls.pump()

        memset(ident[:], 0.0, ['ident'])
        S.add('pool', lambda e: e.affine_select(out=ident[:], in_=ident[:], pattern=[[-1, 128]],
                                                compare_op=ALU.not_equal, fill=1.0, base=0, channel_multiplier=1),
              ['ident'], ['ident'])
        cp(identb[:], ident[:], ['ident'], ['identb'])
        memset(ones_b[:], 1.0, ['ones_b'])
        memset(onesw_b[:], 1.0 / 256.0, ['onesw_b'])
        memset(epst[:], EPS, ['epst'])
        memset(dwbd[:], 0.0, ['dwbd'])
        memset(bhist[:], 0.0, ['bhist'])
        memset(chist[:], 0.0, ['chist'])
        memset(dhist[:], 0.0, ['dhist'])
        dma(rct[:], rctab, [], ['rct'])

        pvcol = {}
        groups = [
            [('n1g', norm1_g.rearrange("l (k p) -> (l k) p", p=128), 32),
             ('n2g', norm2_g.rearrange("l (k p) -> (l k) p", p=128), 32),
             ('mng', mem_norm_g.rearrange("l (k p) -> (l k) p", p=128), 32),
             ('fng', final_norm_g.rearrange("(k p) -> k p", p=128), 8)],
            [('bg0', b_gate.rearrange("l n (m p) -> (l n m) p", p=128)[0:128, :], 128)],
            [('bg1', b_gate.rearrange("l n (m p) -> (l n m) p", p=128)[128:160, :], 32)],
            [('bcw0', b_conv_w[0:2].rearrange("l k (c p) -> (l k c) p", p=128), 124)],
            [('bcw1', b_conv_w[2:4].rearrange("l k (c p) -> (l k c) p", p=128), 124)],
            [('ccw', c_conv_w.rearrange("l k (c p) -> (l k c) p", p=128), 24),
             ('bcb', b_conv_b.rearrange("l (c p) -> (l c) p", p=128), 8),
             ('blg', b_ln_g.rearrange("l (c p) -> (l c) p", p=128), 8),
             ('blb', b_ln_b.rearrange("l (c p) -> (l c) p", p=128), 8),
             ('dsc', d_scale.rearrange("l (c p) -> (l c) p", p=128), 8)],
        ]
        colbase = 0
        stg = acc[:].rearrange("p a b -> p (a b)")
        for gi, grp in enumerate(groups):
            r0 = 0
            for (nm, src, nr) in grp:
                dma(stg[r0:r0 + nr, 0:128], src, [], [('acc', 0), ('acc', 1)])
                pvcol[nm] = colbase + r0
                r0 += nr
            b = PSA.get()
            tp(ps[b][:, 0:r0], stg[0:r0, 0:128], ident[0:r0, 0:r0], [('acc', 0), ('acc', 1), 'ident'], [P(b)])
            cp(PV[:, colbase:colbase + r0], ps[b][:, 0:r0], [P(b)], ['PV'])
            PSA.put(b)
            colbase += r0
        assert colbase <= NPV

        def pv(nm, idx):
            c = pvcol[nm] + idx
            return PV[:, c:c + 1]

        def bgcol(l, n, m):
            r = (l * 5 + n) * 8 + m
            return pv('bg0', r) if r < 128 else pv('bg1', r - 128)

        def bcwcol(l, k, c):
            if l < 2:
                return pv('bcw0', (l * 31 + k) * 2 + c)
            return pv('bcw1', ((l - 2) * 31 + k) * 2 + c)

        def rms_a(off, n, si):
            act(hb[:, 0:8, off:off + n], x[:, :, off:off + n], AF.Square,
                [('x', kc, si) for kc in range(8)], [('h', kc, si) for kc in range(8)])

        def rms_b(off, n, si, gname, gidx0, dst, dkey, stats=True):
            if stats:
                b = PSA.get()
                for kc in range(8):
                    mm(ps[b][:, 0:n], ones_b[:], hb[:, kc, off:off + n], kc == 0, kc == 7,
                       [('h', kc, si), 'ones_b'], [P(b)])
                act(rs[:, 0:n], ps[b][:, 0:n], AF.Sqrt, [P(b), 'epst'], ['rs'], scale=1.0 / D, bias=epst[:])
                PSA.put(b)
                recip(rs[:, 0:n], rs[:, 0:n], ['rs'], ['rs'])
            if dst is not None:
                for kc in range(8):
                    stt(dst[:, kc, off:off + n], x[:, kc, off:off + n], pv(gname, gidx0 + kc), rs[:, 0:n],
                        ALU.mult, ALU.mult, [('x', kc, si), 'rs', 'PV'], [(dkey, kc, si)])

        def rmsnorm(off, n, si, gname, gidx0, dst, dkey, fm=False):
            rms_a(off, n, si)
            rms_b(off, n, si, gname, gidx0, dst, dkey)

        def load_tokens(src_rows_fn, nblk):
            stgs = [acc[:].rearrange("p a b -> p (a b)"), cacc[:].rearrange("p a b -> p (a b)")]
            keys = [[('acc', 0), ('acc', 1)], [('cacc', 0), ('cacc', 1)]]
            for tb in range(nblk):
                sg_, kk = stgs[tb % 2], keys[tb % 2]
                dma(sg_[:, :], src_rows_fn(tb), [], kk)
                for half in range(2):
                    b = PSA.get()
                    for j in range(4):
                        kc = half * 4 + j
                        tp(ps[b][:, j * 128:(j + 1) * 128], sg_[:, kc * 128:(kc + 1) * 128], ident[:],
                           kk + ['ident'], [P(b)])
                    si = tb // 4
                    S.add('act' if half == 0 else 'dve',
                          (lambda e, b=b, half=half, tb=tb:
                           (e.activation(out=x[:, half * 4:half * 4 + 4, tb * 128:(tb + 1) * 128],
                                         in_=ps[b][:].rearrange("p (j t) -> p j t", t=128), func=AF.Copy)
                            if half == 0 else
                            e.tensor_copy(out=x[:, half * 4:half * 4 + 4, tb * 128:(tb + 1) * 128],
                                          in_=ps[b][:].rearrange("p (j t) -> p j t", t=128)))),
                          [P(b)], [('x', half * 4 + j, si) for j in range(4)])
                    PSA.put(b)

        def store_tokens(dst_rows_fn, nblk, nst_list):
            yTs = [acc[:].rearrange("p a b -> p (a b)").rearrange("p (k t) -> p k t", t=128),
                   bext[:].rearrange("p a b -> p (a b)")[:, 0:1024].rearrange("p (k t) -> p k t", t=128)]
            yTk = [[('acc', 0), ('acc', 1)], ['bext']]
            ystgs = [cacc[:].rearrange("p a b -> p (a b)"), cext[:].rearrange("p a b -> p (a b)")[:, 0:1024]]
            ysk = [[('cacc', 0), ('cacc', 1)], ['cext']]
            for si, (off, n) in enumerate(nst_list):
                rmsnorm(off, n, si, 'fng', 0, None, None)
                for tb in range(off // 128, (off + n) // 128):
                    t0 = tb * 128
                    yT, yk = yTs[tb % 2], yTk[tb % 2]
                    ystg, sk = ystgs[tb % 2], ysk[tb % 2]
                    for kc in range(8):
                        stt(yT[:, kc, :], x[:, kc, t0:t0 + 128], pv('fng', kc), rs[:, t0 - off:t0 - off + 128],
                            ALU.mult, ALU.mult, [('x', kc, si), 'rs', 'PV'], yk)
                    for half in range(2):
                        b = PSA.get()
                        for j in range(4):
                            kc = half * 4 + j
                            tp(ps[b][:, j * 128:(j + 1) * 128], yT[:, kc, :], ident[:], yk + ['ident'], [P(b)])
                        if half == 0:
                            act(ystg[:, 0:512], ps[b][:], AF.Copy, [P(b)], sk)
                        else:
                            cp(ystg[:, 512:1024], ps[b][:], [P(b)], sk)
                        PSA.put(b)
                    dma(dst_rows_fn(tb), ystg[:, :], sk, [])

        def hkeys(slots, si):
            return [('h', s, si) for s in slots]

        def prologue(seq):
            load_tokens(lambda tb: memp[seq, tb * 128:(tb + 1) * 128, :], 2)
            rms_a(0, 256, 0)
            for l in range(L):
                rms_b(0, 256, 0, 'mng', l * 8, xn, 'xn', stats=(l == 0))
                jK, kK, wK = WR.acquire(('kvK', l))
                jV, kV, wV = WR.acquire(('kvV', l))
                xr = [('xn', kc, 0) for kc in range(8)]
                for c in range(2):
                    b = PSA.get()
                    for kc in range(8):
                        mm(ps[b][:, 0:256], wK[:, kc, c * 128:(c + 1) * 128], xn[:, kc, 0:256], kc == 0, kc == 7,
                           xr + [kK], [P(b)])
                    act(KT[:, l, c, :], ps[b][:, 0:256], AF.Copy, [P(b)], [('KT', l)])
                    PSA.put(b)
                for mc in range(2):
                    b = PSA.get()
                    for kc in range(8):
                        mm(ps[b][:, 0:256], xn[:, kc, mc * 128:(mc + 1) * 128], wK[:, kc, :], kc == 0, kc == 7,
                           xr + [kK], [P(b)])
                    b2 = PSA.get()
                    for kc in range(8):
                        mm(ps[b2][:, 0:256], xn[:, kc, mc * 128:(mc + 1) * 128], wV[:, kc, :], kc == 0, kc == 7,
                           xr + [kV], [P(b2)])
                    act(gv[0][:], ps[b][:, 0:256], AF.Copy, [P(b)], ['gv0'])
                    PSA.put(b)
                    dma(mk[l, seq, mc * 128:(mc + 1) * 128, :], gv[0][:], ['gv0'], [])
                    act(gv[1][:], ps[b2][:, 0:256], AF.Copy, [P(b2)], ['gv1'])
                    cp(VV[:, l, mc, :], ps[b2][:, 0:256], [P(b2)], [('VV', l)])
                    PSA.put(b2)
                    dma(mv[l, seq, mc * 128:(mc + 1) * 128, :], gv[1][:], ['gv1'], [])
                WR.release(jK)
                WR.release(jV)

        def layer_setup(l, samp=False):
            dma(lnab[:, 0, :], a_ln_g[l:l + 1, :].partition_broadcast(128), [], ['lnab'])
            dma(lnab[:, 1, :], a_ln_b[l:l + 1, :].partition_broadcast(128), [], ['lnab'])
            wst = cacc[:].rearrange("p a b -> p (a b)")[:, 0:512].rearrange("p (g j) -> p g j", j=128)
            ck_ = [('cacc', 0), ('cacc', 1)]
            if not samp:
                dma(wst, a_ws[l].rearrange("g i j -> i g j"), [], ck_)
                S.add('dve', lambda e: e.memset(wst[0:64, :, 64:128], 0.0), ck_, ck_)
                dma(den[0:1, :], a_bs[l:l + 1].rearrange("o g i -> o (g i)"), [], ['den'])
            else:
                S.add('dve', lambda e: e.memset(wst, 0.0), ck_, ck_)
                for q in range(4):
                    dma(wst[32 * q:32 * q + 32, :, 32 * q:32 * q + 32],
                        a_ws[l, :, 0:32, 0:32].rearrange("g i j -> i g j"), [], ck_)
                    dma(den[0:1, :].rearrange("o (g x) -> o g x", x=128)[:, :, 32 * q:32 * q + 32],
                        a_bs[l:l + 1, :, 0:32], [], ['den'])
            b = PSA.get()
            for g in range(4):
                tp(ps[b][:, g * 128:(g + 1) * 128], wst[:, g, :], ident[:], ck_ + ['ident'], [P(b)])
            cp(wmT[:].rearrange("p g i -> p (g i)"), ps[b][:], [P(b)], ['wmT'])
            PSA.put(b)
            cp(bsrow[:], den[0:1, :], ['den'], ['bsrow'])
            for c in range(2):
                for gi in range(2):
                    dma(dwbd[64 * gi:64 * gi + 64, c, 64 * gi:64 * gi + 64], d_w[l, 2 * c + gi], [], ['dwbd'],
                        key=('dw', 0), eng='pool')

        def sample_state_load(l):
            stB = acc[:].rearrange("p a b -> p (a b)")
            stC = cacc[:].rearrange("p a b -> p (a b)")
            kB = [('acc', 0), ('acc', 1)]
            kC = [('cacc', 0), ('cacc', 1)]
            S.add('dve', lambda e: e.memset(stB[:, 0:256], 0.0), kB, kB)
            S.add('dve', lambda e: e.memset(stC[:, 0:256], 0.0), kC, kC)
            for q in range(4):
                dma(stB[32 * q:32 * q + 30, 0:256], scb[l, q], [], kB)
                dma(stC[32 * q:32 * q + 2, 0:256], scc[l, q], [], kC)
                dma(stC[32 * q + 2:32 * q + 17, 0:256], spd[l, q], [], kC)
            for c in range(2):
                b = PSA.get()
                tp(ps[b][:, 0:128], stB[:, c * 128:(c + 1) * 128], ident[:], kB + ['ident'], [P(b)])
                tp(ps[b][:, 128:256], stC[:, c * 128:(c + 1) * 128], ident[:], kC + ['ident'], [P(b)])
                pv_b = ps[b][:, 0:128].rearrange("p (q r) -> p q r", r=32)
                pv_c = ps[b][:, 128:256].rearrange("p (q r) -> p q r", r=32)
                cp(bext[:, c, 0:248].rearrange("p (q t) -> p q t", t=62)[:, :, 0:30], pv_b[:, :, 0:30], [P(b)], ['bext'])
                cp(cext[:, c, 0:136].rearrange("p (q t) -> p q t", t=34)[:, :, 0:2], pv_c[:, :, 0:2], [P(b)], ['cext'])
                cp(dext[:, c, 0:188].rearrange("p (q t) -> p q t", t=47)[:, :, 0:15], pv_c[:, :, 2:17], [P(b)], ['dext'])
                PSA.put(b)
            kst = sgb[:].bitcast(BF16).rearrange("p a (m c) -> p (a m) c", c=256)
            ks_ = [('sgb', 0), ('sgb', 1)]
            dma(kst, ck[l].rearrange("q (mc p) c -> p (q mc) c", p=128), [], ks_, key=('kv', 0), eng='pool')
            dma(VV[:].rearrange("p a b c -> p (a b) c"), cv[l].rearrange("q (mc p) c -> p (q mc) c", p=128), [],
                [('VV', i) for i in range(L)], key=('kv', 1), eng='pool')
            for q in range(4):
                b = PSA.get()
                pbf = ps[b][:].bitcast(BF16)
                for c in range(2):
                    for mc in range(2):
                        o0 = (c * 2 + mc) * 128
                        tp(pbf[:, o0:o0 + 128], kst[:, q * 2 + mc, c * 128:(c + 1) * 128], identb[:],
                           ks_ + ['identb'], [P(b)])
                cp(KT[:, q, :, :].rearrange("p c m -> p (c m)"), pbf[:, 0:512], [P(b)], [('KT', i) for i in range(L)])
                PSA.put(b)

        def sample_state_emit(l):
            stg_ = cacc[:].rearrange("p a b -> p (a b)")
            kC = [('cacc', 0), ('cacc', 1)]
            for (buf, key, H, wdt, dst) in ((bext, 'bext', 30, 30, cbs), (cext, 'cext', 2, 2, ccs),
                                            (dext, 'dext', 15, 15, pds)):
                tl = H + 32
                bs_ = [PSA.get(), PSA.get()]
                for q in range(4):
                    for c in range(2):
                        j = q * 2 + c
                        tp(ps[bs_[j // 4]][0:wdt, (j % 4) * 128:(j % 4 + 1) * 128],
                           buf[:, c, q * tl + tl - wdt:q * tl + tl], ident[:], [key, 'ident'], [P(bs_[j // 4])])
                for hb_ in range(2):
                    cp(stg_[0:wdt, hb_ * 512:(hb_ + 1) * 512], ps[bs_[hb_]][0:wdt, :], [P(bs_[hb_])], kC)
                    PSA.put(bs_[hb_])
                dma(dst[l].rearrange("q r c -> r q c"), stg_[0:wdt, :].rearrange("r (q c) -> r q c", c=256), kC, [])

        def phase_ab(l, si, off, n, first_of_seq, last_of_tile, cvs, part, samp=False, defer_ln=False):
            xr = [('xn', kc, si) for kc in range(8)]
            nb = n // 128
            bextb = bextbs[si]
            pooled = pooleds[si]
            vabuf = vabufs[si]
            isH = (part == 'H')
            isT = (part == 'T')

            def v(ap):
                return ap.rearrange("p (q t) -> p q t", t=32) if samp else ap

            def ext_new(buf, mi, H):
                if samp:
                    return buf[:, mi, 0:4 * (H + 32)].rearrange("p (q t) -> p q t", t=H + 32)[:, :, H:H + 32]
                return buf[:, mi, H:H + n]

            def ext_out(ap2, H):
                if samp:
                    return ap2[:, 0:4 * (H + 32)].rearrange("p (q t) -> p q t", t=H + 32)[:, :, 0:32]
                return ap2[:, 0:n]

            NCB = 4 * 62 - 30 if samp else n
            NCC = 4 * 34 - 2 if samp else n

            def zmm(b, wv, wk, mi):
                for kc in range(8):
                    mm(ps[b][:, 0:n], wv[:, kc, mi * 128:(mi + 1) * 128], xn[:, kc, off:off + n], kc == 0, kc == 7,
                       xr + [wk], [P(b)])

            if isH:
                if samp:
                    sample_state_load(l)
                elif si == 0:
                    cp(bext[:, :, 0:30], bhist[:, l, :, :], ['bhist'], ['bext'], eng='pool')
                    cp(cext[:, :, 0:2], chist[:, l, :, :], ['chist'], ['cext'], eng='pool')
                    cp(dext[:, :, 0:15], dhist[:, l, :, :], ['dhist'], ['dext'], eng='pool')
                else:
                    cp(bext[:, :, 0:30], bext[:, :, 512:542], ['bext'], ['bext'], eng='pool')
                    cp(cext[:, :, 0:2], cext[:, :, 512:514], ['cext'], ['cext'], eng='pool')
                    cp(dext[:, :, 0:15], dext[:, :, 512:527], ['dext'], ['dext'], eng='pool')

                j3, k3, w3 = WR.acquire(('win', l, 3))
                for mi in range(2):
                    b = PSA.get()
                    zmm(b, w3, k3, mi)
                    act(sgb[:, mi, 0:n], ps[b][:, 0:n], AF.Sigmoid, [P(b)], [('sgb', mi)])
                    PSA.put(b)
                WR.release(j3)
                j2, k2, w2 = WR.acquire(('win', l, 2))
                for mi in range(2):
                    b = PSA.get()
                    zmm(b, w2, k2, mi)
                    tt(ext_new(bext, mi, 30), v(ps[b][:, 0:n]), v(sgb[:, mi, 0:n]), ALU.mult, [P(b), ('sgb', mi)], ['bext'])
                    PSA.put(b)
                WR.release(j2)
                if last_of_tile and not samp:
                    cp(bhist[:, l, :, :], bext[:, :, n:n + 30], ['bext'], ['bhist'], eng='pool')
            if isH:
                XB = 30 + NCB
                act(bextb[:, :, 0:XB], bext[:, :, 0:XB], AF.Copy, ['bext'], [('bextb', si)])
            if isT:
                for c in range(2):
                    b = PSA.get()
                    for k in range(31):
                        e_ = c * 31 + k
                        jc, kc_, wc = cvs[e_ // 16]
                        kk = e_ % 16
                        mm(ps[b][:, 0:NCB], wc[:, kk, :], bextb[:, c, k:k + NCB], k == 0, k == 30,
                           [kc_, ('wrx', kc_[1], kk), ('bextb', si)], [P(b)])
                    act(acc[:, c, 0:NCB], ps[b][:, 0:NCB], AF.Identity, [P(b), 'PV'], [('acc', c)],
                        bias=pv('bcb', l * 2 + c))
                    PSA.put(b)
            if isH:
                j5, k5, w5 = WR.acquire(('win', l, 5))
                for mi in range(2):
                    b = PSA.get()
                    zmm(b, w5, k5, mi)
                    act(ext_new(cext, mi, 2), v(ps[b][:, 0:n]), AF.Copy, [P(b)], ['cext'])
                    PSA.put(b)
                WR.release(j5)
                j6, k6, w6 = WR.acquire(('win', l, 6))
                for mi in range(2):
                    b = PSA.get()
                    zmm(b, w6, k6, mi)
                    tt(ext_new(cext, mi, 2), v(ps[b][:, 0:n]), ext_new(cext, mi, 2), ALU.mult, [P(b), 'cext'], ['cext'])
                    PSA.put(b)
                WR.release(j6)
                if last_of_tile and not samp:
                    cp(chist[:, l, :, :], cext[:, :, n:n + 2], ['cext'], ['chist'], eng='pool')
                j4, k4, w4 = WR.acquire(('win', l, 4))
                for mi in range(2):
                    b = PSA.get()
                    zmm(b, w4, k4, mi)
                    act(hb[:, 12 + mi, off:off + n], ps[b][:, 0:n], AF.Copy, [P(b)], [('h', 12 + mi, si)])
                    PSA.put(b)
                WR.release(j4)
                for c in range(2):
                    ts(cacc[:, c, 0:NCC], cext[:, c, 0:NCC], pv('ccw', (l * 3 + 0) * 2 + c), None, ALU.mult, None,
                       ['cext', 'PV'], [('cacc', c)])
                    for k in (1, 2):
                        stt(cacc[:, c, 0:NCC], cext[:, c, k:k + NCC], pv('ccw', (l * 3 + k) * 2 + c), cacc[:, c, 0:NCC],
                            ALU.mult, ALU.add, ['cext', 'PV', ('cacc', c)], [('cacc', c)])
                    tt(v(hb[:, 12 + c, off:off + n]), v(hb[:, 12 + c, off:off + n]), ext_out(cacc[:, c, :], 2), ALU.mult,
                       [('h', 12 + c, si), ('cacc', c)], [('h', 12 + c, si)])

            if isH:
                j7, k7, w7 = WR.acquire(('win', l, 7))
                for mi in range(2):
                    b = PSA.get()
                    zmm(b, w7, k7, mi)
                    act(ext_new(dext, mi, 15), v(ps[b][:, 0:n]), AF.Copy, [P(b)], ['dext'])
                    PSA.put(b)
                WR.release(j7)
                if last_of_tile and not samp:
                    cp(dhist[:, l, :, :], dext[:, :, n:n + 15], ['dext'], ['dhist'], eng='pool')
                E = 4 * 47 if samp else 15 + n

                def dnew(ap2):
                    if samp:
                        return ap2[:, 0:188].rearrange("p (q t) -> p q t", t=47)[:, :, 15:47]
                    return ap2[:, 15:15 + n]

                for c in range(2):
                    dd = dext[:, c, :]
                    tt(trA[:, 1:E], dd[:, 1:E], dd[:, 0:E - 1], ALU.add, ['dext'], ['trA'])
                    tt(trB[:, 3:E], trA[:, 3:E], trA[:, 1:E - 2], ALU.add, ['trA'], ['trB'])
                    if c == 0:
                        srcs = [(trA, 2.0), (trB, 4.0)]
                    else:
                        tt(trA[:, 7:E], trB[:, 7:E], trB[:, 3:E - 4], ALU.add, ['trB'], ['trA'])
                        tt(trB[:, 15:E], trA[:, 15:E], trA[:, 7:E - 8], ALU.add, ['trA'], ['trB'])
                        srcs = [(trA, 8.0), (trB, 16.0)]
                    for gi, (sbuf_, wlen) in enumerate(srcs):
                        pr = slice(64 * gi, 64 * gi + 64)
                        kk = 'trA' if sbuf_ is trA else 'trB'
                        stt(v(pooled[pr, c, 0:n]), dnew(sbuf_[pr, :]), 1.0 / wlen, dnew(dext[pr, c, :]),
                            ALU.mult, ALU.subtract, [kk, 'dext'], [('pooled', si, c)])
                        if first_of_seq and si == 0 and not samp:
                            tt(meansb[pr, 0:15], sbuf_[pr, 15:30], rct[pr, c, :], ALU.mult, [kk, 'rct'], ['meansb'])
                            tt(pooled[pr, c, 0:15], meansb[pr, 0:15], dext[pr, c, 15:30], ALU.subtract,
                               ['meansb', 'dext'], [('pooled', si, c)])
            if isT:
                for c in range(2):
                    b = PSA.get()
                    mm(ps[b][:, 0:n], dwbd[:, c, :], pooled[:, c, 0:n], True, True, ['dwbd', ('pooled', si, c)], [P(b)])
                    ts(hb[:, 14 + c, off:off + n], ps[b][:, 0:n], pv('dsc', l * 2 + c), None, ALU.mult, None,
                       [P(b), 'PV'], [('h', 14 + c, si)])
                    PSA.put(b)

            if isH:
                j8, k8, w8 = WR.acquire(('win', l, 8))
                for mi in range(2):
                    b = PSA.get()
                    zmm(b, w8, k8, mi)
                    act(hb[:, 16 + mi, off:off + n], ps[b][:, 0:n], AF.Copy, [P(b)], [('h', 16 + mi, si)], scale=0.125)
                    PSA.put(b)
                WR.release(j8)
            if isT:
                pti = 0
                ktk = [('KT', i) for i in range(L)] if samp else [('KT', l)]
                vvk = [('VV', i) for i in range(L)] if samp else [('VV', l)]
                for c in range(2):
                    bo = PSA.get()
                    bd = PSA.get()
                    if not samp:
                        sbk = []
                        for hi in range(2):
                            pr = slice(64 * hi, 64 * hi + 64)
                            for mc in range(2):
                                b = PSA.get()
                                mm(ps[b][:, 0:n], KT[pr, l, c, mc * 128:(mc + 1) * 128], hb[pr, 16 + c, off:off + n],
                                   True, True, ktk + [('h', 16 + c, si)], [P(b)])
                                sbk.append(b)
                        for i4 in range(4):
                            b = sbk[i4]
                            act(PT[i4][:, 0:n], ps[b][:, 0:n], AF.Exp, [P(b)], [('PT', i4)])
                            PSA.put(b)
                        for hi in range(2):
                            h = 2 * c + hi
                            pr = slice(64 * hi, 64 * hi + 64)
                            for mc in range(2):
                                i4 = hi * 2 + mc
                                mm(ps[bo][pr, 0:n], VV[:, l, mc, 64 * h:64 * h + 64], PT[i4][:, 0:n], mc == 0, mc == 1,
                                   vvk + [('PT', i4)], [P(bo)])
                        for hi in range(2):
                            pr = slice(64 * hi, 64 * hi + 64)
                            for mc in range(2):
                                i4 = hi * 2 + mc
                                mm(ps[bd][pr, 0:n], ones_b[:, 0:64], PT[i4][:, 0:n], mc == 0, mc == 1,
                                   ['ones_b', ('PT', i4)], [P(bd)])
                    else:
                        pts = []
                        for hi in range(2):
                            pr = slice(64 * hi, 64 * hi + 64)
                            b = PSA.get()
                            for mc in range(2):
                                for q in range(4):
                                    o0 = (mc * 4 + q) * 32
                                    mm(ps[b][:, o0:o0 + 32], KT[pr, q, c, mc * 128:(mc + 1) * 128],
                                       hb[pr, 16 + c, 32 * q:32 * q + 32], True, True, ktk + [('h', 16 + c, si)], [P(b)])
                            pt = PT[pti % 4]
                            pk = ('PT', pti % 4)
                            pti += 1
                            act(pt[:, 0:256], ps[b][:, 0:256], AF.Exp, [P(b)], [pk])
                            PSA.put(b)
                            pts.append((pt, pk))
                        for hi in range(2):
                            h = 2 * c + hi
                            pr = slice(64 * hi, 64 * hi + 64)
                            pt, pk = pts[hi]
                            for q in range(4):
                                for mc in range(2):
                                    o0 = (mc * 4 + q) * 32
                                    mm(ps[bo][pr, 32 * q:32 * q + 32], VV[:, q, mc, 64 * h:64 * h + 64], pt[:, o0:o0 + 32],
                                       mc == 0, mc == 1, vvk + [pk], [P(bo)])
                            for q in range(4):
                                for mc in range(2):
                                    o0 = (mc * 4 + q) * 32
                                    mm(ps[bd][pr, 32 * q:32 * q + 32], ones_b[:, 0:64], pt[:, o0:o0 + 32],
                                       mc == 0, mc == 1, ['ones_b', pk], [P(bd)])
                    act(den[:, 0:n], ps[bd][:, 0:n], AF.Copy, [P(bd)], ['den'])
                    PSA.put(bd)
                    recip(den[:, 0:n], den[:, 0:n], ['den'], ['den'])
                    tt(hb[:, 16 + c, off:off + n], ps[bo][:, 0:n], den[:, 0:n], ALU.mult, [P(bo), 'den'],
                       [('h', 16 + c, si)])
                    PSA.put(bo)

            if isH:
                j0, k0, w0 = WR.acquire(('win', l, 0))
                for mi in range(2):
                    b = PSA.get()
                    zmm(b, w0, k0, mi)
                    act(hb[:, 8 + mi, off:off + n], ps[b][:, 0:n], AF.Gelu_apprx_tanh, [P(b)], [('h', 8 + mi, si)])
                    PSA.put(b)
                WR.release(j0)
                j1, k1, w1 = WR.acquire(('win', l, 1))
                for blk in range(nb):
                    t0 = off + blk * 128
                    b = PSA.get()
                    for kc in range(8):
                        mm(ps[b][:, 0:256], xn[:, kc, t0:t0 + 128], w1[:, kc, :], kc == 0, kc == 7, xr + [k1], [P(b)])
                    u = blk % 4
                    act(gv[u][:], ps[b][:, 0:256], AF.Gelu_apprx_tanh, [P(b)], ['gv%d' % u])
                    PSA.put(b)
                    S.add('dve', lambda e, u=u: e.bn_stats(out=bnst[u][:], in_=gv[u][:]), ['gv%d' % u], ['bnst%d' % u])
                    S.add('dve', lambda e, u=u: e.bn_aggr(out=bnmv_all[:, u, :], in_=bnst[u][:]), ['bnst%d' % u],
                          [('bnmv', u)])
                bk = [('bnmv', u) for u in range(nb)]
                varv = bnmv_all[:, 0:nb, 1:2]
                ts(varv, varv, 0.0, EPS, ALU.max, ALU.add, bk, bk)
                act(varv, varv, AF.Sqrt, bk, bk)
                recip(varv, varv, bk, bk)
                for blk in range(nb):
                    u = blk % 4
                    ts(gv[u][:], gv[u][:], bnmv_all[:, u, 0:1], bnmv_all[:, u, 1:2], ALU.subtract, ALU.mult,
                       ['gv%d' % u, ('bnmv', u)], ['gv%d' % u])
                    tt(gv[u][:], gv[u][:], lnab[:, 0, :], ALU.mult, ['gv%d' % u, 'lnab'], ['gv%d' % u])
                    if samp:
                        tt(gv[u][:], gv[u][:], lnab[:, 1, :], ALU.add, ['gv%d' % u, 'lnab'], ['gv%d' % u])
                        cp(vabuf[:, blk, :], gv[u][:], ['gv%d' % u], [('va', si, blk)])
                        dma(avs[l], gv[u][:], ['gv%d' % u], [])
                    else:
                        tt(vabuf[:, blk, :], gv[u][:], lnab[:, 1, :], ALU.add, ['gv%d' % u, 'lnab'], [('va', si, blk)])
                WR.release(j1)
            if isT:
                for c in range(2):
                    b = PSA.get()
                    for blk in range(nb):
                        for gi in range(2):
                            g = 2 * c + gi
                            pr = slice(64 * gi, 64 * gi + 64)
                            cs = slice(blk * 128, (blk + 1) * 128)
                            mm(ps[b][pr, cs], vabuf[:, blk, 128 * c + 64 * gi:128 * c + 64 * gi + 64], wmT[:, g, :],
                               True, False, [('va', si, blk), 'wmT'], [P(b)])
                            mm(ps[b][pr, cs], ones_b[0:1, 0:64], bsrow[0:1, g * 128:(g + 1) * 128], False, True,
                               ['ones_b', 'bsrow'], [P(b)])
                    tt(hb[:, 8 + c, off:off + n], ps[b][:, 0:n], hb[:, 8 + c, off:off + n], ALU.mult,
                       [P(b), ('h', 8 + c, si)], [('h', 8 + c, si)])
                    PSA.put(b)
            def lnb():
                for c in range(2):
                    act(v(accb[:, c, 0:n]), ext_out(acc[:, c, :], 30), AF.Copy, [('acc', c)], [('accb', c)])
                    act(v(sqb[:, c, 0:n]), ext_out(acc[:, c, :], 30), AF.Square, [('acc', c)], [('sqb', c)])
                bm = PSA.get()
                be = PSA.get()
                for c in range(2):
                    mm(ps[bm][:, 0:n], onesw_b[:], accb[:, c, 0:n], c == 0, c == 1, [('accb', c), 'onesw_b'], [P(bm)])
                for c in range(2):
                    mm(ps[be][:, 0:n], onesw_b[:], sqb[:, c, 0:n], c == 0, c == 1, [('sqb', c), 'onesw_b'], [P(be)])
                act(msq[:, 0:n], ps[bm][:, 0:n], AF.Square, [P(bm)], ['msq'])
                act(meansb[:, 0:n], ps[bm][:, 0:n], AF.Copy, [P(bm)], ['meansb'])
                PSA.put(bm)
                tt(varsb[:, 0:n], ps[be][:, 0:n], msq[:, 0:n], ALU.subtract, [P(be), 'msq'], ['varsb'])
                PSA.put(be)
                ts(varsb[:, 0:n], varsb[:, 0:n], 0.0, EPS, ALU.max, ALU.add, ['varsb'], ['varsb'])
                act(varsb[:, 0:n], varsb[:, 0:n], AF.Sqrt, ['varsb'], ['varsb'])
                recip(varsb[:, 0:n], varsb[:, 0:n], ['varsb'], ['varsb'])
                for c in range(2):
                    ao = ext_out(acc[:, c, :], 30)
                    tt(ao, ao, v(meansb[:, 0:n]), ALU.subtract, [('acc', c), 'meansb'], [('acc', c)])
                    tt(ao, ao, v(varsb[:, 0:n]), ALU.mult, [('acc', c), 'varsb'], [('acc', c)])
                    act(v(hb[:, 10 + c, off:off + n]), ao, AF.Silu, [('acc', c), 'PV'], [('h', 10 + c, si)],
                        scale=pv('blg', l * 2 + c), bias=pv('blb', l * 2 + c))

            ret = None
            if isT:
                if defer_ln:
                    ret = lnb
                else:
                    lnb()
            if samp and isT:
                sample_state_emit(l)
            return ret

        def phase_c(l, sts, hook=None):
            it = 0
            for mp in range(4):
                accs = {}
                for si in range(len(sts)):
                    for mi in range(2):
                        accs[(si, mi)] = PSA.get()
                pending = [None]
                for nn in NN_ORDER:
                    jg, kg, wg = WR.acquire(('gate', l, mp, nn))
                    jb, kb, wb = WR.acquire(('br', l, mp, nn))
                    for si, (off, n) in enumerate(sts):
                        xr = [('xn', kc, si) for kc in range(8)]
                        for mi in range(2):
                            m = mp * 2 + mi
                            ba = accs[(si, mi)]
                            bg_ = PSA.get()
                            for kc in range(8):
                                mm(ps[bg_][:, 0:n], wg[:, kc, mi * 128:(mi + 1) * 128], xn[:, kc, off:off + n],
                                   kc == 0, kc == 7, xr + [kg], [P(bg_)])
                            bp = PSA.get()
                            for c in range(2):
                                mm(ps[bp][:, 0:n], wb[:, c, mi * 128:(mi + 1) * 128], hb[:, 8 + 2 * nn + c, off:off + n],
                                   c == 0, c == 1, [kb, ('h', 8 + 2 * nn + c, si)], [P(bp)])
                            u = it % 3
                            it += 1
                            act(sgC[u][:, 0:n], ps[bg_][:, 0:n], AF.Sigmoid, [P(bg_), 'PV'], [('sgC', u)],
                                bias=bgcol(l, nn, m))
                            PSA.put(bg_)
                            tt(gpC[u][:, 0:n], ps[bp][:, 0:n], sgC[u][:, 0:n], ALU.mult, [P(bp), ('sgC', u)],
                               [('gpC', u)])
                            PSA.put(bp)
                            if pending[0] is not None:
                                pending[0]()

                            def fin(u=u, nn=nn, ba=ba, n=n, off=off, m=m, si=si):
                                mm(ps[ba][:, 0:n], identb[:], gpC[u][:, 0:n], nn == NN_ORDER[0], nn == NN_ORDER[-1],
                                   ['identb', ('gpC', u)], [P(ba)])
                                if nn == NN_ORDER[-1]:
                                    act(hb[:, m, off:off + n], ps[ba][:, 0:n], AF.Copy, [P(ba)], [('h', m, si)])
                                    PSA.put(ba)
                            pending[0] = fin
                    WR.release(jg)
                    WR.release(jb)
                    if hook is not None and mp == 0 and nn == NN_ORDER[0]:
                        hook()
                if pending[0] is not None:
                    pending[0]()

        def phase_d(l, sts, after_st0=None, mid_st1=None):
            slabs = [WR.acquire(('wout', l, j)) for j in range(4)]
            for si, (off, n) in enumerate(sts):
                for j in range(4):
                    jw, kw, ww = slabs[j]
                    for mi in range(2):
                        m = 2 * j + mi
                        b = PSA.get()
                        for kc in range(8):
                            mm(ps[b][:, 0:n], ww[:, kc, mi * 128:(mi + 1) * 128], hb[:, kc, off:off + n],
                               kc == 0, kc == 7, [kw, ('h', kc, si)], [P(b)])
                        tt(x[:, m, off:off + n], ps[b][:, 0:n], x[:, m, off:off + n], ALU.add,
                           [P(b), ('x', m, si)], [('x', m, si)])
                        PSA.put(b)
                    if si == 1 and j == 0 and mid_st1 is not None:
                        mid_st1()
                if si == 0 and after_st0 is not None:
                    after_st0()
            for (jw, kw, ww) in slabs:
                WR.release(jw)

        def phase_e(l, sts, after_st0=None, mid_st1=None, pre_down_b=None):
            it = 0
            for half in range(2):
                j0, j1 = FH[half]
                for j in range(j0, j1):
                    jg, kg, wg = WR.acquire(('fg', l, j))
                    ju, ku, wu = WR.acquire(('fu', l, j))
                    for si, (off, n) in enumerate(sts):
                        xr = [('xn', kc, si) for kc in range(8)]
                        for mi in range(2):
                            f = (j - j0) * 2 + mi
                            bg_ = PSA.get()
                            for kc in range(8):
                                mm(ps[bg_][:, 0:n], wg[:, kc, mi * 128:(mi + 1) * 128], xn[:, kc, off:off + n],
                                   kc == 0, kc == 7, xr + [kg], [P(bg_)])
                            bu = PSA.get()
                            for kc in range(8):
                                mm(ps[bu][:, 0:n], wu[:, kc, mi * 128:(mi + 1) * 128], xn[:, kc, off:off + n],
                                   kc == 0, kc == 7, xr + [ku], [P(bu)])
                            u = it % 3
                            it += 1
                            act(sgC[u][:, 0:n], ps[bg_][:, 0:n], AF.Silu, [P(bg_)], [('sgC', u)])
                            PSA.put(bg_)
                            tt(hb[:, f, off:off + n], ps[bu][:, 0:n], sgC[u][:, 0:n], ALU.mult,
                               [P(bu), ('sgC', u)], [('h', f, si)])
                            PSA.put(bu)
                    WR.release(jg)
                    WR.release(ju)
                nk = (j1 - j0) * 2

                def down(m, si, off, n, kd, wd):
                    b = PSA.get()
                    for f in range(nk):
                        mm(ps[b][:, 0:n], wd[:, f, :], hb[:, f, off:off + n], f == 0, f == nk - 1,
                           [kd, ('h', f, si)], [P(b)])
                    tt(x[:, m, off:off + n], ps[b][:, 0:n], x[:, m, off:off + n], ALU.add,
                       [P(b), ('x', m, si)], [('x', m, si)])
                    PSA.put(b)

                if half == 0:
                    for m in range(8):
                        jd, kd, wd = WR.acquire(('dn', l, half, m))
                        for si, (off, n) in enumerate(sts):
                            down(m, si, off, n, kd, wd)
                        WR.release(jd)
                else:
                    if pre_down_b is not None:
                        pre_down_b()
                    for si, (off, n) in enumerate(sts):
                        for m in range(8):
                            jd, kd, wd = WR.acquire(('dn', l, half, m))
                            down(m, si, off, n, kd, wd)
                            WR.release(jd)
                            if si == 1 and m == 1 and mid_st1 is not None:
                                mid_st1()
                        if si == 0 and after_st0 is not None:
                            after_st0()

        def emit_state(hist, l, width, dst):
            b = PSA.get()
            for c in range(2):
                tp(ps[b][0:width, c * 128:(c + 1) * 128], hist[:, l, c, :], ident[:], ['bhist', 'chist', 'dhist', 'ident'],
                   [P(b)])
            cp(sttmp[0:width, :], ps[b][0:width, 0:256], [P(b)], ['sttmp'])
            PSA.put(b)
            dma(dst, sttmp[0:width, :], ['sttmp'], [])

        PSTS = [(0, 512), (512, 512)]
        nxt_cvs = [None]
        full = []
        for seq in range(KSEQS):
            for l in range(L):
                full += [('kvK', l), ('kvV', l)]
            for t in range(KTILES):
                for l in range(DEPTH_RUN):
                    full += layer_seq(l, 2, l == 0, l == DEPTH_RUN - 1)
        if KSAMP:
            for l in range(DEPTH_RUN):
                full += layer_seq(l, 1, l == 0, l == DEPTH_RUN - 1)
        WR.seq = full

        if KSTOP < 99:
            full = []
            if KSTOP >= 1:
                for l in range(L):
                    full += [('kvK', l), ('kvV', l)]
            if KSTOP >= 4:
                full += [('win', 0, j) for j in (3, 2, 5, 6, 4, 7, 8, 0, 1)]
            WR.seq = full
        for seq in range(KSEQS):
            if KSTOP >= 1:
                prologue(seq)
            if KSTOP < 99:
                if KSTOP >= 2:
                    load_tokens(lambda tb, seq=seq: xp[seq, tb * 128:(tb + 1) * 128, :], 8)
                if KSTOP >= 3:
                    layer_setup(0)
                    rmsnorm(0, 512, 0, 'n1g', 0, xn, 'xn')
                if KSTOP >= 4:
                    phase_ab(0, 0, 0, 512, True, False, None, 'H')
                store_tokens(lambda tb, seq=seq: yp[seq, tb * 128:(tb + 1) * 128, :], 4, [(0, 512)])
                break
            for t in range(KTILES):
                tok0 = t * TP
                load_tokens(lambda tb, seq=seq, tok0=tok0: xp[seq, tok0 + tb * 128:tok0 + (tb + 1) * 128, :], 8)
                if t == 0:
                    memset(bhist[:], 0.0, ['bhist'])
                    memset(chist[:], 0.0, ['chist'])
                    memset(dhist[:], 0.0, ['dhist'])
                for l in range(DEPTH_RUN):
                    layer_setup(l)
                    for si, (off, n) in enumerate(PSTS):
                        if si == 0 and l > 0:
                            continue
                        rmsnorm(off, n, si, 'n1g', l * 8, xn, 'xn')
                    if l == 0:
                        cvs = [WR.acquire(('cvd', l, i)) for i in range(4)]
                    else:
                        cvs = nxt_cvs[0]
                    for si, (off, n) in enumerate(PSTS):
                        phase_ab(l, si, off, n, t == 0, si == len(PSTS) - 1, cvs, 'H')
                    lnb_hook = None
                    for si, (off, n) in enumerate(PSTS):
                        r_ = phase_ab(l, si, off, n, t == 0, si == len(PSTS) - 1, cvs, 'T',
                                      defer_ln=(si == len(PSTS) - 1))
                        if r_ is not None:
                            lnb_hook = r_
                    for (jc, kc_, wc) in cvs:
                        WR.release(jc)
                    phase_c(l, PSTS, hook=lnb_hook)
                    o0, n0 = PSTS[0]
                    phase_d(l, PSTS, after_st0=lambda: rms_a(o0, n0, 0),
                            mid_st1=lambda l=l: rms_b(o0, n0, 0, 'n2g', l * 8, xn, 'xn'))
                    rmsnorm(PSTS[1][0], PSTS[1][1], 1, 'n2g', l * 8, xn, 'xn')
                    if l < DEPTH_RUN - 1:
                        def acq_next(l=l):
                            nxt_cvs[0] = [WR.acquire(('cvd', l + 1, i)) for i in range(4)]
                        phase_e(l, PSTS, after_st0=lambda: rms_a(o0, n0, 0),
                                mid_st1=lambda l=l: rms_b(o0, n0, 0, 'n1g', (l + 1) * 8, xn, 'xn'),
                                pre_down_b=acq_next)
                    else:
                        phase_e(l, PSTS)
                    if t == 1:
                        emit_state(bhist, l, 30, cbp[l, seq])
                        emit_state(chist, l, 2, ccp[l, seq])
                        emit_state(dhist, l, 15, pdp[l, seq])
                store_tokens(lambda tb, seq=seq, tok0=tok0: yp[seq, tok0 + tb * 128:tok0 + (tb + 1) * 128, :], 8, PSTS)

        if KSAMP and KSTOP >= 99:
            SSTS = [(0, 128)]
            load_tokens(lambda tb: xs[0:128, :], 1)
            for l in range(DEPTH_RUN):
                layer_setup(l, samp=True)
                rmsnorm(0, 128, 0, 'n1g', l * 8, xn, 'xn')
                if l == 0:
                    cvs = [WR.acquire(('cvd', l, i)) for i in range(4)]
                else:
                    cvs = nxt_cvs[0]
                phase_ab(l, 0, 0, 128, False, True, cvs, 'H', samp=True)
                phase_ab(l, 0, 0, 128, False, True, cvs, 'T', samp=True)
                for (jc, kc_, wc) in cvs:
                    WR.release(jc)
                phase_c(l, SSTS)
                phase_d(l, SSTS)
                rmsnorm(0, 128, 0, 'n2g', l * 8, xn, 'xn')
                if l < DEPTH_RUN - 1:
                    def acq_next_s(l=l):
                        nxt_cvs[0] = [WR.acquire(('cvd', l + 1, i)) for i in range(4)]
                    phase_e(l, SSTS, pre_down_b=acq_next_s)
                else:
                    phase_e(l, SSTS)
            store_tokens(lambda tb: ys[0:128, :], 1, SSTS)

        assert WR.nxt == len(WR.seq), (WR.nxt, len(WR.seq))
        S.emit(es)
    return nc


_CACHE = {}


def kernel(**inp):
    f32 = np.float32
    g = {k: np.ascontiguousarray(np.asarray(v, dtype=f32)) for k, v in inp.items()}
    if 'nc' not in _CACHE:
        _CACHE['nc'] = build_program()
    nc = _CACHE['nc']
    wl = [1.0, 1.0, 1.0, 1.0]
    rc = np.zeros((128, 2, 15), f32)
    wins = {(0, 0): 2, (0, 1): 4, (1, 0): 8, (1, 1): 16}
    for c in range(2):
        for gi in range(2):
            w = wins[(c, gi)]
            for t in range(15):
                rc[64 * gi:64 * gi + 64, c, t] = 1.0 / min(w, t + 1)
    wnames = ["norm1_g", "mem_norm_g", "w_in", "a_ln_g", "a_ln_b", "a_ws", "a_bs", "b_conv_w", "b_conv_b", "b_ln_g",
              "b_ln_b", "c_conv_w", "d_w", "d_scale", "w_mem_kv", "w_branch", "w_gate", "b_gate", "w_out", "norm2_g",
              "w_ffn_gate", "w_ffn_up", "w_ffn_down", "final_norm_g"]
    in_maps = []
    for c in range(NCORES):
        m = {k: g[k] for k in wnames}
        m["xp"] = g["x_prompt"][2 * c:2 * c + 2]
        m["xs"] = g["x_sample"][4 * c:4 * c + 4].reshape(128, D)
        m["memp"] = g["mem_prompt"][2 * c:2 * c + 2]
        m["ck"] = np.ascontiguousarray(g["cache_mem_k"][:, 4 * c:4 * c + 4].reshape(L, 4, NMEM, W))
        m["cv"] = np.ascontiguousarray(g["cache_mem_v"][:, 4 * c:4 * c + 4].reshape(L, 4, NMEM, W))
        m["scb"] = np.ascontiguousarray(g["state_conv_b"][:, 4 * c:4 * c + 4])
        m["scc"] = np.ascontiguousarray(g["state_conv_c"][:, 4 * c:4 * c + 4])
        m["spd"] = np.ascontiguousarray(g["state_pool_d"][:, 4 * c:4 * c + 4])
        m["rctab"] = rc
        in_maps.append(m)
    res = run_bass_kernel_spmd(nc, in_maps, core_ids=list(range(NCORES)))
    R = res.results

    def cat(name, axis):
        return np.concatenate([np.asarray(r[name]) for r in R], axis=axis)

    y_prompt = cat("yp", 0)
    y_sample = cat("ys", 0).reshape(32, 32, D)
    new_mem_k = cat("mk", 1).reshape(L, 16, NMEM, 4, 64)
    new_mem_v = cat("mv", 1).reshape(L, 16, NMEM, 4, 64)
    cb_p = cat("cbp", 1)
    cc_p = cat("ccp", 1)
    pd_p = cat("pdp", 1)
    av_s = cat("avs", 1).reshape(L, 32, 32, W)
    cb_s = cat("cbs", 1)
    cc_s = cat("ccs", 1)
    pd_s = cat("pds", 1)
    return (y_prompt, y_sample, new_mem_k, new_mem_v, cb_p, cc_p, pd_p, av_s, cb_s, cc_s, pd_s)
```

```python
import numpy as np
from contextlib import ExitStack
import concourse.bass as bass
import concourse.mybir as mybir
from concourse.bass_utils import run_bass_kernel_spmd

F32 = mybir.dt.float32
BF16 = mybir.dt.bfloat16
AF = mybir.ActivationFunctionType
ALU = mybir.AluOpType

NCORES = 8
D = 1024
W = 256
DFF = 2816
L = 4
SEQ = 2048
TP = 1024
NMEM = 256
EPS = 1e-6
ENGS = ('pe', 'act', 'dve', 'pool', 'sp')
NSLOT = 8
import os
DEPTH_RUN = int(os.environ.get('KDEPTH', L))
KSEQS = int(os.environ.get('KSEQS', 2))
KTILES = int(os.environ.get('KTILES', 2))
KSTOP = int(os.environ.get('KSTOP', 99))
KSAMP = int(os.environ.get('KSAMP', 1))


class Sched:
    def __init__(self, nc):
        self.nc = nc
        self.ops = {e: [] for e in ENGS}
        self.clock = {e: {} for e in ENGS}
        self.evclock = {}
        self.lastw = {}
        self.readers = {}
        self.dma_count = {}
        self.signals = {e: set() for e in ENGS}

    def add(self, eng, fn, reads=(), writes=(), dma_key=None, extra=()):
        deps = set(extra)
        for r in reads:
            ev = self.lastw.get(r)
            if ev is not None:
                deps.add(ev)
            if isinstance(r, tuple) and r[0] == 'ps':
                for rv in self.readers.get(r, ()):
                    if rv[0] != eng:
                        deps.add(rv)
        for w in writes:
            ev = self.lastw.get(w)
            if ev is not None:
                deps.add(ev)
            rd = self.readers.get(w)
            if rd:
                deps.update(rd)
        if dma_key is not None and self.dma_count.get(dma_key, 0) > 0:
            deps.add((dma_key, self.dma_count[dma_key]))
        clk = self.clock[eng]
        waits = []
        best = {}
        for (s, n) in deps:
            if best.get(s, 0) < n:
                best[s] = n
        for (s, n) in sorted(best.items(), key=lambda t: str(t[0])):
            if clk.get(s, 0) >= n:
                continue
            waits.append((s, n))
            oc = self.evclock[(s, n)]
            for k, v in oc.items():
                if clk.get(k, 0) < v:
                    clk[k] = v
            if s in self.signals:
                self.signals[s].add(n)
        idx = len(self.ops[eng]) + 1
        if dma_key is None:
            ev = (eng, idx)
            if eng == 'pe':
                clk['pe'] = idx
        else:
            n = self.dma_count.get(dma_key, 0) + 1
            self.dma_count[dma_key] = n
            ev = (dma_key, n)
        snap = dict(clk)
        snap[ev[0]] = ev[1]
        self.evclock[ev] = snap
        self.ops[eng].append((fn, waits, idx, ev if dma_key is not None else None))
        for r in reads:
            self.readers.setdefault(r, []).append(ev)
        for w in writes:
            self.lastw[w] = ev
            self.readers[w] = []
        return ev

    def emit(self, es):
        nc = self.nc
        sems = {}
        for e in ENGS:
            if self.signals[e]:
                sems[e] = es.enter_context(nc.semaphore("s_" + e))
        for i, k in enumerate(self.dma_count):
            sems[k] = es.enter_context(nc.semaphore("d%d" % i))
        rank = {}
        for e in ENGS:
            for i, n in enumerate(sorted(self.signals[e])):
                rank[(e, n)] = i + 1

        def val(s, n):
            if s in self.signals:
                return rank[(s, n)]
            return 16 * n

        block = es.enter_context(nc.Block())

        def run(engname, eng):
            sig = self.signals[engname]
            for (fn, waits, idx, dma) in self.ops[engname]:
                for (s, n) in waits:
                    eng.wait_ge(sems[s], val(s, n))
                inst = fn(eng)
                if dma is not None:
                    inst.then_inc(sems[dma[0]], 16)
                elif idx in sig:
                    inst.then_inc(sems[engname], 1)
            if engname == 'sp':
                for k, n in self.dma_count.items():
                    eng.wait_ge(sems[k], 16 * n)

        @block.tensor
        def _(e):
            run('pe', e)

        @block.scalar
        def _(e):
            run('act', e)

        @block.vector
        def _(e):
            run('dve', e)

        @block.gpsimd
        def _(e):
            run('pool', e)

        @block.sync
        def _(e):
            run('sp', e)


def build_program():
    nc = bass.Bass("TRN2", target_bir_lowering=False)

    def din(name, shape):
        return nc.dram_tensor(name, list(shape), F32, kind="ExternalInput").ap()

    def dout(name, shape):
        return nc.dram_tensor(name, list(shape), F32, kind="ExternalOutput").ap()

    xp = din("xp", [2, SEQ, D])
    xs = din("xs", [128, D])
    memp = din("memp", [2, NMEM, D])
    ck = din("ck", [L, 4, NMEM, W])
    cv = din("cv", [L, 4, NMEM, W])
    scb = din("scb", [L, 4, 30, W])
    scc = din("scc", [L, 4, 2, W])
    spd = din("spd", [L, 4, 15, W])
    norm1_g = din("norm1_g", [L, D])
    mem_norm_g = din("mem_norm_g", [L, D])
    w_in = din("w_in", [L, D, 9 * W])
    a_ln_g = din("a_ln_g", [L, W])
    a_ln_b = din("a_ln_b", [L, W])
    a_ws = din("a_ws", [L, 4, 128, 128])
    a_bs = din("a_bs", [L, 4, 128])
    b_conv_w = din("b_conv_w", [L, 31, W])
    b_conv_b = din("b_conv_b", [L, W])
    b_ln_g = din("b_ln_g", [L, W])
    b_ln_b = din("b_ln_b", [L, W])
    c_conv_w = din("c_conv_w", [L, 3, W])
    d_w = din("d_w", [L, 4, 64, 64])
    d_scale = din("d_scale", [L, W])
    w_mem_kv = din("w_mem_kv", [L, D, 2 * W])
    w_branch = din("w_branch", [L, 5, W, D])
    w_gate = din("w_gate", [L, 5, D, D])
    b_gate = din("b_gate", [L, 5, D])
    w_out = din("w_out", [L, D, D])
    norm2_g = din("norm2_g", [L, D])
    w_ffn_gate = din("w_ffn_gate", [L, D, DFF])
    w_ffn_up = din("w_ffn_up", [L, D, DFF])
    w_ffn_down = din("w_ffn_down", [L, DFF, D])
    final_norm_g = din("final_norm_g", [D])
    rctab = din("rctab", [128, 2, 15])

    yp = dout("yp", [2, SEQ, D])
    ys = dout("ys", [128, D])
    mk = dout("mk", [L, 2, NMEM, W])
    mv = dout("mv", [L, 2, NMEM, W])
    cbp = dout("cbp", [L, 2, 30, W])
    ccp = dout("ccp", [L, 2, 2, W])
    pdp = dout("pdp", [L, 2, 15, W])
    avs = dout("avs", [L, 128, W])
    cbs = dout("cbs", [L, 4, 30, W])
    ccs = dout("ccs", [L, 4, 2, W])
    pds = dout("pds", [L, 4, 15, W])

    es = ExitStack()
    with es:
        S = Sched(nc)

        def SB(name, shape, dt):
            return es.enter_context(nc.sbuf_tensor(name, list(shape), dt))

        x = SB("x", [128, 8, TP], F32)
        xn = SB("xn", [128, 8, TP], BF16)
        hb = SB("hb", [128, 18, TP], BF16)
        wr = [SB("wr%d" % i, [128, 2048], BF16) for i in range(NSLOT)]
        KT = SB("KT", [128, L, 2, 256], BF16)
        VV = SB("VV", [128, L, 2, 256], BF16)
        bext = SB("bext", [128, 2, 30 + 512], F32)
        bextbs = [SB("bextb%d" % i, [128, 2, 30 + 512 + 2], BF16) for i in range(2)]
        cext = SB("cext", [128, 2, 2 + 512], F32)
        dext = SB("dext", [128, 2, 15 + 512], F32)
        acc = SB("acc", [128, 2, 512], F32)
        cacc = SB("cacc", [128, 2, 512], F32)
        trA = SB("trA", [128, 15 + 512], F32)
        trB = SB("trB", [128, 15 + 512], F32)
        pooleds = [SB("pooled%d" % i, [128, 2, 512], BF16) for i in range(2)]
        accb = SB("accb", [128, 2, 512], BF16)
        sqb = SB("sqb", [128, 2, 512], BF16)
        sgb = SB("sgb", [128, 2, 512], F32)
        vabufs = [SB("vabuf%d" % i, [128, 4, 256], BF16) for i in range(2)]
        gv = [SB("gv%d" % i, [128, 256], F32) for i in range(4)]
        bnst = [SB("bnst%d" % i, [128, 6], F32) for i in range(4)]
        bnmv_all = SB("bnmv_all", [128, 4, 2], F32)
        rs = SB("rs", [128, 512], F32)
        meansb = SB("meansb", [128, 512], F32)
        msq = SB("msq", [128, 512], F32)
        varsb = SB("varsb", [128, 512], F32)
        sgC = [SB("sgC%d" % i, [128, 512], F32) for i in range(3)]
        gpC = [SB("gpC%d" % i, [128, 512], BF16) for i in range(3)]
        PT = [SB("PT%d" % i, [128, 512], BF16) for i in range(4)]
        den = SB("den", [128, 512], F32)
        bhist = SB("bhist", [128, L, 2, 30], F32)
        chist = SB("chist", [128, L, 2, 2], F32)
        dhist = SB("dhist", [128, L, 2, 15], F32)
        ident = SB("ident", [128, 128], F32)
        identb = SB("identb", [128, 128], BF16)
        ones_b = SB("ones_b", [128, 128], BF16)
        onesw_b = SB("onesw_b", [128, 128], BF16)
        epst = SB("epst", [128, 1], F32)
        rct = SB("rct", [128, 2, 15], F32)
        NPV = 640
        PV = SB("PV", [128, NPV], F32)
        lnab = SB("lnab", [128, 2, 256], F32)
        wmT = SB("wmT", [128, 4, 128], BF16)
        bsrow = SB("bsrow", [1, 512], BF16)
        dwbd = SB("dwbd", [128, 2, 128], BF16)
        sttmp = SB("sttmp", [128, 256], F32)
        ps = [es.enter_context(nc.psum_tensor("ps%d" % i, [128, 512], F32)) for i in range(8)]

        def act(out, in_, func, reads, writes, **kw):
            S.add('act', lambda e: e.activation(out=out, in_=in_, func=func, **kw), reads, writes)

        def tt(out, a, b, op, reads, writes, eng='dve'):
            S.add(eng, lambda e: e.tensor_tensor(out=out, in0=a, in1=b, op=op), reads, writes)

        def ts(out, a, s1, s2, op0, op1, reads, writes, eng='dve'):
            if s2 is None:
                S.add(eng, lambda e: e.tensor_scalar(out=out, in0=a, scalar1=s1, scalar2=None, op0=op0), reads, writes)
            else:
                S.add(eng, lambda e: e.tensor_scalar(out=out, in0=a, scalar1=s1, scalar2=s2, op0=op0, op1=op1), reads, writes)

        def stt(out, a, sc, b, op0, op1, reads, writes):
            S.add('dve', lambda e: e.scalar_tensor_tensor(out=out, in0=a, scalar=sc, in1=b, op0=op0, op1=op1), reads, writes)

        def cp(out, in_, reads, writes, eng='dve'):
            S.add(eng, lambda e: e.tensor_copy(out=out, in_=in_), reads, writes)

        def mm(out, lhsT, rhs, start, stop, reads, writes):
            S.add('pe', lambda e: e.matmul(out, lhsT, rhs, start=start, stop=stop), reads, writes)

        def tp(out, in_, idt, reads, writes):
            S.add('pe', lambda e: e.transpose(out=out, in_=in_, identity=idt), reads, writes)

        dma_ctr = [0]

        def dma(out, in_, reads, writes, key=None, eng='sp'):
            if key is None:
                dma_ctr[0] += 1
                key = ('m', dma_ctr[0] % 16)
            S.add(eng, lambda e: e.dma_start(out=out, in_=in_), reads, writes, dma_key=key)

        def recip(out, in_, reads, writes):
            S.add('dve', lambda e: e.reciprocal(out=out, in_=in_), reads, writes)

        def memset(ap, v, writes, eng='pool'):
            S.add(eng, lambda e: e.memset(ap, v), (), writes)

        class PSA:
            free = list(range(8))

            @classmethod
            def get(cls):
                assert cls.free, "PSUM exhausted"
                return cls.free.pop(0)

            @classmethod
            def put(cls, b):
                cls.free.append(b)

        def P(b):
            return ('ps', b)

        def wdesc(tag):
            k = tag[0]
            if k == 'win':
                _, l, j = tag
                return w_in[l, :, 256 * j:256 * j + 256].rearrange("(k p) n -> p k n", p=128), 8, 256
            if k == 'gate':
                _, l, mp, n = tag
                return w_gate[l, n, :, 256 * mp:256 * mp + 256].rearrange("(k p) n -> p k n", p=128), 8, 256
            if k == 'br':
                _, l, mp, n = tag
                return w_branch[l, n, :, 256 * mp:256 * mp + 256].rearrange("(c p) m -> p c m", p=128), 2, 256
            if k == 'wout':
                _, l, j = tag
                return w_out[l, :, 256 * j:256 * j + 256].rearrange("(k p) n -> p k n", p=128), 8, 256
            if k == 'fg':
                _, l, j = tag
                return w_ffn_gate[l, :, 256 * j:256 * j + 256].rearrange("(k p) n -> p k n", p=128), 8, 256
            if k == 'fu':
                _, l, j = tag
                return w_ffn_up[l, :, 256 * j:256 * j + 256].rearrange("(k p) n -> p k n", p=128), 8, 256
            if k == 'dn':
                _, l, half, m = tag
                r0, r1 = (0, 1536) if half == 0 else (1536, DFF)
                return (w_ffn_down[l, r0:r1, 128 * m:128 * m + 128].rearrange("(k p) n -> p k n", p=128),
                        (r1 - r0) // 128, 128)
            if k == 'cvd':
                return None, 16, 128
            if k == 'kvK':
                return w_mem_kv[tag[1], :, 0:256].rearrange("(k p) n -> p k n", p=128), 8, 256
            if k == 'kvV':
                return w_mem_kv[tag[1], :, 256:512].rearrange("(k p) n -> p k n", p=128), 8, 256
            raise ValueError(tag)

        NN_ORDER = (2, 3, 4, 0, 1)
        FH = [(0, 6), (6, 11)]

        def layer_seq(l, nst, first, last):
            seq = []
            if first:
                for i in range(4):
                    seq.append(('cvd', l, i))
            for _ in range(nst):
                for j in (3, 2, 5, 6, 4, 7, 8, 0, 1):
                    seq.append(('win', l, j))
            for mp in range(4):
                for n in NN_ORDER:
                    seq.append(('gate', l, mp, n))
                    seq.append(('br', l, mp, n))
            for j in range(4):
                seq.append(('wout', l, j))
            for half in range(2):
                for j in range(*FH[half]):
                    seq.append(('fg', l, j))
                    seq.append(('fu', l, j))
                if half == 1 and not last:
                    for i in range(4):
                        seq.append(('cvd', l + 1, i))
                for _ in range(nst if half == 1 else 1):
                    for m in range(8):
                        seq.append(('dn', l, half, m))
            return seq

        MAXFLY = 3
        CVD_ENG = os.environ.get('KCVD', 'act')

        class WR:
            evs = []
            seq = []
            loaded = 0
            nxt = 0
            free = list(range(NSLOT))
            slot_of = {}

            @classmethod
            def pump(cls):
                while cls.loaded < len(cls.seq) and cls.free:
                    j = cls.loaded
                    tag = cls.seq[j]
                    src, a, b = wdesc(tag)
                    slot = cls.free.pop(0)
                    cls.slot_of[j] = slot
                    dst = wr[slot][:, 0:a * b].rearrange("p (a b) -> p a b", b=b)
                    allk = [('wr', slot)] + [('wrx', slot, kk) for kk in range(16)]
                    if tag[0] == 'cvd':
                        _, l_, i_ = tag
                        for kk in range(16):
                            e_ = i_ * 16 + kk
                            if e_ >= 62:
                                break
                            c_, k_ = e_ // 31, e_ % 31
                            wk = allk if kk == 0 else [('wrx', slot, kk)]
                            if CVD_ENG == 'act':
                                act(dst[:, kk, :], identb[:], AF.Copy, ['identb', 'PV'], wk, scale=bcwcol(l_, k_, c_))
                            else:
                                ts(dst[:, kk, :], identb[:], bcwcol(l_, k_, c_), None, ALU.mult, None, ['identb', 'PV'], wk,
                                   eng=CVD_ENG)
                        cls.evs.append(None)
                    else:
                        prev = None
                        cnt = 0
                        for jj in range(j - 1, -1, -1):
                            if cls.seq[jj][0] != 'cvd':
                                cnt += 1
                                if cnt == MAXFLY:
                                    prev = cls.evs[jj]
                                    break
                        extra = [prev] if prev is not None else []
                        ev = S.add('pool', lambda e, dst=dst, src=src: e.dma_start(out=dst, in_=src), (), allk,
                                   dma_key=('wr', slot), extra=extra)
                        cls.evs.append(ev)
                    cls.loaded += 1

            @classmethod
            def acquire(cls, tag):
                j = cls.nxt
                assert cls.seq[j] == tag, (cls.seq[j], tag)
                cls.nxt += 1
                cls.pump()
                assert j < cls.loaded, "weight ring: no free slot for %s" % (tag,)
                _, a, b = wdesc(tag)
                slot = cls.slot_of[j]
                return j, ('wr', slot), wr[slot][:, 0:a * b].rearrange("p (a b) -> p a b", b=b)

            @classmethod
            def release(cls, j):
                cls.free.append(cls.slot_of[j])
                cls.pump()

        memset(ident[:], 0.0, ['ident'])
        S.add('pool', lambda e: e.affine_select(out=ident[:], in_=ident[:], pattern=[[-1, 128]],
                                                compare_op=ALU.not_equal, fill=1.0, base=0, channel_multiplier=1),
              ['ident'], ['ident'])
        cp(identb[:], ident[:], ['ident'], ['identb'])
        memset(ones_b[:], 1.0, ['ones_b'])
        memset(onesw_b[:], 1.0 / 256.0, ['onesw_b'])
        memset(epst[:], EPS, ['epst'])
        memset(dwbd[:], 0.0, ['dwbd'])
        memset(bhist[:], 0.0, ['bhist'])
        memset(chist[:], 0.0, ['chist'])
        memset(dhist[:], 0.0, ['dhist'])
        dma(rct[:], rctab, [], ['rct'])

        pvcol = {}
        groups = [
            [('n1g', norm1_g.rearrange("l (k p) -> (l k) p", p=128), 32),
             ('n2g', norm2_g.rearrange("l (k p) -> (l k) p", p=128), 32),
             ('mng', mem_norm_g.rearrange("l (k p) -> (l k) p", p=128), 32),
             ('fng', final_norm_g.rearrange("(k p) -> k p", p=128), 8)],
            [('bg0', b_gate.rearrange("l n (m p) -> (l n m) p", p=128)[0:128, :], 128)],
            [('bg1', b_gate.rearrange("l n (m p) -> (l n m) p", p=128)[128:160, :], 32)],
            [('bcw0', b_conv_w[0:2].rearrange("l k (c p) -> (l k c) p", p=128), 124)],
            [('bcw1', b_conv_w[2:4].rearrange("l k (c p) -> (l k c) p", p=128), 124)],
            [('ccw', c_conv_w.rearrange("l k (c p) -> (l k c) p", p=128), 24),
             ('bcb', b_conv_b.rearrange("l (c p) -> (l c) p", p=128), 8),
             ('blg', b_ln_g.rearrange("l (c p) -> (l c) p", p=128), 8),
             ('blb', b_ln_b.rearrange("l (c p) -> (l c) p", p=128), 8),
             ('dsc', d_scale.rearrange("l (c p) -> (l c) p", p=128), 8)],
        ]
        colbase = 0
        stg = acc[:].rearrange("p a b -> p (a b)")
        for gi, grp in enumerate(groups):
            r0 = 0
            for (nm, src, nr) in grp:
                dma(stg[r0:r0 + nr, 0:128], src, [], [('acc', 0), ('acc', 1)])
                pvcol[nm] = colbase + r0
                r0 += nr
            b = PSA.get()
            tp(ps[b][:, 0:r0], stg[0:r0, 0:128], ident[0:r0, 0:r0], [('acc', 0), ('acc', 1), 'ident'], [P(b)])
            cp(PV[:, colbase:colbase + r0], ps[b][:, 0:r0], [P(b)], ['PV'])
            PSA.put(b)
            colbase += r0
        assert colbase <= NPV

        def pv(nm, idx):
            c = pvcol[nm] + idx
            return PV[:, c:c + 1]

        def bgcol(l, n, m):
            r = (l * 5 + n) * 8 + m
            return pv('bg0', r) if r < 128 else pv('bg1', r - 128)

        def bcwcol(l, k, c):
            if l < 2:
                return pv('bcw0', (l * 31 + k) * 2 + c)
            return pv('bcw1', ((l - 2) * 31 + k) * 2 + c)

        def rms_a(off, n, si):
            act(hb[:, 0:8, off:off + n], x[:, :, off:off + n], AF.Square,
                [('x', kc, si) for kc in range(8)], [('h', kc, si) for kc in range(8)])

        def rms_b(off, n, si, gname, gidx0, dst, dkey, stats=True):
            if stats:
                b = PSA.get()
                for kc in range(8):
                    mm(ps[b][:, 0:n], ones_b[:], hb[:, kc, off:off + n], kc == 0, kc == 7,
                       [('h', kc, si), 'ones_b'], [P(b)])
                act(rs[:, 0:n], ps[b][:, 0:n], AF.Sqrt, [P(b), 'epst'], ['rs'], scale=1.0 / D, bias=epst[:])
                PSA.put(b)
                recip(rs[:, 0:n], rs[:, 0:n], ['rs'], ['rs'])
            if dst is not None:
                for kc in range(8):
                    stt(dst[:, kc, off:off + n], x[:, kc, off:off + n], pv(gname, gidx0 + kc), rs[:, 0:n],
                        ALU.mult, ALU.mult, [('x', kc, si), 'rs', 'PV'], [(dkey, kc, si)])

        def rmsnorm(off, n, si, gname, gidx0, dst, dkey, fm=False):
            rms_a(off, n, si)
            rms_b(off, n, si, gname, gidx0, dst, dkey)

        def load_tokens(src_rows_fn, nblk):
            stgs = [acc[:].rearrange("p a b -> p (a b)"), cacc[:].rearrange("p a b -> p (a b)")]
            keys = [[('acc', 0), ('acc', 1)], [('cacc', 0), ('cacc', 1)]]
            for tb in range(nblk):
                sg_, kk = stgs[tb % 2], keys[tb % 2]
                dma(sg_[:, :], src_rows_fn(tb), [], kk)
                for half in range(2):
                    b = PSA.get()
                    for j in range(4):
                        kc = half * 4 + j
                        tp(ps[b][:, j * 128:(j + 1) * 128], sg_[:, kc * 128:(kc + 1) * 128], ident[:],
                           kk + ['ident'], [P(b)])
                    si = tb // 4
                    S.add('act' if half == 0 else 'dve',
                          (lambda e, b=b, half=half, tb=tb:
                           (e.activation(out=x[:, half * 4:half * 4 + 4, tb * 128:(tb + 1) * 128],
                                         in_=ps[b][:].rearrange("p (j t) -> p j t", t=128), func=AF.Copy)
                            if half == 0 else
                            e.tensor_copy(out=x[:, half * 4:half * 4 + 4, tb * 128:(tb + 1) * 128],
                                          in_=ps[b][:].rearrange("p (j t) -> p j t", t=128)))),
                          [P(b)], [('x', half * 4 + j, si) for j in range(4)])
                    PSA.put(b)

        def store_tokens(dst_rows_fn, nblk, nst_list):
            ystgs = [cacc[:].rearrange("p a b -> p (a b)"), cext[:].rearrange("p a b -> p (a b)")[:, 0:1024]]
            ysk = [[('cacc', 0), ('cacc', 1)], ['cext']]
            for si, (off, n) in enumerate(nst_list):
                rms_a(off, n, si)
                rms_b(off, n, si, 'fng', 0, x, 'x')
                for tb in range(off // 128, (off + n) // 128):
                    t0 = tb * 128
                    ystg, sk = ystgs[tb % 2], ysk[tb % 2]
                    for half in range(2):
                        b = PSA.get()
                        for j in range(4):
                            kc = half * 4 + j
                            tp(ps[b][:, j * 128:(j + 1) * 128], x[:, kc, t0:t0 + 128], ident[:],
                               [('x', kc, si), 'ident'], [P(b)])
                        if half == 0:
                            act(ystg[:, 0:512], ps[b][:], AF.Copy, [P(b)], sk)
                        else:
                            cp(ystg[:, 512:1024], ps[b][:], [P(b)], sk)
                        PSA.put(b)
                    dma(dst_rows_fn(tb), ystg[:, :], sk, [])

        def hkeys(slots, si):
            return [('h', s, si) for s in slots]

        def prologue(seq):
            load_tokens(lambda tb: memp[seq, tb * 128:(tb + 1) * 128, :], 2)
            rms_a(0, 256, 0)
            for l in range(L):
                rms_b(0, 256, 0, 'mng', l * 8, xn, 'xn', stats=(l == 0))
                jK, kK, wK = WR.acquire(('kvK', l))
                jV, kV, wV = WR.acquire(('kvV', l))
                xr = [('xn', kc, 0) for kc in range(8)]
                for c in range(2):
                    b = PSA.get()
                    for kc in range(8):
                        mm(ps[b][:, 0:256], wK[:, kc, c * 128:(c + 1) * 128], xn[:, kc, 0:256], kc == 0, kc == 7,
                           xr + [kK], [P(b)])
                    act(KT[:, l, c, :], ps[b][:, 0:256], AF.Copy, [P(b)], [('KT', l)])
                    PSA.put(b)
                for mc in range(2):
                    b = PSA.get()
                    for kc in range(8):
                        mm(ps[b][:, 0:256], xn[:, kc, mc * 128:(mc + 1) * 128], wK[:, kc, :], kc == 0, kc == 7,
                           xr + [kK], [P(b)])
                    b2 = PSA.get()
                    for kc in range(8):
                        mm(ps[b2][:, 0:256], xn[:, kc, mc * 128:(mc + 1) * 128], wV[:, kc, :], kc == 0, kc == 7,
                           xr + [kV], [P(b2)])
                    act(gv[0][:], ps[b][:, 0:256], AF.Copy, [P(b)], ['gv0'])
                    PSA.put(b)
                    dma(mk[l, seq, mc * 128:(mc + 1) * 128, :], gv[0][:], ['gv0'], [])
                    act(gv[1][:], ps[b2][:, 0:256], AF.Copy, [P(b2)], ['gv1'])
                    cp(VV[:, l, mc, :], ps[b2][:, 0:256], [P(b2)], [('VV', l)])
                    PSA.put(b2)
                    dma(mv[l, seq, mc * 128:(mc + 1) * 128, :], gv[1][:], ['gv1'], [])
                WR.release(jK)
                WR.release(jV)

        def layer_setup(l, samp=False):
            dma(lnab[:, 0, :], a_ln_g[l:l + 1, :].partition_broadcast(128), [], ['lnab'])
            dma(lnab[:, 1, :], a_ln_b[l:l + 1, :].partition_broadcast(128), [], ['lnab'])
            wst = cacc[:].rearrange("p a b -> p (a b)")[:, 0:512].rearrange("p (g j) -> p g j", j=128)
            ck_ = [('cacc', 0), ('cacc', 1)]
            if not samp:
                dma(wst, a_ws[l].rearrange("g i j -> i g j"), [], ck_)
                S.add('dve', lambda e: e.memset(wst[0:64, :, 64:128], 0.0), ck_, ck_)
                dma(den[0:1, :], a_bs[l:l + 1].rearrange("o g i -> o (g i)"), [], ['den'])
            else:
                S.add('dve', lambda e: e.memset(wst, 0.0), ck_, ck_)
                for q in range(4):
                    dma(wst[32 * q:32 * q + 32, :, 32 * q:32 * q + 32],
                        a_ws[l, :, 0:32, 0:32].rearrange("g i j -> i g j"), [], ck_)
                    dma(den[0:1, :].rearrange("o (g x) -> o g x", x=128)[:, :, 32 * q:32 * q + 32],
                        a_bs[l:l + 1, :, 0:32], [], ['den'])
            b = PSA.get()
            for g in range(4):
                tp(ps[b][:, g * 128:(g + 1) * 128], wst[:, g, :], ident[:], ck_ + ['ident'], [P(b)])
            cp(wmT[:].rearrange("p g i -> p (g i)"), ps[b][:], [P(b)], ['wmT'])
            PSA.put(b)
            cp(bsrow[:], den[0:1, :], ['den'], ['bsrow'])
            for c in range(2):
                for gi in range(2):
                    dma(dwbd[64 * gi:64 * gi + 64, c, 64 * gi:64 * gi + 64], d_w[l, 2 * c + gi], [], ['dwbd'],
                        key=('dw', 0), eng='pool')

        def sample_state_load(l):
            stB = acc[:].rearrange("p a b -> p (a b)")
            stC = cacc[:].rearrange("p a b -> p (a b)")
            kB = [('acc', 0), ('acc', 1)]
            kC = [('cacc', 0), ('cacc', 1)]
            S.add('dve', lambda e: e.memset(stB[:, 0:256], 0.0), kB, kB)
            S.add('dve', lambda e: e.memset(stC[:, 0:256], 0.0), kC, kC)
            for q in range(4):
                dma(stB[32 * q:32 * q + 30, 0:256], scb[l, q], [], kB)
                dma(stC[32 * q:32 * q + 2, 0:256], scc[l, q], [], kC)
                dma(stC[32 * q + 2:32 * q + 17, 0:256], spd[l, q], [], kC)
            for c in range(2):
                b = PSA.get()
                tp(ps[b][:, 0:128], stB[:, c * 128:(c + 1) * 128], ident[:], kB + ['ident'], [P(b)])
                tp(ps[b][:, 128:256], stC[:, c * 128:(c + 1) * 128], ident[:], kC + ['ident'], [P(b)])
                pv_b = ps[b][:, 0:128].rearrange("p (q r) -> p q r", r=32)
                pv_c = ps[b][:, 128:256].rearrange("p (q r) -> p q r", r=32)
                cp(bext[:, c, 0:248].rearrange("p (q t) -> p q t", t=62)[:, :, 0:30], pv_b[:, :, 0:30], [P(b)], ['bext'])
                cp(cext[:, c, 0:136].rearrange("p (q t) -> p q t", t=34)[:, :, 0:2], pv_c[:, :, 0:2], [P(b)], ['cext'])
                cp(dext[:, c, 0:188].rearrange("p (q t) -> p q t", t=47)[:, :, 0:15], pv_c[:, :, 2:17], [P(b)], ['dext'])
                PSA.put(b)
            kst = sgb[:].bitcast(BF16).rearrange("p a (m c) -> p (a m) c", c=256)
            ks_ = [('sgb', 0), ('sgb', 1)]
            dma(kst, ck[l].rearrange("q (mc p) c -> p (q mc) c", p=128), [], ks_, key=('kv', 0), eng='pool')
            dma(VV[:].rearrange("p a b c -> p (a b) c"), cv[l].rearrange("q (mc p) c -> p (q mc) c", p=128), [],
                [('VV', i) for i in range(L)], key=('kv', 1), eng='pool')
            for q in range(4):
                b = PSA.get()
                pbf = ps[b][:].bitcast(BF16)
                for c in range(2):
                    for mc in range(2):
                        o0 = (c * 2 + mc) * 128
                        tp(pbf[:, o0:o0 + 128], kst[:, q * 2 + mc, c * 128:(c + 1) * 128], identb[:],
                           ks_ + ['identb'], [P(b)])
                cp(KT[:, q, :, :].rearrange("p c m -> p (c m)"), pbf[:, 0:512], [P(b)], [('KT', i) for i in range(L)])
                PSA.put(b)

        def sample_state_emit(l):
            stg_ = cacc[:].rearrange("p a b -> p (a b)")
            kC = [('cacc', 0), ('cacc', 1)]
            for (buf, key, H, wdt, dst) in ((bext, 'bext', 30, 30, cbs), (cext, 'cext', 2, 2, ccs),
                                            (dext, 'dext', 15, 15, pds)):
                tl = H + 32
                bs_ = [PSA.get(), PSA.get()]
                for q in range(4):
                    for c in range(2):
                        j = q * 2 + c
                        tp(ps[bs_[j // 4]][0:wdt, (j % 4) * 128:(j % 4 + 1) * 128],
                           buf[:, c, q * tl + tl - wdt:q * tl + tl], ident[:], [key, 'ident'], [P(bs_[j // 4])])
                for hb_ in range(2):
                    cp(stg_[0:wdt, hb_ * 512:(hb_ + 1) * 512], ps[bs_[hb_]][0:wdt, :], [P(bs_[hb_])], kC)
                    PSA.put(bs_[hb_])
                dma(dst[l].rearrange("q r c -> r q c"), stg_[0:wdt, :].rearrange("r (q c) -> r q c", c=256), kC, [])

        def phase_ab(l, si, off, n, first_of_seq, last_of_tile, cvs, part, samp=False, defer_ln=False):
            xr = [('xn', kc, si) for kc in range(8)]
            nb = n // 128
            bextb = bextbs[si]
            pooled = pooleds[si]
            vabuf = vabufs[si]
            isH = (part == 'H')
            isT = (part == 'T')

            def v(ap):
                return ap.rearrange("p (q t) -> p q t", t=32) if samp else ap

            def ext_new(buf, mi, H):
                if samp:
                    return buf[:, mi, 0:4 * (H + 32)].rearrange("p (q t) -> p q t", t=H + 32)[:, :, H:H + 32]
                return buf[:, mi, H:H + n]

            def ext_out(ap2, H):
                if samp:
                    return ap2[:, 0:4 * (H + 32)].rearrange("p (q t) -> p q t", t=H + 32)[:, :, 0:32]
                return ap2[:, 0:n]

            NCB = 4 * 62 - 30 if samp else n
            NCC = 4 * 34 - 2 if samp else n

            def zmm(b, wv, wk, mi):
                for kc in range(8):
                    mm(ps[b][:, 0:n], wv[:, kc, mi * 128:(mi + 1) * 128], xn[:, kc, off:off + n], kc == 0, kc == 7,
                       xr + [wk], [P(b)])

            if isH:
                if samp:
                    sample_state_load(l)
                elif si == 0:
                    cp(bext[:, :, 0:30], bhist[:, l, :, :], ['bhist'], ['bext'], eng='pool')
                    cp(cext[:, :, 0:2], chist[:, l, :, :], ['chist'], ['cext'], eng='pool')
                    cp(dext[:, :, 0:15], dhist[:, l, :, :], ['dhist'], ['dext'], eng='pool')
                else:
                    cp(bext[:, :, 0:30], bext[:, :, 512:542], ['bext'], ['bext'], eng='pool')
                    cp(cext[:, :, 0:2], cext[:, :, 512:514], ['cext'], ['cext'], eng='pool')
                    cp(dext[:, :, 0:15], dext[:, :, 512:527], ['dext'], ['dext'], eng='pool')

                j3, k3, w3 = WR.acquire(('win', l, 3))
                for mi in range(2):
                    b = PSA.get()
                    zmm(b, w3, k3, mi)
                    act(sgb[:, mi, 0:n], ps[b][:, 0:n], AF.Sigmoid, [P(b)], [('sgb', mi)])
                    PSA.put(b)
                WR.release(j3)
                j2, k2, w2 = WR.acquire(('win', l, 2))
                for mi in range(2):
                    b = PSA.get()
                    zmm(b, w2, k2, mi)
                    tt(ext_new(bext, mi, 30), v(ps[b][:, 0:n]), v(sgb[:, mi, 0:n]), ALU.mult, [P(b), ('sgb', mi)], ['bext'])
                    PSA.put(b)
                WR.release(j2)
                if last_of_tile and not samp:
                    cp(bhist[:, l, :, :], bext[:, :, n:n + 30], ['bext'], ['bhist'], eng='pool')
            if isH:
                XB = 30 + NCB
                act(bextb[:, :, 0:XB], bext[:, :, 0:XB], AF.Copy, ['bext'], [('bextb', si)])
            if isT:
                for c in range(2):
                    b = PSA.get()
                    for k in range(31):
                        e_ = c * 31 + k
                        jc, kc_, wc = cvs[e_ // 16]
                        kk = e_ % 16
                        mm(ps[b][:, 0:NCB], wc[:, kk, :], bextb[:, c, k:k + NCB], k == 0, k == 30,
                           [kc_, ('wrx', kc_[1], kk), ('bextb', si)], [P(b)])
                    act(acc[:, c, 0:NCB], ps[b][:, 0:NCB], AF.Identity, [P(b), 'PV'], [('acc', c)],
                        bias=pv('bcb', l * 2 + c))
                    PSA.put(b)
            if isH:
                j5, k5, w5 = WR.acquire(('win', l, 5))
                for mi in range(2):
                    b = PSA.get()
                    zmm(b, w5, k5, mi)
                    act(ext_new(cext, mi, 2), v(ps[b][:, 0:n]), AF.Copy, [P(b)], ['cext'])
                    PSA.put(b)
                WR.release(j5)
                j6, k6, w6 = WR.acquire(('win', l, 6))
                for mi in range(2):
                    b = PSA.get()
                    zmm(b, w6, k6, mi)
                    tt(ext_new(cext, mi, 2), v(ps[b][:, 0:n]), ext_new(cext, mi, 2), ALU.mult, [P(b), 'cext'], ['cext'])
                    PSA.put(b)
                WR.release(j6)
                if last_of_tile and not samp:
                    cp(chist[:, l, :, :], cext[:, :, n:n + 2], ['cext'], ['chist'], eng='pool')
                j4, k4, w4 = WR.acquire(('win', l, 4))
                for mi in range(2):
                    b = PSA.get()
                    zmm(b, w4, k4, mi)
                    act(hb[:, 12 + mi, off:off + n], ps[b][:, 0:n], AF.Copy, [P(b)], [('h', 12 + mi, si)])
                    PSA.put(b)
                WR.release(j4)
                for c in range(2):
                    ts(cacc[:, c, 0:NCC], cext[:, c, 0:NCC], pv('ccw', (l * 3 + 0) * 2 + c), None, ALU.mult, None,
                       ['cext', 'PV'], [('cacc', c)])
                    for k in (1, 2):
                        stt(cacc[:, c, 0:NCC], cext[:, c, k:k + NCC], pv('ccw', (l * 3 + k) * 2 + c), cacc[:, c, 0:NCC],
                            ALU.mult, ALU.add, ['cext', 'PV', ('cacc', c)], [('cacc', c)])
                    tt(v(hb[:, 12 + c, off:off + n]), v(hb[:, 12 + c, off:off + n]), ext_out(cacc[:, c, :], 2), ALU.mult,
                       [('h', 12 + c, si), ('cacc', c)], [('h', 12 + c, si)])

            if isH:
                j7, k7, w7 = WR.acquire(('win', l, 7))
                for mi in range(2):
                    b = PSA.get()
                    zmm(b, w7, k7, mi)
                    act(ext_new(dext, mi, 15), v(ps[b][:, 0:n]), AF.Copy, [P(b)], ['dext'])
                    PSA.put(b)
                WR.release(j7)
                if last_of_tile and not samp:
                    cp(dhist[:, l, :, :], dext[:, :, n:n + 15], ['dext'], ['dhist'], eng='pool')
                E = 4 * 47 if samp else 15 + n

                def dnew(ap2):
                    if samp:
                        return ap2[:, 0:188].rearrange("p (q t) -> p q t", t=47)[:, :, 15:47]
                    return ap2[:, 15:15 + n]

                for c in range(2):
                    dd = dext[:, c, :]
                    tt(trA[:, 1:E], dd[:, 1:E], dd[:, 0:E - 1], ALU.add, ['dext'], ['trA'])
                    tt(trB[:, 3:E], trA[:, 3:E], trA[:, 1:E - 2], ALU.add, ['trA'], ['trB'])
                    if c == 0:
                        srcs = [(trA, 2.0), (trB, 4.0)]
                    else:
                        tt(trA[:, 7:E], trB[:, 7:E], trB[:, 3:E - 4], ALU.add, ['trB'], ['trA'])
                        tt(trB[:, 15:E], trA[:, 15:E], trA[:, 7:E - 8], ALU.add, ['trA'], ['trB'])
                        srcs = [(trA, 8.0), (trB, 16.0)]
                    for gi, (sbuf_, wlen) in enumerate(srcs):
                        pr = slice(64 * gi, 64 * gi + 64)
                        kk = 'trA' if sbuf_ is trA else 'trB'
                        stt(v(pooled[pr, c, 0:n]), dnew(sbuf_[pr, :]), 1.0 / wlen, dnew(dext[pr, c, :]),
                            ALU.mult, ALU.subtract, [kk, 'dext'], [('pooled', si, c)])
                        if first_of_seq and si == 0 and not samp:
                            tt(meansb[pr, 0:15], sbuf_[pr, 15:30], rct[pr, c, :], ALU.mult, [kk, 'rct'], ['meansb'])
                            tt(pooled[pr, c, 0:15], meansb[pr, 0:15], dext[pr, c, 15:30], ALU.subtract,
                               ['meansb', 'dext'], [('pooled', si, c)])
            if isT:
                for c in range(2):
                    b = PSA.get()
                    mm(ps[b][:, 0:n], dwbd[:, c, :], pooled[:, c, 0:n], True, True, ['dwbd', ('pooled', si, c)], [P(b)])
                    ts(hb[:, 14 + c, off:off + n], ps[b][:, 0:n], pv('dsc', l * 2 + c), None, ALU.mult, None,
                       [P(b), 'PV'], [('h', 14 + c, si)])
                    PSA.put(b)

            if isH:
                j8, k8, w8 = WR.acquire(('win', l, 8))
                for mi in range(2):
                    b = PSA.get()
                    zmm(b, w8, k8, mi)
                    act(hb[:, 16 + mi, off:off + n], ps[b][:, 0:n], AF.Copy, [P(b)], [('h', 16 + mi, si)], scale=0.125)
                    PSA.put(b)
                WR.release(j8)
            if isT:
                pti = 0
                ktk = [('KT', i) for i in range(L)] if samp else [('KT', l)]
                vvk = [('VV', i) for i in range(L)] if samp else [('VV', l)]
                for c in range(2):
                    bo = PSA.get()
                    bd = PSA.get()
                    if not samp:
                        sbk = []
                        for hi in range(2):
                            pr = slice(64 * hi, 64 * hi + 64)
                            for mc in range(2):
                                b = PSA.get()
                                mm(ps[b][:, 0:n], KT[pr, l, c, mc * 128:(mc + 1) * 128], hb[pr, 16 + c, off:off + n],
                                   True, True, ktk + [('h', 16 + c, si)], [P(b)])
                                sbk.append(b)
                        for i4 in range(4):
                            b = sbk[i4]
                            act(PT[i4][:, 0:n], ps[b][:, 0:n], AF.Exp, [P(b)], [('PT', i4)])
                            PSA.put(b)
                        for hi in range(2):
                            h = 2 * c + hi
                            pr = slice(64 * hi, 64 * hi + 64)
                            for mc in range(2):
                                i4 = hi * 2 + mc
                                mm(ps[bo][pr, 0:n], VV[:, l, mc, 64 * h:64 * h + 64], PT[i4][:, 0:n], mc == 0, mc == 1,
                                   vvk + [('PT', i4)], [P(bo)])
                        for hi in range(2):
                            pr = slice(64 * hi, 64 * hi + 64)
                            for mc in range(2):
                                i4 = hi * 2 + mc
                                mm(ps[bd][pr, 0:n], ones_b[:, 0:64], PT[i4][:, 0:n], mc == 0, mc == 1,
                                   ['ones_b', ('PT', i4)], [P(bd)])
                    else:
                        pts = []
                        for hi in range(2):
                            pr = slice(64 * hi, 64 * hi + 64)
                            b = PSA.get()
                            for mc in range(2):
                                for q in range(4):
                                    o0 = (mc * 4 + q) * 32
                                    mm(ps[b][:, o0:o0 + 32], KT[pr, q, c, mc * 128:(mc + 1) * 128],
                                       hb[pr, 16 + c, 32 * q:32 * q + 32], True, True, ktk + [('h', 16 + c, si)], [P(b)])
                            pt = PT[pti % 4]
                            pk = ('PT', pti % 4)
                            pti += 1
                            act(pt[:, 0:256], ps[b][:, 0:256], AF.Exp, [P(b)], [pk])
                            PSA.put(b)
                            pts.append((pt, pk))
                        for hi in range(2):
                            h = 2 * c + hi
                            pr = slice(64 * hi, 64 * hi + 64)
                            pt, pk = pts[hi]
                            for q in range(4):
                                for mc in range(2):
                                    o0 = (mc * 4 + q) * 32
                                    mm(ps[bo][pr, 32 * q:32 * q + 32], VV[:, q, mc, 64 * h:64 * h + 64], pt[:, o0:o0 + 32],
                                       mc == 0, mc == 1, vvk + [pk], [P(bo)])
                            for q in range(4):
                                for mc in range(2):
                                    o0 = (mc * 4 + q) * 32
                                    mm(ps[bd][pr, 32 * q:32 * q + 32], ones_b[:, 0:64], pt[:, o0:o0 + 32],
                                       mc == 0, mc == 1, ['ones_b', pk], [P(bd)])
                    act(den[:, 0:n], ps[bd][:, 0:n], AF.Copy, [P(bd)], ['den'])
                    PSA.put(bd)
                    recip(den[:, 0:n], den[:, 0:n], ['den'], ['den'])
                    tt(hb[:, 16 + c, off:off + n], ps[bo][:, 0:n], den[:, 0:n], ALU.mult, [P(bo), 'den'],
                       [('h', 16 + c, si)])
                    PSA.put(bo)

            if isH:
                j0, k0, w0 = WR.acquire(('win', l, 0))
                for mi in range(2):
                    b = PSA.get()
                    zmm(b, w0, k0, mi)
                    act(hb[:, 8 + mi, off:off + n], ps[b][:, 0:n], AF.Gelu_apprx_tanh, [P(b)], [('h', 8 + mi, si)])
                    PSA.put(b)
                WR.release(j0)
                j1, k1, w1 = WR.acquire(('win', l, 1))
                for blk in range(nb):
                    t0 = off + blk * 128
                    b = PSA.get()
                    for kc in range(8):
                        mm(ps[b][:, 0:256], xn[:, kc, t0:t0 + 128], w1[:, kc, :], kc == 0, kc == 7, xr + [k1], [P(b)])
                    u = blk % 4
                    act(gv[u][:], ps[b][:, 0:256], AF.Gelu_apprx_tanh, [P(b)], ['gv%d' % u])
                    PSA.put(b)
                    S.add('dve', lambda e, u=u: e.bn_stats(out=bnst[u][:], in_=gv[u][:]), ['gv%d' % u], ['bnst%d' % u])
                    S.add('dve', lambda e, u=u: e.bn_aggr(out=bnmv_all[:, u, :], in_=bnst[u][:]), ['bnst%d' % u],
                          [('bnmv', u)])
                bk = [('bnmv', u) for u in range(nb)]
                varv = bnmv_all[:, 0:nb, 1:2]
                ts(varv, varv, 0.0, EPS, ALU.max, ALU.add, bk, bk)
                act(varv, varv, AF.Sqrt, bk, bk)
                recip(varv, varv, bk, bk)
                for blk in range(nb):
                    u = blk % 4
                    ts(gv[u][:], gv[u][:], bnmv_all[:, u, 0:1], bnmv_all[:, u, 1:2], ALU.subtract, ALU.mult,
                       ['gv%d' % u, ('bnmv', u)], ['gv%d' % u])
                    tt(gv[u][:], gv[u][:], lnab[:, 0, :], ALU.mult, ['gv%d' % u, 'lnab'], ['gv%d' % u])
                    if samp:
                        tt(gv[u][:], gv[u][:], lnab[:, 1, :], ALU.add, ['gv%d' % u, 'lnab'], ['gv%d' % u])
                        cp(vabuf[:, blk, :], gv[u][:], ['gv%d' % u], [('va', si, blk)])
                        dma(avs[l], gv[u][:], ['gv%d' % u], [])
                    else:
                        tt(vabuf[:, blk, :], gv[u][:], lnab[:, 1, :], ALU.add, ['gv%d' % u, 'lnab'], [('va', si, blk)])
                WR.release(j1)
            if isT:
                for c in range(2):
                    b = PSA.get()
                    for blk in range(nb):
                        for gi in range(2):
                            g = 2 * c + gi
                            pr = slice(64 * gi, 64 * gi + 64)
                            cs = slice(blk * 128, (blk + 1) * 128)
                            mm(ps[b][pr, cs], vabuf[:, blk, 128 * c + 64 * gi:128 * c + 64 * gi + 64], wmT[:, g, :],
                               True, False, [('va', si, blk), 'wmT'], [P(b)])
                            mm(ps[b][pr, cs], ones_b[0:1, 0:64], bsrow[0:1, g * 128:(g + 1) * 128], False, True,
                               ['ones_b', 'bsrow'], [P(b)])
                    tt(hb[:, 8 + c, off:off + n], ps[b][:, 0:n], hb[:, 8 + c, off:off + n], ALU.mult,
                       [P(b), ('h', 8 + c, si)], [('h', 8 + c, si)])
                    PSA.put(b)
            def lnb():
                for c in range(2):
                    act(v(accb[:, c, 0:n]), ext_out(acc[:, c, :], 30), AF.Copy, [('acc', c)], [('accb', c)])
                    act(v(sqb[:, c, 0:n]), ext_out(acc[:, c, :], 30), AF.Square, [('acc', c)], [('sqb', c)])
                bm = PSA.get()
                be = PSA.get()
                for c in range(2):
                    mm(ps[bm][:, 0:n], onesw_b[:], accb[:, c, 0:n], c == 0, c == 1, [('accb', c), 'onesw_b'], [P(bm)])
                for c in range(2):
                    mm(ps[be][:, 0:n], onesw_b[:], sqb[:, c, 0:n], c == 0, c == 1, [('sqb', c), 'onesw_b'], [P(be)])
                act(msq[:, 0:n], ps[bm][:, 0:n], AF.Square, [P(bm)], ['msq'])
                act(meansb[:, 0:n], ps[bm][:, 0:n], AF.Copy, [P(bm)], ['meansb'])
                PSA.put(bm)
                tt(varsb[:, 0:n], ps[be][:, 0:n], msq[:, 0:n], ALU.subtract, [P(be), 'msq'], ['varsb'])
                PSA.put(be)
                ts(varsb[:, 0:n], varsb[:, 0:n], 0.0, EPS, ALU.max, ALU.add, ['varsb'], ['varsb'])
                act(varsb[:, 0:n], varsb[:, 0:n], AF.Sqrt, ['varsb'], ['varsb'])
                recip(varsb[:, 0:n], varsb[:, 0:n], ['varsb'], ['varsb'])
                for c in range(2):
                    ao = ext_out(acc[:, c, :], 30)
                    tt(ao, ao, v(meansb[:, 0:n]), ALU.subtract, [('acc', c), 'meansb'], [('acc', c)])
                    tt(ao, ao, v(varsb[:, 0:n]), ALU.mult, [('acc', c), 'varsb'], [('acc', c)])
                    act(v(hb[:, 10 + c, off:off + n]), ao, AF.Silu, [('acc', c), 'PV'], [('h', 10 + c, si)],
                        scale=pv('blg', l * 2 + c), bias=pv('blb', l * 2 + c))

            ret = None
            if isT:
                if defer_ln:
                    ret = lnb
                else:
                    lnb()
            if samp and isT:
                sample_state_emit(l)
            return ret

        def phase_c(l, sts, hook=None):
            it = 0
            for mp in range(4):
                accs = {}
                for si in range(len(sts)):
                    for mi in range(2):
                        accs[(si, mi)] = PSA.get()
                pending = [None]
                for nn in NN_ORDER:
                    jg, kg, wg = WR.acquire(('gate', l, mp, nn))
                    jb, kb, wb = WR.acquire(('br', l, mp, nn))
                    for si, (off, n) in enumerate(sts):
                        xr = [('xn', kc, si) for kc in range(8)]
                        for mi in range(2):
                            m = mp * 2 + mi
                            ba = accs[(si, mi)]
                            bg_ = PSA.get()
                            for kc in range(8):
                                mm(ps[bg_][:, 0:n], wg[:, kc, mi * 128:(mi + 1) * 128], xn[:, kc, off:off + n],
                                   kc == 0, kc == 7, xr + [kg], [P(bg_)])
                            bp = PSA.get()
                            for c in range(2):
                                mm(ps[bp][:, 0:n], wb[:, c, mi * 128:(mi + 1) * 128], hb[:, 8 + 2 * nn + c, off:off + n],
                                   c == 0, c == 1, [kb, ('h', 8 + 2 * nn + c, si)], [P(bp)])
                            u = it % 3
                            it += 1
                            act(sgC[u][:, 0:n], ps[bg_][:, 0:n], AF.Sigmoid, [P(bg_), 'PV'], [('sgC', u)],
                                bias=bgcol(l, nn, m))
                            PSA.put(bg_)
                            tt(gpC[u][:, 0:n], ps[bp][:, 0:n], sgC[u][:, 0:n], ALU.mult, [P(bp), ('sgC', u)],
                               [('gpC', u)])
                            PSA.put(bp)
                            if pending[0] is not None:
                                pending[0]()

                            def fin(u=u, nn=nn, ba=ba, n=n, off=off, m=m, si=si):
                                mm(ps[ba][:, 0:n], identb[:], gpC[u][:, 0:n], nn == NN_ORDER[0], nn == NN_ORDER[-1],
                                   ['identb', ('gpC', u)], [P(ba)])
                                if nn == NN_ORDER[-1]:
                                    act(hb[:, m, off:off + n], ps[ba][:, 0:n], AF.Copy, [P(ba)], [('h', m, si)])
                                    PSA.put(ba)
                            pending[0] = fin
                    WR.release(jg)
                    WR.release(jb)
                    if hook is not None and mp == 0 and nn == NN_ORDER[0]:
                        hook()
                if pending[0] is not None:
                    pending[0]()

        def phase_d(l, sts, after_st0=None, mid_st1=None):
            slabs = [WR.acquire(('wout', l, j)) for j in range(4)]
            for si, (off, n) in enumerate(sts):
                for j in range(4):
                    jw, kw, ww = slabs[j]
                    for mi in range(2):
                        m = 2 * j + mi
                        b = PSA.get()
                        for kc in range(8):
                            mm(ps[b][:, 0:n], ww[:, kc, mi * 128:(mi + 1) * 128], hb[:, kc, off:off + n],
                               kc == 0, kc == 7, [kw, ('h', kc, si)], [P(b)])
                        tt(x[:, m, off:off + n], ps[b][:, 0:n], x[:, m, off:off + n], ALU.add,
                           [P(b), ('x', m, si)], [('x', m, si)])
                        PSA.put(b)
                    if si == 1 and j == 0 and mid_st1 is not None:
                        mid_st1()
                if si == 0 and after_st0 is not None:
                    after_st0()
            for (jw, kw, ww) in slabs:
                WR.release(jw)

        def phase_e(l, sts, after_st0=None, mid_st1=None, pre_down_b=None):
            it = 0
            for half in range(2):
                j0, j1 = FH[half]
                for j in range(j0, j1):
                    jg, kg, wg = WR.acquire(('fg', l, j))
                    ju, ku, wu = WR.acquire(('fu', l, j))
                    for si, (off, n) in enumerate(sts):
                        xr = [('xn', kc, si) for kc in range(8)]
                        for mi in range(2):
                            f = (j - j0) * 2 + mi
                            bg_ = PSA.get()
                            for kc in range(8):
                                mm(ps[bg_][:, 0:n], wg[:, kc, mi * 128:(mi + 1) * 128], xn[:, kc, off:off + n],
                                   kc == 0, kc == 7, xr + [kg], [P(bg_)])
                            bu = PSA.get()
                            for kc in range(8):
                                mm(ps[bu][:, 0:n], wu[:, kc, mi * 128:(mi + 1) * 128], xn[:, kc, off:off + n],
                                   kc == 0, kc == 7, xr + [ku], [P(bu)])
                            u = it % 3
                            it += 1
                            act(sgC[u][:, 0:n], ps[bg_][:, 0:n], AF.Silu, [P(bg_)], [('sgC', u)])
                            PSA.put(bg_)
                            tt(hb[:, f, off:off + n], ps[bu][:, 0:n], sgC[u][:, 0:n], ALU.mult,
                               [P(bu), ('sgC', u)], [('h', f, si)])
                            PSA.put(bu)
                    WR.release(jg)
                    WR.release(ju)
                nk = (j1 - j0) * 2

                def down(m, si, off, n, kd, wd):
                    b = PSA.get()
                    for f in range(nk):
                        mm(ps[b][:, 0:n], wd[:, f, :], hb[:, f, off:off + n], f == 0, f == nk - 1,
                           [kd, ('h', f, si)], [P(b)])
                    tt(x[:, m, off:off + n], ps[b][:, 0:n], x[:, m, off:off + n], ALU.add,
                       [P(b), ('x', m, si)], [('x', m, si)])
                    PSA.put(b)

                if half == 0:
                    for m in range(8):
                        jd, kd, wd = WR.acquire(('dn', l, half, m))
                        for si, (off, n) in enumerate(sts):
                            down(m, si, off, n, kd, wd)
                        WR.release(jd)
                else:
                    if pre_down_b is not None:
                        pre_down_b()
                    for si, (off, n) in enumerate(sts):
                        for m in range(8):
                            jd, kd, wd = WR.acquire(('dn', l, half, m))
                            down(m, si, off, n, kd, wd)
                            WR.release(jd)
                            if si == 1 and m == 1 and mid_st1 is not None:
                                mid_st1()
                        if si == 0 and after_st0 is not None:
                            after_st0()

        def emit_state(hist, l, width, dst):
            b = PSA.get()
            for c in range(2):
                tp(ps[b][0:width, c * 128:(c + 1) * 128], hist[:, l, c, :], ident[:], ['bhist', 'chist', 'dhist', 'ident'],
                   [P(b)])
            cp(sttmp[0:width, :], ps[b][0:width, 0:256], [P(b)], ['sttmp'])
            PSA.put(b)
            dma(dst, sttmp[0:width, :], ['sttmp'], [])

        PSTS = [(0, 512), (512, 512)]
        nxt_cvs = [None]
        full = []
        for seq in range(KSEQS):
            for l in range(L):
                full += [('kvK', l), ('kvV', l)]
            for t in range(KTILES):
                for l in range(DEPTH_RUN):
                    full += layer_seq(l, 2, l == 0, l == DEPTH_RUN - 1)
        if KSAMP:
            for l in range(DEPTH_RUN):
                full += layer_seq(l, 1, l == 0, l == DEPTH_RUN - 1)
        WR.seq = full

        if KSTOP < 99:
            full = []
            if KSTOP >= 1:
                for l in range(L):
                    full += [('kvK', l), ('kvV', l)]
            if KSTOP >= 4:
                full += [('win', 0, j) for j in (3, 2, 5, 6, 4, 7, 8, 0, 1)]
            WR.seq = full
        for seq in range(KSEQS):
            if KSTOP >= 1:
                prologue(seq)
            if KSTOP < 99:
                if KSTOP >= 2:
                    load_tokens(lambda tb, seq=seq: xp[seq, tb * 128:(tb + 1) * 128, :], 8)
                if KSTOP >= 3:
                    layer_setup(0)
                    rmsnorm(0, 512, 0, 'n1g', 0, xn, 'xn')
                if KSTOP >= 4:
                    phase_ab(0, 0, 0, 512, True, False, None, 'H')
                store_tokens(lambda tb, seq=seq: yp[seq, tb * 128:(tb + 1) * 128, :], 4, [(0, 512)])
                break
            for t in range(KTILES):
                tok0 = t * TP
                load_tokens(lambda tb, seq=seq, tok0=tok0: xp[seq, tok0 + tb * 128:tok0 + (tb + 1) * 128, :], 8)
                if t == 0:
                    memset(bhist[:], 0.0, ['bhist'])
                    memset(chist[:], 0.0, ['chist'])
                    memset(dhist[:], 0.0, ['dhist'])
                for l in range(DEPTH_RUN):
                    layer_setup(l)
                    for si, (off, n) in enumerate(PSTS):
                        if si == 0 and l > 0:
                            continue
                        rmsnorm(off, n, si, 'n1g', l * 8, xn, 'xn')
                    if l == 0:
                        cvs = [WR.acquire(('cvd', l, i)) for i in range(4)]
                    else:
                        cvs = nxt_cvs[0]
                    for si, (off, n) in enumerate(PSTS):
                        phase_ab(l, si, off, n, t == 0, si == len(PSTS) - 1, cvs, 'H')
                    lnb_hook = None
                    for si, (off, n) in enumerate(PSTS):
                        r_ = phase_ab(l, si, off, n, t == 0, si == len(PSTS) - 1, cvs, 'T',
                                      defer_ln=(si == len(PSTS) - 1))
                        if r_ is not None:
                            lnb_hook = r_
                    for (jc, kc_, wc) in cvs:
                        WR.release(jc)
                    phase_c(l, PSTS, hook=lnb_hook)
                    o0, n0 = PSTS[0]
                    phase_d(l, PSTS, after_st0=lambda: rms_a(o0, n0, 0),
                            mid_st1=lambda l=l: rms_b(o0, n0, 0, 'n2g', l * 8, xn, 'xn'))
                    rmsnorm(PSTS[1][0], PSTS[1][1], 1, 'n2g', l * 8, xn, 'xn')
                    if l < DEPTH_RUN - 1:
                        def acq_next(l=l):
                            nxt_cvs[0] = [WR.acquire(('cvd', l + 1, i)) for i in range(4)]
                        phase_e(l, PSTS, after_st0=lambda: rms_a(o0, n0, 0),
                                mid_st1=lambda l=l: rms_b(o0, n0, 0, 'n1g', (l + 1) * 8, xn, 'xn'),
                                pre_down_b=acq_next)
                    else:
                        phase_e(l, PSTS)
                    if t == 1:
                        emit_state(bhist, l, 30, cbp[l, seq])
                        emit_state(chist, l, 2, ccp[l, seq])
                        emit_state(dhist, l, 15, pdp[l, seq])
                store_tokens(lambda tb, seq=seq, tok0=tok0: yp[seq, tok0 + tb * 128:tok0 + (tb + 1) * 128, :], 8, PSTS)

        if KSAMP and KSTOP >= 99:
            SSTS = [(0, 128)]
            load_tokens(lambda tb: xs[0:128, :], 1)
            for l in range(DEPTH_RUN):
                layer_setup(l, samp=True)
                rmsnorm(0, 128, 0, 'n1g', l * 8, xn, 'xn')
                if l == 0:
                    cvs = [WR.acquire(('cvd', l, i)) for i in range(4)]
                else:
                    cvs = nxt_cvs[0]
                phase_ab(l, 0, 0, 128, False, True, cvs, 'H', samp=True)
                phase_ab(l, 0, 0, 128, False, True, cvs, 'T', samp=True)
                for (jc, kc_, wc) in cvs:
                    WR.release(jc)
                phase_c(l, SSTS)
                phase_d(l, SSTS)
                rmsnorm(0, 128, 0, 'n2g', l * 8, xn, 'xn')
                if l < DEPTH_RUN - 1:
                    def acq_next_s(l=l):
                        nxt_cvs[0] = [WR.acquire(('cvd', l + 1, i)) for i in range(4)]
                    phase_e(l, SSTS, pre_down_b=acq_next_s)
                else:
                    phase_e(l, SSTS)
            store_tokens(lambda tb: ys[0:128, :], 1, SSTS)

        assert WR.nxt == len(WR.seq), (WR.nxt, len(WR.seq))
        S.emit(es)
    return nc


_CACHE = {}


def kernel(**inp):
    f32 = np.float32
    g = {k: np.ascontiguousarray(np.asarray(v, dtype=f32)) for k, v in inp.items()}
    if 'nc' not in _CACHE:
        _CACHE['nc'] = build_program()
    nc = _CACHE['nc']
    wl = [1.0, 1.0, 1.0, 1.0]
    rc = np.zeros((128, 2, 15), f32)
    wins = {(0, 0): 2, (0, 1): 4, (1, 0): 8, (1, 1): 16}
    for c in range(2):
        for gi in range(2):
            w = wins[(c, gi)]
            for t in range(15):
                rc[64 * gi:64 * gi + 64, c, t] = 1.0 / min(w, t + 1)
    wnames = ["norm1_g", "mem_norm_g", "w_in", "a_ln_g", "a_ln_b", "a_ws", "a_bs", "b_conv_w", "b_conv_b", "b_ln_g",
              "b_ln_b", "c_conv_w", "d_w", "d_scale", "w_mem_kv", "w_branch", "w_gate", "b_gate", "w_out", "norm2_g",
              "w_ffn_gate", "w_ffn_up", "w_ffn_down", "final_norm_g"]
    in_maps = []
    for c in range(NCORES):
        m = {k: g[k] for k in wnames}
        m["xp"] = g["x_prompt"][2 * c:2 * c + 2]
        m["xs"] = g["x_sample"][4 * c:4 * c + 4].reshape(128, D)
        m["memp"] = g["mem_prompt"][2 * c:2 * c + 2]
        m["ck"] = np.ascontiguousarray(g["cache_mem_k"][:, 4 * c:4 * c + 4].reshape(L, 4, NMEM, W))
        m["cv"] = np.ascontiguousarray(g["cache_mem_v"][:, 4 * c:4 * c + 4].reshape(L, 4, NMEM, W))
        m["scb"] = np.ascontiguousarray(g["state_conv_b"][:, 4 * c:4 * c + 4])
        m["scc"] = np.ascontiguousarray(g["state_conv_c"][:, 4 * c:4 * c + 4])
        m["spd"] = np.ascontiguousarray(g["state_pool_d"][:, 4 * c:4 * c + 4])
        m["rctab"] = rc
        in_maps.append(m)
    res = run_bass_kernel_spmd(nc, in_maps, core_ids=list(range(NCORES)))
    R = res.results

    def cat(name, axis):
        return np.concatenate([np.asarray(r[name]) for r in R], axis=axis)

    y_prompt = cat("yp", 0)
    y_sample = cat("ys", 0).reshape(32, 32, D)
    new_mem_k = cat("mk", 1).reshape(L, 16, NMEM, 4, 64)
    new_mem_v = cat("mv", 1).reshape(L, 16, NMEM, 4, 64)
    cb_p = cat("cbp", 1)
    cc_p = cat("ccp", 1)
    pd_p = cat("pdp", 1)
    av_s = cat("avs", 1).reshape(L, 32, 32, W)
    cb_s = cat("cbs", 1)
    cc_s = cat("ccs", 1)
    pd_s = cat("pds", 1)
    return (y_prompt, y_sample, new_mem_k, new_mem_v, cb_p, cc_p, pd_p, av_s, cb_s, cc_s, pd_s)
```

```python
import numpy as np
from contextlib import ExitStack
import concourse.bass as bass
import concourse.mybir as mybir
from concourse.bass_utils import run_bass_kernel_spmd

F32 = mybir.dt.float32
BF16 = mybir.dt.bfloat16
AF = mybir.ActivationFunctionType
ALU = mybir.AluOpType

NCORES = 8
D = 1024
W = 256
DFF = 2816
L = 4
SEQ = 2048
TP = 1024
NMEM = 256
EPS = 1e-6
ENGS = ('pe', 'act', 'dve', 'pool', 'sp')
NSLOT = 8
import os
DEPTH_RUN = int(os.environ.get('KDEPTH', L))
KSEQS = int(os.environ.get('KSEQS', 2))
KTILES = int(os.environ.get('KTILES', 2))
KSTOP = int(os.environ.get('KSTOP', 99))
KSAMP = int(os.environ.get('KSAMP', 1))


class Sched:
    def __init__(self, nc):
        self.nc = nc
        self.ops = {e: [] for e in ENGS}
        self.clock = {e: {} for e in ENGS}
        self.evclock = {}
        self.lastw = {}
        self.readers = {}
        self.dma_count = {}
        self.signals = {e: set() for e in ENGS}

    def add(self, eng, fn, reads=(), writes=(), dma_key=None, extra=()):
        deps = set(extra)
        for r in reads:
            ev = self.lastw.get(r)
            if ev is not None:
                deps.add(ev)
            if isinstance(r, tuple) and r[0] == 'ps':
                for rv in self.readers.get(r, ()):
                    if rv[0] != eng:
                        deps.add(rv)
        for w in writes:
            ev = self.lastw.get(w)
            if ev is not None:
                deps.add(ev)
            rd = self.readers.get(w)
            if rd:
                deps.update(rd)
        if dma_key is not None and self.dma_count.get(dma_key, 0) > 0:
            deps.add((dma_key, self.dma_count[dma_key]))
        clk = self.clock[eng]
        waits = []
        best = {}
        for (s, n) in deps:
            if best.get(s, 0) < n:
                best[s] = n
        for (s, n) in sorted(best.items(), key=lambda t: str(t[0])):
            if clk.get(s, 0) >= n:
                continue
            waits.append((s, n))
            oc = self.evclock[(s, n)]
            for k, v in oc.items():
                if clk.get(k, 0) < v:
                    clk[k] = v
            if s in self.signals:
                self.signals[s].add(n)
        idx = len(self.ops[eng]) + 1
        if dma_key is None:
            ev = (eng, idx)
            if eng == 'pe':
                clk['pe'] = idx
        else:
            n = self.dma_count.get(dma_key, 0) + 1
            self.dma_count[dma_key] = n
            ev = (dma_key, n)
        snap = dict(clk)
        snap[ev[0]] = ev[1]
        self.evclock[ev] = snap
        self.ops[eng].append((fn, waits, idx, ev if dma_key is not None else None))
        for r in reads:
            self.readers.setdefault(r, []).append(ev)
        for w in writes:
            self.lastw[w] = ev
            self.readers[w] = []
        return ev

    def emit(self, es):
        nc = self.nc
        sems = {}
        for e in ENGS:
            if self.signals[e]:
                sems[e] = es.enter_context(nc.semaphore("s_" + e))
        for i, k in enumerate(self.dma_count):
            sems[k] = es.enter_context(nc.semaphore("d%d" % i))
        rank = {}
        for e in ENGS:
            for i, n in enumerate(sorted(self.signals[e])):
                rank[(e, n)] = i + 1

        def val(s, n):
            if s in self.signals:
                return rank[(s, n)]
            return 16 * n

        block = es.enter_context(nc.Block())

        def run(engname, eng):
            sig = self.signals[engname]
            for (fn, waits, idx, dma) in self.ops[engname]:
                for (s, n) in waits:
                    eng.wait_ge(sems[s], val(s, n))
                inst = fn(eng)
                if dma is not None:
                    inst.then_inc(sems[dma[0]], 16)
                elif idx in sig:
                    inst.then_inc(sems[engname], 1)
            if engname == 'sp':
                for k, n in self.dma_count.items():
                    eng.wait_ge(sems[k], 16 * n)

        @block.tensor
        def _(e):
            run('pe', e)

        @block.scalar
        def _(e):
            run('act', e)

        @block.vector
        def _(e):
            run('dve', e)

        @block.gpsimd
        def _(e):
            run('pool', e)

        @block.sync
        def _(e):
            run('sp', e)


def build_program():
    nc = bass.Bass("TRN2", target_bir_lowering=False)

    def din(name, shape):
        return nc.dram_tensor(name, list(shape), F32, kind="ExternalInput").ap()

    def dout(name, shape):
        return nc.dram_tensor(name, list(shape), F32, kind="ExternalOutput").ap()

    xp = din("xp", [2, SEQ, D])
    xs = din("xs", [128, D])
    memp = din("memp", [2, NMEM, D])
    ck = din("ck", [L, 4, NMEM, W])
    cv = din("cv", [L, 4, NMEM, W])
    scb = din("scb", [L, 4, 30, W])
    scc = din("scc", [L, 4, 2, W])
    spd = din("spd", [L, 4, 15, W])
    norm1_g = din("norm1_g", [L, D])
    mem_norm_g = din("mem_norm_g", [L, D])
    w_in = din("w_in", [L, D, 9 * W])
    a_ln_g = din("a_ln_g", [L, W])
    a_ln_b = din("a_ln_b", [L, W])
    a_ws = din("a_ws", [L, 4, 128, 128])
    a_bs = din("a_bs", [L, 4, 128])
    b_conv_w = din("b_conv_w", [L, 31, W])
    b_conv_b = din("b_conv_b", [L, W])
    b_ln_g = din("b_ln_g", [L, W])
    b_ln_b = din("b_ln_b", [L, W])
    c_conv_w = din("c_conv_w", [L, 3, W])
    d_w = din("d_w", [L, 4, 64, 64])
    d_scale = din("d_scale", [L, W])
    w_mem_kv = din("w_mem_kv", [L, D, 2 * W])
    w_branch = din("w_branch", [L, 5, W, D])
    w_gate = din("w_gate", [L, 5, D, D])
    b_gate = din("b_gate", [L, 5, D])
    w_out = din("w_out", [L, D, D])
    norm2_g = din("norm2_g", [L, D])
    w_ffn_gate = din("w_ffn_gate", [L, D, DFF])
    w_ffn_up = din("w_ffn_up", [L, D, DFF])
    w_ffn_down = din("w_ffn_down", [L, DFF, D])
    final_norm_g = din("final_norm_g", [D])
    rctab = din("rctab", [128, 2, 15])

    yp = dout("yp", [2, SEQ, D])
    ys = dout("ys", [128, D])
    mk = dout("mk", [L, 2, NMEM, W])
    mv = dout("mv", [L, 2, NMEM, W])
    cbp = dout("cbp", [L, 2, 30, W])
    ccp = dout("ccp", [L, 2, 2, W])
    pdp = dout("pdp", [L, 2, 15, W])
    avs = dout("avs", [L, 128, W])
    cbs = dout("cbs", [L, 4, 30, W])
    ccs = dout("ccs", [L, 4, 2, W])
    pds = dout("pds", [L, 4, 15, W])

    es = ExitStack()
    with es:
        S = Sched(nc)

        def SB(name, shape, dt):
            return es.enter_context(nc.sbuf_tensor(name, list(shape), dt))

        x = SB("x", [128, 8, TP], F32)
        xn = SB("xn", [128, 8, TP], BF16)
        hb = SB("hb", [128, 18, TP], BF16)
        wr = [SB("wr%d" % i, [128, 2048], BF16) for i in range(NSLOT)]
        KT = SB("KT", [128, L, 2, 256], BF16)
        VV = SB("VV", [128, L, 2, 256], BF16)
        bext = SB("bext", [128, 2, 30 + 512], F32)
        bextbs = [SB("bextb%d" % i, [128, 2, 30 + 512 + 2], BF16) for i in range(2)]
        cext = SB("cext", [128, 2, 2 + 512], F32)
        dext = SB("dext", [128, 2, 15 + 512], F32)
        acc = SB("acc", [128, 2, 512], F32)
        cacc = SB("cacc", [128, 2, 512], F32)
        trA = SB("trA", [128, 15 + 512], F32)
        trB = SB("trB", [128, 15 + 512], F32)
        pooleds = [SB("pooled%d" % i, [128, 2, 512], BF16) for i in range(2)]
        accb = SB("accb", [128, 2, 512], BF16)
        sqb = SB("sqb", [128, 2, 512], BF16)
        sgb = SB("sgb", [128, 2, 512], F32)
        vabufs = [SB("vabuf%d" % i, [128, 4, 256], BF16) for i in range(2)]
        gv = [SB("gv%d" % i, [128, 256], F32) for i in range(4)]
        bnst = [SB("bnst%d" % i, [128, 6], F32) for i in range(4)]
        bnmv_all = SB("bnmv_all", [128, 4, 2], F32)
        rs = SB("rs", [128, 512], F32)
        meansb = SB("meansb", [128, 512], F32)
        msq = SB("msq", [128, 512], F32)
        varsb = SB("varsb", [128, 512], F32)
        sgC = [SB("sgC%d" % i, [128, 512], F32) for i in range(3)]
        gpC = [SB("gpC%d" % i, [128, 512], BF16) for i in range(3)]
        PT = [SB("PT%d" % i, [128, 512], BF16) for i in range(4)]
        den = SB("den", [128, 512], F32)
        bhist = SB("bhist", [128, L, 2, 30], F32)
        chist = SB("chist", [128, L, 2, 2], F32)
        dhist = SB("dhist", [128, L, 2, 15], F32)
        ident = SB("ident", [128, 128], F32)
        identb = SB("identb", [128, 128], BF16)
        ones_b = SB("ones_b", [128, 128], BF16)
        onesw_b = SB("onesw_b", [128, 128], BF16)
        epst = SB("epst", [128, 1], F32)
        rct = SB("rct", [128, 2, 15], F32)
        NPV = 640
        PV = SB("PV", [128, NPV], F32)
        lnab = SB("lnab", [128, 2, 256], F32)
        wmT = SB("wmT", [128, 4, 128], BF16)
        bsrow = SB("bsrow", [1, 512], BF16)
        dwbd = SB("dwbd", [128, 2, 128], BF16)
        sttmp = SB("sttmp", [128, 256], F32)
        ps = [es.enter_context(nc.psum_tensor("ps%d" % i, [128, 512], F32)) for i in range(8)]

        def act(out, in_, func, reads, writes, **kw):
            S.add('act', lambda e: e.activation(out=out, in_=in_, func=func, **kw), reads, writes)

        def tt(out, a, b, op, reads, writes, eng='dve'):
            S.add(eng, lambda e: e.tensor_tensor(out=out, in0=a, in1=b, op=op), reads, writes)

        def ts(out, a, s1, s2, op0, op1, reads, writes, eng='dve'):
            if s2 is None:
                S.add(eng, lambda e: e.tensor_scalar(out=out, in0=a, scalar1=s1, scalar2=None, op0=op0), reads, writes)
            else:
                S.add(eng, lambda e: e.tensor_scalar(out=out, in0=a, scalar1=s1, scalar2=s2, op0=op0, op1=op1), reads, writes)

        def stt(out, a, sc, b, op0, op1, reads, writes):
            S.add('dve', lambda e: e.scalar_tensor_tensor(out=out, in0=a, scalar=sc, in1=b, op0=op0, op1=op1), reads, writes)

        def cp(out, in_, reads, writes, eng='dve'):
            S.add(eng, lambda e: e.tensor_copy(out=out, in_=in_), reads, writes)

        def mm(out, lhsT, rhs, start, stop, reads, writes):
            S.add('pe', lambda e: e.matmul(out, lhsT, rhs, start=start, stop=stop), reads, writes)

        def tp(out, in_, idt, reads, writes):
            S.add('pe', lambda e: e.transpose(out=out, in_=in_, identity=idt), reads, writes)

        dma_ctr = [0]

        def dma(out, in_, reads, writes, key=None, eng='sp'):
            if key is None:
                dma_ctr[0] += 1
                key = ('m', dma_ctr[0] % 16)
            S.add(eng, lambda e: e.dma_start(out=out, in_=in_), reads, writes, dma_key=key)

        def recip(out, in_, reads, writes):
            S.add('dve', lambda e: e.reciprocal(out=out, in_=in_), reads, writes)

        def memset(ap, v, writes, eng='pool'):
            S.add(eng, lambda e: e.memset(ap, v), (), writes)

        class PSA:
            free = list(range(8))

            @classmethod
            def get(cls):
                assert cls.free, "PSUM exhausted"
                return cls.free.pop(0)

            @classmethod
            def put(cls, b):
                cls.free.append(b)

        def P(b):
            return ('ps', b)

        def wdesc(tag):
            k = tag[0]
            if k == 'win':
                _, l, j = tag
                return w_in[l, :, 256 * j:256 * j + 256].rearrange("(k p) n -> p k n", p=128), 8, 256
            if k == 'gate':
                _, l, mp, n = tag
                return w_gate[l, n, :, 256 * mp:256 * mp + 256].rearrange("(k p) n -> p k n", p=128), 8, 256
            if k == 'br':
                _, l, mp, n = tag
                return w_branch[l, n, :, 256 * mp:256 * mp + 256].rearrange("(c p) m -> p c m", p=128), 2, 256
            if k == 'wout':
                _, l, j = tag
                return w_out[l, :, 256 * j:256 * j + 256].rearrange("(k p) n -> p k n", p=128), 8, 256
            if k == 'fg':
                _, l, j = tag
                return w_ffn_gate[l, :, 256 * j:256 * j + 256].rearrange("(k p) n -> p k n", p=128), 8, 256
            if k == 'fu':
                _, l, j = tag
                return w_ffn_up[l, :, 256 * j:256 * j + 256].rearrange("(k p) n -> p k n", p=128), 8, 256
            if k == 'dn':
                _, l, half, m = tag
                r0, r1 = (0, 1536) if half == 0 else (1536, DFF)
                return (w_ffn_down[l, r0:r1, 128 * m:128 * m + 128].rearrange("(k p) n -> p k n", p=128),
                        (r1 - r0) // 128, 128)
            if k == 'cvd':
                return None, 16, 128
            if k == 'kvK':
                return w_mem_kv[tag[1], :, 0:256].rearrange("(k p) n -> p k n", p=128), 8, 256
            if k == 'kvV':
                return w_mem_kv[tag[1], :, 256:512].rearrange("(k p) n -> p k n", p=128), 8, 256
            raise ValueError(tag)

        NN_ORDER = (2, 3, 4, 0, 1)
        FH = [(0, 6), (6, 11)]

        def layer_seq(l, nst, first, last):
            seq = []
            if first:
                for i in range(4):
                    seq.append(('cvd', l, i))
            for _ in range(nst):
                for j in (3, 2, 5, 6, 4, 7, 8, 0, 1):
                    seq.append(('win', l, j))
            for mp in range(4):
                for n in NN_ORDER:
                    seq.append(('gate', l, mp, n))
                    seq.append(('br', l, mp, n))
            for j in range(4):
                seq.append(('wout', l, j))
            for half in range(2):
                for j in range(*FH[half]):
                    seq.append(('fg', l, j))
                    seq.append(('fu', l, j))
                if half == 1 and not last:
                    for i in range(4):
                        seq.append(('cvd', l + 1, i))
                for _ in range(nst if half == 1 else 1):
                    for m in range(8):
                        seq.append(('dn', l, half, m))
            return seq

        MAXFLY = 3
        CVD_ENG = os.environ.get('KCVD', 'act')

        class WR:
            evs = []
            seq = []
            loaded = 0
            nxt = 0
            free = list(range(NSLOT))
            slot_of = {}

            @classmethod
            def pump(cls):
                while cls.loaded < len(cls.seq) and cls.free:
                    j = cls.loaded
                    tag = cls.seq[j]
                    src, a, b = wdesc(tag)
                    slot = cls.free.pop(0)
                    cls.slot_of[j] = slot
                    dst = wr[slot][:, 0:a * b].rearrange("p (a b) -> p a b", b=b)
                    allk = [('wr', slot)] + [('wrx', slot, kk) for kk in range(16)]
                    if tag[0] == 'cvd':
                        _, l_, i_ = tag
                        for kk in range(16):
                            e_ = i_ * 16 + kk
                            if e_ >= 62:
                                break
                            c_, k_ = e_ // 31, e_ % 31
                            wk = allk if kk == 0 else [('wrx', slot, kk)]
                            if CVD_ENG == 'act':
                                act(dst[:, kk, :], identb[:], AF.Copy, ['identb', 'PV'], wk, scale=bcwcol(l_, k_, c_))
                            else:
                                ts(dst[:, kk, :], identb[:], bcwcol(l_, k_, c_), None, ALU.mult, None, ['identb', 'PV'], wk,
                                   eng=CVD_ENG)
                        cls.evs.append(None)
                    else:
                        prev = None
                        cnt = 0
                        for jj in range(j - 1, -1, -1):
                            if cls.seq[jj][0] != 'cvd':
                                cnt += 1
                                if cnt == MAXFLY:
                                    prev = cls.evs[jj]
                                    break
                        extra = [prev] if prev is not None else []
                        ev = S.add('pool', lambda e, dst=dst, src=src: e.dma_start(out=dst, in_=src), (), allk,
                                   dma_key=('wr', slot), extra=extra)
                        cls.evs.append(ev)
                    cls.loaded += 1

            @classmethod
            def acquire(cls, tag):
                j = cls.nxt
                assert cls.seq[j] == tag, (cls.seq[j], tag)
                cls.nxt += 1
                cls.pump()
                assert j < cls.loaded, "weight ring: no free slot for %s" % (tag,)
                _, a, b = wdesc(tag)
                slot = cls.slot_of[j]
                return j, ('wr', slot), wr[slot][:, 0:a * b].rearrange("p (a b) -> p a b", b=b)

            @classmethod
            def release(cls, j):
                cls.free.append(cls.slot_of[j])
                cls.pump()

        memset(ident[:], 0.0, ['ident'])
        S.add('pool', lambda e: e.affine_select(out=ident[:], in_=ident[:], pattern=[[-1, 128]],
                                                compare_op=ALU.not_equal, fill=1.0, base=0, channel_multiplier=1),
              ['ident'], ['ident'])
        cp(identb[:], ident[:], ['ident'], ['identb'])
        memset(ones_b[:], 1.0, ['ones_b'])
        memset(onesw_b[:], 1.0 / 256.0, ['onesw_b'])
        memset(epst[:], EPS, ['epst'])
        memset(dwbd[:], 0.0, ['dwbd'])
        memset(bhist[:], 0.0, ['bhist'])
        memset(chist[:], 0.0, ['chist'])
        memset(dhist[:], 0.0, ['dhist'])
        dma(rct[:], rctab, [], ['rct'])

        pvcol = {}
        groups = [
            [('n1g', norm1_g.rearrange("l (k p) -> (l k) p", p=128), 32),
             ('n2g', norm2_g.rearrange("l (k p) -> (l k) p", p=128), 32),
             ('mng', mem_norm_g.rearrange("l (k p) -> (l k) p", p=128), 32),
             ('fng', final_norm_g.rearrange("(k p) -> k p", p=128), 8)],
            [('bg0', b_gate.rearrange("l n (m p) -> (l n m) p", p=128)[0:128, :], 128)],
            [('bg1', b_gate.rearrange("l n (m p) -> (l n m) p", p=128)[128:160, :], 32)],
            [('bcw0', b_conv_w[0:2].rearrange("l k (c p) -> (l k c) p", p=128), 124)],
            [('bcw1', b_conv_w[2:4].rearrange("l k (c p) -> (l k c) p", p=128), 124)],
            [('ccw', c_conv_w.rearrange("l k (c p) -> (l k c) p", p=128), 24),
             ('bcb', b_conv_b.rearrange("l (c p) -> (l c) p", p=128), 8),
             ('blg', b_ln_g.rearrange("l (c p) -> (l c) p", p=128), 8),
             ('blb', b_ln_b.rearrange("l (c p) -> (l c) p", p=128), 8),
             ('dsc', d_scale.rearrange("l (c p) -> (l c) p", p=128), 8)],
        ]
        colbase = 0
        stgs_ = [acc[:].rearrange("p a b -> p (a b)"), cacc[:].rearrange("p a b -> p (a b)")]
        stgk_ = [[('acc', 0), ('acc', 1)], [('cacc', 0), ('cacc', 1)]]
        for gi, grp in enumerate(groups):
            stg, sk_ = stgs_[gi % 2], stgk_[gi % 2]
            r0 = 0
            for (nm, src, nr) in grp:
                dma(stg[r0:r0 + nr, 0:128], src, [], sk_)
                pvcol[nm] = colbase + r0
                r0 += nr
            b = PSA.get()
            tp(ps[b][:, 0:r0], stg[0:r0, 0:128], ident[0:r0, 0:r0], sk_ + ['ident'], [P(b)])
            cp(PV[:, colbase:colbase + r0], ps[b][:, 0:r0], [P(b)], ['PV'])
            PSA.put(b)
            colbase += r0
        assert colbase <= NPV

        def pv(nm, idx):
            c = pvcol[nm] + idx
            return PV[:, c:c + 1]

        def bgcol(l, n, m):
            r = (l * 5 + n) * 8 + m
            return pv('bg0', r) if r < 128 else pv('bg1', r - 128)

        def bcwcol(l, k, c):
            if l < 2:
                return pv('bcw0', (l * 31 + k) * 2 + c)
            return pv('bcw1', ((l - 2) * 31 + k) * 2 + c)

        def rms_a(off, n, si):
            act(hb[:, 0:8, off:off + n], x[:, :, off:off + n], AF.Square,
                [('x', kc, si) for kc in range(8)], [('h', kc, si) for kc in range(8)])

        def rms_b(off, n, si, gname, gidx0, dst, dkey, stats=True):
            if stats:
                b = PSA.get()
                for kc in range(8):
                    mm(ps[b][:, 0:n], ones_b[:], hb[:, kc, off:off + n], kc == 0, kc == 7,
                       [('h', kc, si), 'ones_b'], [P(b)])
                act(rs[:, 0:n], ps[b][:, 0:n], AF.Sqrt, [P(b), 'epst'], ['rs'], scale=1.0 / D, bias=epst[:])
                PSA.put(b)
                recip(rs[:, 0:n], rs[:, 0:n], ['rs'], ['rs'])
            if dst is not None:
                for kc in range(8):
                    stt(dst[:, kc, off:off + n], x[:, kc, off:off + n], pv(gname, gidx0 + kc), rs[:, 0:n],
                        ALU.mult, ALU.mult, [('x', kc, si), 'rs', 'PV'], [(dkey, kc, si)])

        def rmsnorm(off, n, si, gname, gidx0, dst, dkey, fm=False):
            rms_a(off, n, si)
            rms_b(off, n, si, gname, gidx0, dst, dkey)

        def load_tokens(src_rows_fn, nblk):
            stgs = [acc[:].rearrange("p a b -> p (a b)"), cacc[:].rearrange("p a b -> p (a b)")]
            keys = [[('acc', 0), ('acc', 1)], [('cacc', 0), ('cacc', 1)]]
            for tb in range(nblk):
                sg_, kk = stgs[tb % 2], keys[tb % 2]
                dma(sg_[:, :], src_rows_fn(tb), [], kk)
                for half in range(2):
                    b = PSA.get()
                    for j in range(4):
                        kc = half * 4 + j
                        tp(ps[b][:, j * 128:(j + 1) * 128], sg_[:, kc * 128:(kc + 1) * 128], ident[:],
                           kk + ['ident'], [P(b)])
                    si = tb // 4
                    S.add('act' if half == 0 else 'dve',
                          (lambda e, b=b, half=half, tb=tb:
                           (e.activation(out=x[:, half * 4:half * 4 + 4, tb * 128:(tb + 1) * 128],
                                         in_=ps[b][:].rearrange("p (j t) -> p j t", t=128), func=AF.Copy)
                            if half == 0 else
                            e.tensor_copy(out=x[:, half * 4:half * 4 + 4, tb * 128:(tb + 1) * 128],
                                          in_=ps[b][:].rearrange("p (j t) -> p j t", t=128)))),
                          [P(b)], [('x', half * 4 + j, si) for j in range(4)])
                    PSA.put(b)

        def store_tokens(dst_rows_fn, nblk, nst_list):
            ystgs = [cacc[:].rearrange("p a b -> p (a b)"), cext[:].rearrange("p a b -> p (a b)")[:, 0:1024]]
            ysk = [[('cacc', 0), ('cacc', 1)], ['cext']]
            for si, (off, n) in enumerate(nst_list):
                rms_a(off, n, si)
                rms_b(off, n, si, 'fng', 0, x, 'x')
                for tb in range(off // 128, (off + n) // 128):
                    t0 = tb * 128
                    ystg, sk = ystgs[tb % 2], ysk[tb % 2]
                    for half in range(2):
                        b = PSA.get()
                        for j in range(4):
                            kc = half * 4 + j
                            tp(ps[b][:, j * 128:(j + 1) * 128], x[:, kc, t0:t0 + 128], ident[:],
                               [('x', kc, si), 'ident'], [P(b)])
                        if half == 0:
                            act(ystg[:, 0:512], ps[b][:], AF.Copy, [P(b)], sk)
                        else:
                            cp(ystg[:, 512:1024], ps[b][:], [P(b)], sk)
                        PSA.put(b)
                    dma(dst_rows_fn(tb), ystg[:, :], sk, [])

        def hkeys(slots, si):
            return [('h', s, si) for s in slots]

        def prologue(seq):
            load_tokens(lambda tb: memp[seq, tb * 128:(tb + 1) * 128, :], 2)
            rms_a(0, 256, 0)
            for l in range(L):
                rms_b(0, 256, 0, 'mng', l * 8, xn, 'xn', stats=(l == 0))
                jK, kK, wK = WR.acquire(('kvK', l))
                jV, kV, wV = WR.acquire(('kvV', l))
                xr = [('xn', kc, 0) for kc in range(8)]
                for c in range(2):
                    b = PSA.get()
                    for kc in range(8):
                        mm(ps[b][:, 0:256], wK[:, kc, c * 128:(c + 1) * 128], xn[:, kc, 0:256], kc == 0, kc == 7,
                           xr + [kK], [P(b)])
                    act(KT[:, l, c, :], ps[b][:, 0:256], AF.Copy, [P(b)], [('KT', l)])
                    PSA.put(b)
                for mc in range(2):
                    b = PSA.get()
                    for kc in range(8):
                        mm(ps[b][:, 0:256], xn[:, kc, mc * 128:(mc + 1) * 128], wK[:, kc, :], kc == 0, kc == 7,
                           xr + [kK], [P(b)])
                    b2 = PSA.get()
                    for kc in range(8):
                        mm(ps[b2][:, 0:256], xn[:, kc, mc * 128:(mc + 1) * 128], wV[:, kc, :], kc == 0, kc == 7,
                           xr + [kV], [P(b2)])
                    act(gv[0][:], ps[b][:, 0:256], AF.Copy, [P(b)], ['gv0'])
                    PSA.put(b)
                    dma(mk[l, seq, mc * 128:(mc + 1) * 128, :], gv[0][:], ['gv0'], [])
                    act(gv[1][:], ps[b2][:, 0:256], AF.Copy, [P(b2)], ['gv1'])
                    cp(VV[:, l, mc, :], ps[b2][:, 0:256], [P(b2)], [('VV', l)])
                    PSA.put(b2)
                    dma(mv[l, seq, mc * 128:(mc + 1) * 128, :], gv[1][:], ['gv1'], [])
                WR.release(jK)
                WR.release(jV)

        def layer_setup(l, samp=False):
            dma(lnab[:, 0, :], a_ln_g[l:l + 1, :].partition_broadcast(128), [], ['lnab'])
            dma(lnab[:, 1, :], a_ln_b[l:l + 1, :].partition_broadcast(128), [], ['lnab'])
            wst = cacc[:].rearrange("p a b -> p (a b)")[:, 0:512].rearrange("p (g j) -> p g j", j=128)
            ck_ = [('cacc', 0), ('cacc', 1)]
            if not samp:
                dma(wst, a_ws[l].rearrange("g i j -> i g j"), [], ck_)
                S.add('dve', lambda e: e.memset(wst[0:64, :, 64:128], 0.0), ck_, ck_)
                dma(den[0:1, :], a_bs[l:l + 1].rearrange("o g i -> o (g i)"), [], ['den'])
            else:
                S.add('dve', lambda e: e.memset(wst, 0.0), ck_, ck_)
                for q in range(4):
                    dma(wst[32 * q:32 * q + 32, :, 32 * q:32 * q + 32],
                        a_ws[l, :, 0:32, 0:32].rearrange("g i j -> i g j"), [], ck_)
                    dma(den[0:1, :].rearrange("o (g x) -> o g x", x=128)[:, :, 32 * q:32 * q + 32],
                        a_bs[l:l + 1, :, 0:32], [], ['den'])
            b = PSA.get()
            for g in range(4):
                tp(ps[b][:, g * 128:(g + 1) * 128], wst[:, g, :], ident[:], ck_ + ['ident'], [P(b)])
            cp(wmT[:].rearrange("p g i -> p (g i)"), ps[b][:], [P(b)], ['wmT'])
            PSA.put(b)
            cp(bsrow[:], den[0:1, :], ['den'], ['bsrow'])
            for c in range(2):
                for gi in range(2):
                    dma(dwbd[64 * gi:64 * gi + 64, c, 64 * gi:64 * gi + 64], d_w[l, 2 * c + gi], [], ['dwbd'],
                        key=('dw', 0), eng='pool')

        def sample_state_load(l):
            stB = acc[:].rearrange("p a b -> p (a b)")
            stC = cacc[:].rearrange("p a b -> p (a b)")
            kB = [('acc', 0), ('acc', 1)]
            kC = [('cacc', 0), ('cacc', 1)]
            S.add('dve', lambda e: e.memset(stB[:, 0:256], 0.0), kB, kB)
            S.add('dve', lambda e: e.memset(stC[:, 0:256], 0.0), kC, kC)
            for q in range(4):
                dma(stB[32 * q:32 * q + 30, 0:256], scb[l, q], [], kB)
                dma(stC[32 * q:32 * q + 2, 0:256], scc[l, q], [], kC)
                dma(stC[32 * q + 2:32 * q + 17, 0:256], spd[l, q], [], kC)
            for c in range(2):
                b = PSA.get()
                tp(ps[b][:, 0:128], stB[:, c * 128:(c + 1) * 128], ident[:], kB + ['ident'], [P(b)])
                tp(ps[b][:, 128:256], stC[:, c * 128:(c + 1) * 128], ident[:], kC + ['ident'], [P(b)])
                pv_b = ps[b][:, 0:128].rearrange("p (q r) -> p q r", r=32)
                pv_c = ps[b][:, 128:256].rearrange("p (q r) -> p q r", r=32)
                cp(bext[:, c, 0:248].rearrange("p (q t) -> p q t", t=62)[:, :, 0:30], pv_b[:, :, 0:30], [P(b)], ['bext'])
                cp(cext[:, c, 0:136].rearrange("p (q t) -> p q t", t=34)[:, :, 0:2], pv_c[:, :, 0:2], [P(b)], ['cext'])
                cp(dext[:, c, 0:188].rearrange("p (q t) -> p q t", t=47)[:, :, 0:15], pv_c[:, :, 2:17], [P(b)], ['dext'])
                PSA.put(b)
            kst = sgb[:].bitcast(BF16).rearrange("p a (m c) -> p (a m) c", c=256)
            ks_ = [('sgb', 0), ('sgb', 1)]
            dma(kst, ck[l].rearrange("q (mc p) c -> p (q mc) c", p=128), [], ks_, key=('kv', 0), eng='pool')
            dma(VV[:].rearrange("p a b c -> p (a b) c"), cv[l].rearrange("q (mc p) c -> p (q mc) c", p=128), [],
                [('VV', i) for i in range(L)], key=('kv', 1), eng='pool')
            for q in range(4):
                b = PSA.get()
                pbf = ps[b][:].bitcast(BF16)
                for c in range(2):
                    for mc in range(2):
                        o0 = (c * 2 + mc) * 128
                        tp(pbf[:, o0:o0 + 128], kst[:, q * 2 + mc, c * 128:(c + 1) * 128], identb[:],
                           ks_ + ['identb'], [P(b)])
                cp(KT[:, q, :, :].rearrange("p c m -> p (c m)"), pbf[:, 0:512], [P(b)], [('KT', i) for i in range(L)])
                PSA.put(b)

        def sample_state_emit(l):
            stg_ = cacc[:].rearrange("p a b -> p (a b)")
            kC = [('cacc', 0), ('cacc', 1)]
            for (buf, key, H, wdt, dst) in ((bext, 'bext', 30, 30, cbs), (cext, 'cext', 2, 2, ccs),
                                            (dext, 'dext', 15, 15, pds)):
                tl = H + 32
                bs_ = [PSA.get(), PSA.get()]
                for q in range(4):
                    for c in range(2):
                        j = q * 2 + c
                        tp(ps[bs_[j // 4]][0:wdt, (j % 4) * 128:(j % 4 + 1) * 128],
                           buf[:, c, q * tl + tl - wdt:q * tl + tl], ident[:], [key, 'ident'], [P(bs_[j // 4])])
                for hb_ in range(2):
                    cp(stg_[0:wdt, hb_ * 512:(hb_ + 1) * 512], ps[bs_[hb_]][0:wdt, :], [P(bs_[hb_])], kC)
                    PSA.put(bs_[hb_])
                dma(dst[l].rearrange("q r c -> r q c"), stg_[0:wdt, :].rearrange("r (q c) -> r q c", c=256), kC, [])

        def phase_ab(l, si, off, n, first_of_seq, last_of_tile, cvs, part, samp=False, defer_ln=False):
            xr = [('xn', kc, si) for kc in range(8)]
            nb = n // 128
            bextb = bextbs[si]
            pooled = pooleds[si]
            vabuf = vabufs[si]
            isH = (part == 'H')
            isT = (part == 'T')

            def v(ap):
                return ap.rearrange("p (q t) -> p q t", t=32) if samp else ap

            def ext_new(buf, mi, H):
                if samp:
                    return buf[:, mi, 0:4 * (H + 32)].rearrange("p (q t) -> p q t", t=H + 32)[:, :, H:H + 32]
                return buf[:, mi, H:H + n]

            def ext_out(ap2, H):
                if samp:
                    return ap2[:, 0:4 * (H + 32)].rearrange("p (q t) -> p q t", t=H + 32)[:, :, 0:32]
                return ap2[:, 0:n]

            NCB = 4 * 62 - 30 if samp else n
            NCC = 4 * 34 - 2 if samp else n

            def zmm(b, wv, wk, mi):
                for kc in range(8):
                    mm(ps[b][:, 0:n], wv[:, kc, mi * 128:(mi + 1) * 128], xn[:, kc, off:off + n], kc == 0, kc == 7,
                       xr + [wk], [P(b)])

            if isH:
                if samp:
                    sample_state_load(l)
                elif si == 0:
                    cp(bext[:, :, 0:30], bhist[:, l, :, :], ['bhist'], ['bext'], eng='pool')
                    cp(cext[:, :, 0:2], chist[:, l, :, :], ['chist'], ['cext'], eng='pool')
                    cp(dext[:, :, 0:15], dhist[:, l, :, :], ['dhist'], ['dext'], eng='pool')
                else:
                    cp(bext[:, :, 0:30], bext[:, :, 512:542], ['bext'], ['bext'], eng='pool')
                    cp(cext[:, :, 0:2], cext[:, :, 512:514], ['cext'], ['cext'], eng='pool')
                    cp(dext[:, :, 0:15], dext[:, :, 512:527], ['dext'], ['dext'], eng='pool')

                j3, k3, w3 = WR.acquire(('win', l, 3))
                for mi in range(2):
                    b = PSA.get()
                    zmm(b, w3, k3, mi)
                    act(sgb[:, mi, 0:n], ps[b][:, 0:n], AF.Sigmoid, [P(b)], [('sgb', mi)])
                    PSA.put(b)
                WR.release(j3)
                j2, k2, w2 = WR.acquire(('win', l, 2))
                for mi in range(2):
                    b = PSA.get()
                    zmm(b, w2, k2, mi)
                    tt(ext_new(bext, mi, 30), v(ps[b][:, 0:n]), v(sgb[:, mi, 0:n]), ALU.mult, [P(b), ('sgb', mi)], ['bext'])
                    PSA.put(b)
                WR.release(j2)
                if last_of_tile and not samp:
                    cp(bhist[:, l, :, :], bext[:, :, n:n + 30], ['bext'], ['bhist'], eng='pool')
            if isH:
                XB = 30 + NCB
                act(bextb[:, :, 0:XB], bext[:, :, 0:XB], AF.Copy, ['bext'], [('bextb', si)])
            if isT:
                for c in range(2):
                    b = PSA.get()
                    for k in range(31):
                        e_ = c * 31 + k
                        jc, kc_, wc = cvs[e_ // 16]
                        kk = e_ % 16
                        mm(ps[b][:, 0:NCB], wc[:, kk, :], bextb[:, c, k:k + NCB], k == 0, k == 30,
                           [kc_, ('wrx', kc_[1], kk), ('bextb', si)], [P(b)])
                    act(acc[:, c, 0:NCB], ps[b][:, 0:NCB], AF.Identity, [P(b), 'PV'], [('acc', c)],
                        bias=pv('bcb', l * 2 + c))
                    PSA.put(b)
            if isH:
                j5, k5, w5 = WR.acquire(('win', l, 5))
                for mi in range(2):
                    b = PSA.get()
                    zmm(b, w5, k5, mi)
                    act(ext_new(cext, mi, 2), v(ps[b][:, 0:n]), AF.Copy, [P(b)], ['cext'])
                    PSA.put(b)
                WR.release(j5)
                j6, k6, w6 = WR.acquire(('win', l, 6))
                for mi in range(2):
                    b = PSA.get()
                    zmm(b, w6, k6, mi)
                    tt(ext_new(cext, mi, 2), v(ps[b][:, 0:n]), ext_new(cext, mi, 2), ALU.mult, [P(b), 'cext'], ['cext'])
                    PSA.put(b)
                WR.release(j6)
                if last_of_tile and not samp:
                    cp(chist[:, l, :, :], cext[:, :, n:n + 2], ['cext'], ['chist'], eng='pool')
                j4, k4, w4 = WR.acquire(('win', l, 4))
                for mi in range(2):
                    b = PSA.get()
                    zmm(b, w4, k4, mi)
                    act(hb[:, 12 + mi, off:off + n], ps[b][:, 0:n], AF.Copy, [P(b)], [('h', 12 + mi, si)])
                    PSA.put(b)
                WR.release(j4)
                for c in range(2):
                    ts(cacc[:, c, 0:NCC], cext[:, c, 0:NCC], pv('ccw', (l * 3 + 0) * 2 + c), None, ALU.mult, None,
                       ['cext', 'PV'], [('cacc', c)])
                    for k in (1, 2):
                        stt(cacc[:, c, 0:NCC], cext[:, c, k:k + NCC], pv('ccw', (l * 3 + k) * 2 + c), cacc[:, c, 0:NCC],
                            ALU.mult, ALU.add, ['cext', 'PV', ('cacc', c)], [('cacc', c)])
                    tt(v(hb[:, 12 + c, off:off + n]), v(hb[:, 12 + c, off:off + n]), ext_out(cacc[:, c, :], 2), ALU.mult,
                       [('h', 12 + c, si), ('cacc', c)], [('h', 12 + c, si)])

            if isH:
                j7, k7, w7 = WR.acquire(('win', l, 7))
                for mi in range(2):
                    b = PSA.get()
                    zmm(b, w7, k7, mi)
                    act(ext_new(dext, mi, 15), v(ps[b][:, 0:n]), AF.Copy, [P(b)], ['dext'])
                    PSA.put(b)
                WR.release(j7)
                if last_of_tile and not samp:
                    cp(dhist[:, l, :, :], dext[:, :, n:n + 15], ['dext'], ['dhist'], eng='pool')
                E = 4 * 47 if samp else 15 + n

                def dnew(ap2):
                    if samp:
                        return ap2[:, 0:188].rearrange("p (q t) -> p q t", t=47)[:, :, 15:47]
                    return ap2[:, 15:15 + n]

                for c in range(2):
                    dd = dext[:, c, :]
                    tt(trA[:, 1:E], dd[:, 1:E], dd[:, 0:E - 1], ALU.add, ['dext'], ['trA'])
                    tt(trB[:, 3:E], trA[:, 3:E], trA[:, 1:E - 2], ALU.add, ['trA'], ['trB'])
                    if c == 0:
                        srcs = [(trA, 2.0), (trB, 4.0)]
                    else:
                        tt(trA[:, 7:E], trB[:, 7:E], trB[:, 3:E - 4], ALU.add, ['trB'], ['trA'])
                        tt(trB[:, 15:E], trA[:, 15:E], trA[:, 7:E - 8], ALU.add, ['trA'], ['trB'])
                        srcs = [(trA, 8.0), (trB, 16.0)]
                    for gi, (sbuf_, wlen) in enumerate(srcs):
                        pr = slice(64 * gi, 64 * gi + 64)
                        kk = 'trA' if sbuf_ is trA else 'trB'
                        stt(v(pooled[pr, c, 0:n]), dnew(sbuf_[pr, :]), 1.0 / wlen, dnew(dext[pr, c, :]),
                            ALU.mult, ALU.subtract, [kk, 'dext'], [('pooled', si, c)])
                        if first_of_seq and si == 0 and not samp:
                            tt(meansb[pr, 0:15], sbuf_[pr, 15:30], rct[pr, c, :], ALU.mult, [kk, 'rct'], ['meansb'])
                            tt(pooled[pr, c, 0:15], meansb[pr, 0:15], dext[pr, c, 15:30], ALU.subtract,
                               ['meansb', 'dext'], [('pooled', si, c)])
            if isT:
                for c in range(2):
                    b = PSA.get()
                    mm(ps[b][:, 0:n], dwbd[:, c, :], pooled[:, c, 0:n], True, True, ['dwbd', ('pooled', si, c)], [P(b)])
                    ts(hb[:, 14 + c, off:off + n], ps[b][:, 0:n], pv('dsc', l * 2 + c), None, ALU.mult, None,
                       [P(b), 'PV'], [('h', 14 + c, si)])
                    PSA.put(b)

            if isH:
                j8, k8, w8 = WR.acquire(('win', l, 8))
                for mi in range(2):
                    b = PSA.get()
                    zmm(b, w8, k8, mi)
                    act(hb[:, 16 + mi, off:off + n], ps[b][:, 0:n], AF.Copy, [P(b)], [('h', 16 + mi, si)], scale=0.125)
                    PSA.put(b)
                WR.release(j8)
            if isT:
                pti = 0
                ktk = [('KT', i) for i in range(L)] if samp else [('KT', l)]
                vvk = [('VV', i) for i in range(L)] if samp else [('VV', l)]
                for c in range(2):
                    bo = PSA.get()
                    bd = PSA.get()
                    if not samp:
                        sbk = []
                        for hi in range(2):
                            pr = slice(64 * hi, 64 * hi + 64)
                            for mc in range(2):
                                b = PSA.get()
                                mm(ps[b][:, 0:n], KT[pr, l, c, mc * 128:(mc + 1) * 128], hb[pr, 16 + c, off:off + n],
                                   True, True, ktk + [('h', 16 + c, si)], [P(b)])
                                sbk.append(b)
                        for i4 in range(4):
                            b = sbk[i4]
                            act(PT[i4][:, 0:n], ps[b][:, 0:n], AF.Exp, [P(b)], [('PT', i4)])
                            PSA.put(b)
                        for hi in range(2):
                            h = 2 * c + hi
                            pr = slice(64 * hi, 64 * hi + 64)
                            for mc in range(2):
                                i4 = hi * 2 + mc
                                mm(ps[bo][pr, 0:n], VV[:, l, mc, 64 * h:64 * h + 64], PT[i4][:, 0:n], mc == 0, mc == 1,
                                   vvk + [('PT', i4)], [P(bo)])
                        for hi in range(2):
                            pr = slice(64 * hi, 64 * hi + 64)
                            for mc in range(2):
                                i4 = hi * 2 + mc
                                mm(ps[bd][pr, 0:n], ones_b[:, 0:64], PT[i4][:, 0:n], mc == 0, mc == 1,
                                   ['ones_b', ('PT', i4)], [P(bd)])
                    else:
                        pts = []
                        for hi in range(2):
                            pr = slice(64 * hi, 64 * hi + 64)
                            b = PSA.get()
                            for mc in range(2):
                                for q in range(4):
                                    o0 = (mc * 4 + q) * 32
                                    mm(ps[b][:, o0:o0 + 32], KT[pr, q, c, mc * 128:(mc + 1) * 128],
                                       hb[pr, 16 + c, 32 * q:32 * q + 32], True, True, ktk + [('h', 16 + c, si)], [P(b)])
                            pt = PT[pti % 4]
                            pk = ('PT', pti % 4)
                            pti += 1
                            act(pt[:, 0:256], ps[b][:, 0:256], AF.Exp, [P(b)], [pk])
                            PSA.put(b)
                            pts.append((pt, pk))
                        for hi in range(2):
                            h = 2 * c + hi
                            pr = slice(64 * hi, 64 * hi + 64)
                            pt, pk = pts[hi]
                            for q in range(4):
                                for mc in range(2):
                                    o0 = (mc * 4 + q) * 32
                                    mm(ps[bo][pr, 32 * q:32 * q + 32], VV[:, q, mc, 64 * h:64 * h + 64], pt[:, o0:o0 + 32],
                                       mc == 0, mc == 1, vvk + [pk], [P(bo)])
                            for q in range(4):
                                for mc in range(2):
                                    o0 = (mc * 4 + q) * 32
                                    mm(ps[bd][pr, 32 * q:32 * q + 32], ones_b[:, 0:64], pt[:, o0:o0 + 32],
                                       mc == 0, mc == 1, ['ones_b', pk], [P(bd)])
                    act(den[:, 0:n], ps[bd][:, 0:n], AF.Copy, [P(bd)], ['den'])
                    PSA.put(bd)
                    recip(den[:, 0:n], den[:, 0:n], ['den'], ['den'])
                    tt(hb[:, 16 + c, off:off + n], ps[bo][:, 0:n], den[:, 0:n], ALU.mult, [P(bo), 'den'],
                       [('h', 16 + c, si)])
                    PSA.put(bo)

            if isH:
                j0, k0, w0 = WR.acquire(('win', l, 0))
                for mi in range(2):
                    b = PSA.get()
                    zmm(b, w0, k0, mi)
                    act(hb[:, 8 + mi, off:off + n], ps[b][:, 0:n], AF.Gelu_apprx_tanh, [P(b)], [('h', 8 + mi, si)])
                    PSA.put(b)
                WR.release(j0)
                j1, k1, w1 = WR.acquire(('win', l, 1))
                for blk in range(nb):
                    t0 = off + blk * 128
                    b = PSA.get()
                    for kc in range(8):
                        mm(ps[b][:, 0:256], xn[:, kc, t0:t0 + 128], w1[:, kc, :], kc == 0, kc == 7, xr + [k1], [P(b)])
                    u = blk % 4
                    act(gv[u][:], ps[b][:, 0:256], AF.Gelu_apprx_tanh, [P(b)], ['gv%d' % u])
                    PSA.put(b)
                    S.add('dve', lambda e, u=u: e.bn_stats(out=bnst[u][:], in_=gv[u][:]), ['gv%d' % u], ['bnst%d' % u])
                    S.add('dve', lambda e, u=u: e.bn_aggr(out=bnmv_all[:, u, :], in_=bnst[u][:]), ['bnst%d' % u],
                          [('bnmv', u)])
                bk = [('bnmv', u) for u in range(nb)]
                varv = bnmv_all[:, 0:nb, 1:2]
                ts(varv, varv, 0.0, EPS, ALU.max, ALU.add, bk, bk)
                act(varv, varv, AF.Sqrt, bk, bk)
                recip(varv, varv, bk, bk)
                for blk in range(nb):
                    u = blk % 4
                    ts(gv[u][:], gv[u][:], bnmv_all[:, u, 0:1], bnmv_all[:, u, 1:2], ALU.subtract, ALU.mult,
                       ['gv%d' % u, ('bnmv', u)], ['gv%d' % u])
                    tt(gv[u][:], gv[u][:], lnab[:, 0, :], ALU.mult, ['gv%d' % u, 'lnab'], ['gv%d' % u])
                    if samp:
                        tt(gv[u][:], gv[u][:], lnab[:, 1, :], ALU.add, ['gv%d' % u, 'lnab'], ['gv%d' % u])
                        cp(vabuf[:, blk, :], gv[u][:], ['gv%d' % u], [('va', si, blk)])
                        dma(avs[l], gv[u][:], ['gv%d' % u], [])
                    else:
                        tt(vabuf[:, blk, :], gv[u][:], lnab[:, 1, :], ALU.add, ['gv%d' % u, 'lnab'], [('va', si, blk)])
                WR.release(j1)
            if isT:
                for c in range(2):
                    b = PSA.get()
                    for blk in range(nb):
                        for gi in range(2):
                            g = 2 * c + gi
                            pr = slice(64 * gi, 64 * gi + 64)
                            cs = slice(blk * 128, (blk + 1) * 128)
                            mm(ps[b][pr, cs], vabuf[:, blk, 128 * c + 64 * gi:128 * c + 64 * gi + 64], wmT[:, g, :],
                               True, False, [('va', si, blk), 'wmT'], [P(b)])
                            mm(ps[b][pr, cs], ones_b[0:1, 0:64], bsrow[0:1, g * 128:(g + 1) * 128], False, True,
                               ['ones_b', 'bsrow'], [P(b)])
                    tt(hb[:, 8 + c, off:off + n], ps[b][:, 0:n], hb[:, 8 + c, off:off + n], ALU.mult,
                       [P(b), ('h', 8 + c, si)], [('h', 8 + c, si)])
                    PSA.put(b)
            def lnb():
                for c in range(2):
                    act(v(accb[:, c, 0:n]), ext_out(acc[:, c, :], 30), AF.Copy, [('acc', c)], [('accb', c)])
                    act(v(sqb[:, c, 0:n]), ext_out(acc[:, c, :], 30), AF.Square, [('acc', c)], [('sqb', c)])
                bm = PSA.get()
                be = PSA.get()
                for c in range(2):
                    mm(ps[bm][:, 0:n], onesw_b[:], accb[:, c, 0:n], c == 0, c == 1, [('accb', c), 'onesw_b'], [P(bm)])
                for c in range(2):
                    mm(ps[be][:, 0:n], onesw_b[:], sqb[:, c, 0:n], c == 0, c == 1, [('sqb', c), 'onesw_b'], [P(be)])
                act(msq[:, 0:n], ps[bm][:, 0:n], AF.Square, [P(bm)], ['msq'])
                act(meansb[:, 0:n], ps[bm][:, 0:n], AF.Copy, [P(bm)], ['meansb'])
                PSA.put(bm)
                tt(varsb[:, 0:n], ps[be][:, 0:n], msq[:, 0:n], ALU.subtract, [P(be), 'msq'], ['varsb'])
                PSA.put(be)
                ts(varsb[:, 0:n], varsb[:, 0:n], 0.0, EPS, ALU.max, ALU.add, ['varsb'], ['varsb'])
                act(varsb[:, 0:n], varsb[:, 0:n], AF.Sqrt, ['varsb'], ['varsb'])
                recip(varsb[:, 0:n], varsb[:, 0:n], ['varsb'], ['varsb'])
                for c in range(2):
                    ao = ext_out(acc[:, c, :], 30)
                    tt(ao, ao, v(meansb[:, 0:n]), ALU.subtract, [('acc', c), 'meansb'], [('acc', c)])
                    tt(ao, ao, v(varsb[:, 0:n]), ALU.mult, [('acc', c), 'varsb'], [('acc', c)])
                    act(v(hb[:, 10 + c, off:off + n]), ao, AF.Silu, [('acc', c), 'PV'], [('h', 10 + c, si)],
                        scale=pv('blg', l * 2 + c), bias=pv('blb', l * 2 + c))

            ret = None
            if isT:
                if defer_ln:
                    ret = lnb
                else:
                    lnb()
            if samp and isT:
                sample_state_emit(l)
            return ret

        def phase_c(l, sts, hook=None):
            it = 0
            for mp in range(4):
                accs = {}
                for si in range(len(sts)):
                    for mi in range(2):
                        accs[(si, mi)] = PSA.get()
                pending = [None]
                for nn in NN_ORDER:
                    jg, kg, wg = WR.acquire(('gate', l, mp, nn))
                    jb, kb, wb = WR.acquire(('br', l, mp, nn))
                    for si, (off, n) in enumerate(sts):
                        xr = [('xn', kc, si) for kc in range(8)]
                        for mi in range(2):
                            m = mp * 2 + mi
                            ba = accs[(si, mi)]
                            bg_ = PSA.get()
                            for kc in range(8):
                                mm(ps[bg_][:, 0:n], wg[:, kc, mi * 128:(mi + 1) * 128], xn[:, kc, off:off + n],
                                   kc == 0, kc == 7, xr + [kg], [P(bg_)])
                            bp = PSA.get()
                            for c in range(2):
                                mm(ps[bp][:, 0:n], wb[:, c, mi * 128:(mi + 1) * 128], hb[:, 8 + 2 * nn + c, off:off + n],
                                   c == 0, c == 1, [kb, ('h', 8 + 2 * nn + c, si)], [P(bp)])
                            u = it % 3
                            it += 1
                            act(sgC[u][:, 0:n], ps[bg_][:, 0:n], AF.Sigmoid, [P(bg_), 'PV'], [('sgC', u)],
                                bias=bgcol(l, nn, m))
                            PSA.put(bg_)
                            tt(gpC[u][:, 0:n], ps[bp][:, 0:n], sgC[u][:, 0:n], ALU.mult, [P(bp), ('sgC', u)],
                               [('gpC', u)])
                            PSA.put(bp)
                            if pending[0] is not None:
                                pending[0]()

                            def fin(u=u, nn=nn, ba=ba, n=n, off=off, m=m, si=si):
                                mm(ps[ba][:, 0:n], identb[:], gpC[u][:, 0:n], nn == NN_ORDER[0], nn == NN_ORDER[-1],
                                   ['identb', ('gpC', u)], [P(ba)])
                                if nn == NN_ORDER[-1]:
                                    act(hb[:, m, off:off + n], ps[ba][:, 0:n], AF.Copy, [P(ba)], [('h', m, si)])
                                    PSA.put(ba)
                            pending[0] = fin
                    WR.release(jg)
                    WR.release(jb)
                    if hook is not None and mp == 0 and nn == NN_ORDER[0]:
                        hook()
                if pending[0] is not None:
                    pending[0]()

        def phase_d(l, sts, after_st0=None, mid_st1=None):
            slabs = [WR.acquire(('wout', l, j)) for j in range(4)]
            for si, (off, n) in enumerate(sts):
                for j in range(4):
                    jw, kw, ww = slabs[j]
                    for mi in range(2):
                        m = 2 * j + mi
                        b = PSA.get()
                        for kc in range(8):
                            mm(ps[b][:, 0:n], ww[:, kc, mi * 128:(mi + 1) * 128], hb[:, kc, off:off + n],
                               kc == 0, kc == 7, [kw, ('h', kc, si)], [P(b)])
                        tt(x[:, m, off:off + n], ps[b][:, 0:n], x[:, m, off:off + n], ALU.add,
                           [P(b), ('x', m, si)], [('x', m, si)])
                        PSA.put(b)
                    if si == 1 and j == 0 and mid_st1 is not None:
                        mid_st1()
                if si == 0 and after_st0 is not None:
                    after_st0()
            for (jw, kw, ww) in slabs:
                WR.release(jw)

        def phase_e(l, sts, after_st0=None, mid_st1=None, pre_down_b=None):
            it = 0
            for half in range(2):
                j0, j1 = FH[half]
                for j in range(j0, j1):
                    jg, kg, wg = WR.acquire(('fg', l, j))
                    ju, ku, wu = WR.acquire(('fu', l, j))
                    for si, (off, n) in enumerate(sts):
                        xr = [('xn', kc, si) for kc in range(8)]
                        for mi in range(2):
                            f = (j - j0) * 2 + mi
                            bg_ = PSA.get()
                            for kc in range(8):
                                mm(ps[bg_][:, 0:n], wg[:, kc, mi * 128:(mi + 1) * 128], xn[:, kc, off:off + n],
                                   kc == 0, kc == 7, xr + [kg], [P(bg_)])
                            bu = PSA.get()
                            for kc in range(8):
                                mm(ps[bu][:, 0:n], wu[:, kc, mi * 128:(mi + 1) * 128], xn[:, kc, off:off + n],
                                   kc == 0, kc == 7, xr + [ku], [P(bu)])
                            u = it % 3
                            it += 1
                            act(sgC[u][:, 0:n], ps[bg_][:, 0:n], AF.Silu, [P(bg_)], [('sgC', u)])
                            PSA.put(bg_)
                            tt(hb[:, f, off:off + n], ps[bu][:, 0:n], sgC[u][:, 0:n], ALU.mult,
                               [P(bu), ('sgC', u)], [('h', f, si)])
                            PSA.put(bu)
                    WR.release(jg)
                    WR.release(ju)
                nk = (j1 - j0) * 2

                def down(m, si, off, n, kd, wd):
                    b = PSA.get()
                    for f in range(nk):
                        mm(ps[b][:, 0:n], wd[:, f, :], hb[:, f, off:off + n], f == 0, f == nk - 1,
                           [kd, ('h', f, si)], [P(b)])
                    tt(x[:, m, off:off + n], ps[b][:, 0:n], x[:, m, off:off + n], ALU.add,
                       [P(b), ('x', m, si)], [('x', m, si)])
                    PSA.put(b)

                if half == 0:
                    for m in range(8):
                        jd, kd, wd = WR.acquire(('dn', l, half, m))
                        for si, (off, n) in enumerate(sts):
                            down(m, si, off, n, kd, wd)
                        WR.release(jd)
                else:
                    if pre_down_b is not None:
                        pre_down_b()
                    for si, (off, n) in enumerate(sts):
                        for m in range(8):
                            jd, kd, wd = WR.acquire(('dn', l, half, m))
                            down(m, si, off, n, kd, wd)
                            WR.release(jd)
                            if si == 1 and m == 1 and mid_st1 is not None:
                                mid_st1()
                        if si == 0 and after_st0 is not None:
                            after_st0()

        def emit_state(hist, l, width, dst):
            b = PSA.get()
            for c in range(2):
                tp(ps[b][0:width, c * 128:(c + 1) * 128], hist[:, l, c, :], ident[:], ['bhist', 'chist', 'dhist', 'ident'],
                   [P(b)])
            cp(sttmp[0:width, :], ps[b][0:width, 0:256], [P(b)], ['sttmp'])
            PSA.put(b)
            dma(dst, sttmp[0:width, :], ['sttmp'], [])

        PSTS = [(0, 512), (512, 512)]
        nxt_cvs = [None]
        full = []
        for seq in range(KSEQS):
            for l in range(L):
                full += [('kvK', l), ('kvV', l)]
            for t in range(KTILES):
                for l in range(DEPTH_RUN):
                    full += layer_seq(l, 2, l == 0, l == DEPTH_RUN - 1)
        if KSAMP:
            for l in range(DEPTH_RUN):
                full += layer_seq(l, 1, l == 0, l == DEPTH_RUN - 1)
        WR.seq = full

        if KSTOP < 99:
            full = []
            if KSTOP >= 1:
                for l in range(L):
                    full += [('kvK', l), ('kvV', l)]
            if KSTOP >= 4:
                full += [('win', 0, j) for j in (3, 2, 5, 6, 4, 7, 8, 0, 1)]
            WR.seq = full
        for seq in range(KSEQS):
            if KSTOP >= 1:
                prologue(seq)
            if KSTOP < 99:
                if KSTOP >= 2:
                    load_tokens(lambda tb, seq=seq: xp[seq, tb * 128:(tb + 1) * 128, :], 8)
                if KSTOP >= 3:
                    layer_setup(0)
                    rmsnorm(0, 512, 0, 'n1g', 0, xn, 'xn')
                if KSTOP >= 4:
                    phase_ab(0, 0, 0, 512, True, False, None, 'H')
                store_tokens(lambda tb, seq=seq: yp[seq, tb * 128:(tb + 1) * 128, :], 4, [(0, 512)])
                break
            for t in range(KTILES):
                tok0 = t * TP
                load_tokens(lambda tb, seq=seq, tok0=tok0: xp[seq, tok0 + tb * 128:tok0 + (tb + 1) * 128, :], 8)
                if t == 0:
                    memset(bhist[:], 0.0, ['bhist'])
                    memset(chist[:], 0.0, ['chist'])
                    memset(dhist[:], 0.0, ['dhist'])
                for l in range(DEPTH_RUN):
                    layer_setup(l)
                    for si, (off, n) in enumerate(PSTS):
                        if si == 0 and l > 0:
                            continue
                        rmsnorm(off, n, si, 'n1g', l * 8, xn, 'xn')
                    if l == 0:
                        cvs = [WR.acquire(('cvd', l, i)) for i in range(4)]
                    else:
                        cvs = nxt_cvs[0]
                    for si, (off, n) in enumerate(PSTS):
                        phase_ab(l, si, off, n, t == 0, si == len(PSTS) - 1, cvs, 'H')
                    lnb_hook = None
                    for si, (off, n) in enumerate(PSTS):
                        r_ = phase_ab(l, si, off, n, t == 0, si == len(PSTS) - 1, cvs, 'T',
                                      defer_ln=(si == len(PSTS) - 1))
                        if r_ is not None:
                            lnb_hook = r_
                    for (jc, kc_, wc) in cvs:
                        WR.release(jc)
                    phase_c(l, PSTS, hook=lnb_hook)
                    o0, n0 = PSTS[0]
                    phase_d(l, PSTS, after_st0=lambda: rms_a(o0, n0, 0),
                            mid_st1=lambda l=l: rms_b(o0, n0, 0, 'n2g', l * 8, xn, 'xn'))
                    rmsnorm(PSTS[1][0], PSTS[1][1], 1, 'n2g', l * 8, xn, 'xn')
                    if l < DEPTH_RUN - 1:
                        def acq_next(l=l):
                            nxt_cvs[0] = [WR.acquire(('cvd', l + 1, i)) for i in range(4)]
                        phase_e(l, PSTS, after_st0=lambda: rms_a(o0, n0, 0),
                                mid_st1=lambda l=l: rms_b(o0, n0, 0, 'n1g', (l + 1) * 8, xn, 'xn'),
                                pre_down_b=acq_next)
                    else:
                        phase_e(l, PSTS)
                    if t == 1:
                        emit_state(bhist, l, 30, cbp[l, seq])
                        emit_state(chist, l, 2, ccp[l, seq])
                        emit_state(dhist, l, 15, pdp[l, seq])
                store_tokens(lambda tb, seq=seq, tok0=tok0: yp[seq, tok0 + tb * 128:tok0 + (tb + 1) * 128, :], 8, PSTS)

        if KSAMP and KSTOP >= 99:
            SSTS = [(0, 128)]
            load_tokens(lambda tb: xs[0:128, :], 1)
            for l in range(DEPTH_RUN):
                layer_setup(l, samp=True)
                rmsnorm(0, 128, 0, 'n1g', l * 8, xn, 'xn')
                if l == 0:
                    cvs = [WR.acquire(('cvd', l, i)) for i in range(4)]
                else:
                    cvs = nxt_cvs[0]
                phase_ab(l, 0, 0, 128, False, True, cvs, 'H', samp=True)
                phase_ab(l, 0, 0, 128, False, True, cvs, 'T', samp=True)
                for (jc, kc_, wc) in cvs:
                    WR.release(jc)
                phase_c(l, SSTS)
                phase_d(l, SSTS)
                rmsnorm(0, 128, 0, 'n2g', l * 8, xn, 'xn')
                if l < DEPTH_RUN - 1:
                    def acq_next_s(l=l):
                        nxt_cvs[0] = [WR.acquire(('cvd', l + 1, i)) for i in range(4)]
                    phase_e(l, SSTS, pre_down_b=acq_next_s)
                else:
                    phase_e(l, SSTS)
            store_tokens(lambda tb: ys[0:128, :], 1, SSTS)

        assert WR.nxt == len(WR.seq), (WR.nxt, len(WR.seq))
        S.emit(es)
    return nc


_CACHE = {}


def kernel(**inp):
    f32 = np.float32
    g = {k: np.ascontiguousarray(np.asarray(v, dtype=f32)) for k, v in inp.items()}
    if 'nc' not in _CACHE:
        _CACHE['nc'] = build_program()
    nc = _CACHE['nc']
    wl = [1.0, 1.0, 1.0, 1.0]
    rc = np.zeros((128, 2, 15), f32)
    wins = {(0, 0): 2, (0, 1): 4, (1, 0): 8, (1, 1): 16}
    for c in range(2):
        for gi in range(2):
            w = wins[(c, gi)]
            for t in range(15):
                rc[64 * gi:64 * gi + 64, c, t] = 1.0 / min(w, t + 1)
    wnames = ["norm1_g", "mem_norm_g", "w_in", "a_ln_g", "a_ln_b", "a_ws", "a_bs", "b_conv_w", "b_conv_b", "b_ln_g",
              "b_ln_b", "c_conv_w", "d_w", "d_scale", "w_mem_kv", "w_branch", "w_gate", "b_gate", "w_out", "norm2_g",
              "w_ffn_gate", "w_ffn_up", "w_ffn_down", "final_norm_g"]
    in_maps = []
    for c in range(NCORES):
        m = {k: g[k] for k in wnames}
        m["xp"] = g["x_prompt"][2 * c:2 * c + 2]
        m["xs"] = g["x_sample"][4 * c:4 * c + 4].reshape(128, D)
        m["memp"] = g["mem_prompt"][2 * c:2 * c + 2]
        m["ck"] = np.ascontiguousarray(g["cache_mem_k"][:, 4 * c:4 * c + 4].reshape(L, 4, NMEM, W))
        m["cv"] = np.ascontiguousarray(g["cache_mem_v"][:, 4 * c:4 * c + 4].reshape(L, 4, NMEM, W))
        m["scb"] = np.ascontiguousarray(g["state_conv_b"][:, 4 * c:4 * c + 4])
        m["scc"] = np.ascontiguousarray(g["state_conv_c"][:, 4 * c:4 * c + 4])
        m["spd"] = np.ascontiguousarray(g["state_pool_d"][:, 4 * c:4 * c + 4])
        m["rctab"] = rc
        in_maps.append(m)
    res = run_bass_kernel_spmd(nc, in_maps, core_ids=list(range(NCORES)))
    R = res.results

    def cat(name, axis):
        return np.concatenate([np.asarray(r[name]) for r in R], axis=axis)

    y_prompt = cat("yp", 0)
    y_sample = cat("ys", 0).reshape(32, 32, D)
    new_mem_k = cat("mk", 1).reshape(L, 16, NMEM, 4, 64)
    new_mem_v = cat("mv", 1).reshape(L, 16, NMEM, 4, 64)
    cb_p = cat("cbp", 1)
    cc_p = cat("ccp", 1)
    pd_p = cat("pdp", 1)
    av_s = cat("avs", 1).reshape(L, 32, 32, W)
    cb_s = cat("cbs", 1)
    cc_s = cat("ccs", 1)
    pd_s = cat("pds", 1)
    return (y_prompt, y_sample, new_mem_k, new_mem_v, cb_p, cc_p, pd_p, av_s, cb_s, cc_s, pd_s)
```
